# Optimizing a Trainium2 kernel written in Bass

```python
import math
import jax
import jax.numpy as jnp
from jax import lax
import numpy as np

D_MODEL = 2048
BATCH = 1
SEQ = 8192
DEPTH = 2

CTX_LEN = 256
GRID_W = 64

D_INNER = 2 * D_MODEL
SSM_HEAD_DIM = 64
SSM_HEADS = D_INNER // SSM_HEAD_DIM
SSM_GROUPS = 8
HEADS_PER_GROUP = SSM_HEADS // SSM_GROUPS
SSM_STATE = 128
SSM_CONV = 5
CHUNK = 128
D_BC = SSM_GROUPS * SSM_STATE
D_XBC = D_INNER + 2 * D_BC

D_CONV = D_MODEL
CONV_K = 31

N_EXPERTS = 16
EC_CAPACITY_FACTOR = 2
D_EXPERT = 3 * D_MODEL // 2

N_MOD = 6
ALPHA = (2 * DEPTH) ** 0.25
BETA = (8 * DEPTH) ** -0.25
LN_EPS = 1e-5

O_Z = 0
O_XBC = O_Z + D_INNER
O_DT = O_XBC + D_XBC
O_GLU = O_DT + 2 * SSM_HEADS
O_GATE = O_GLU + 2 * D_CONV
D_PROJ = O_GATE + 2 * D_MODEL

kernel_name = "hybrid_ssd_conformer_ecmoe_prefix_dit"


def layer_norm(x, g, b):
    xf = x.astype(jnp.float32)
    mu = jnp.mean(xf, axis=-1, keepdims=True)
    xc = xf - mu
    var = jnp.mean(xc * xc, axis=-1, keepdims=True)
    return (xc * lax.rsqrt(var + LN_EPS) * g + b).astype(x.dtype)


def modulate(x, shift, scale):
    return x * (1 + scale) + shift


def dwconv(x, w, b):
    k = w.shape[0]
    y = lax.conv_general_dilated(
        x, w[:, None, :].astype(x.dtype), window_strides=(1,),
        padding=[(k // 2, k // 2)], dimension_numbers=('NWC', 'WIO', 'NWC'),
        feature_group_count=x.shape[-1])
    return y + b


def axial_dwconv(h, w, b):
    bn, n, ch = h.shape
    rows = n // GRID_W
    half = ch // 2
    hh = h[..., :half].reshape(bn * rows, GRID_W, half)
    yh = dwconv(hh, w[:, :half], b[:half]).reshape(bn, n, half)
    hv = h[..., half:].reshape(bn, rows, GRID_W, ch - half).transpose(0, 2, 1, 3)
    hv = hv.reshape(bn * GRID_W, rows, ch - half)
    yv = dwconv(hv, w[:, half:], b[half:]).reshape(bn, GRID_W, rows, ch - half)
    yv = yv.transpose(0, 2, 1, 3).reshape(bn, n, ch - half)
    return jnp.concatenate([yh, yv], axis=-1)


def ssd_inputs(xbc, dt_raw, dt_bias):
    bn, L, _ = xbc.shape
    xs = xbc[..., :D_INNER].reshape(bn, L, SSM_GROUPS, HEADS_PER_GROUP, SSM_HEAD_DIM)
    bm = xbc[..., D_INNER:D_INNER + D_BC].reshape(bn, L, SSM_GROUPS, SSM_STATE)
    cm = xbc[..., D_INNER + D_BC:].reshape(bn, L, SSM_GROUPS, SSM_STATE)
    dt = jax.nn.softplus((dt_raw + dt_bias.reshape(-1)).astype(jnp.float32))
    dt = dt.reshape(bn, L, 2, SSM_GROUPS, HEADS_PER_GROUP)
    return xs, bm, cm, dt


def ssd_chunked(xdt, a, bm, cm, h0):
    bn, L, g, e, p = xdt.shape
    nc = L // CHUNK
    xdt = xdt.reshape(bn, nc, CHUNK, g, e, p)
    a = a.reshape(bn, nc, CHUNK, g, e)
    bm = bm.reshape(bn, nc, CHUNK, g, -1)
    cm = cm.reshape(bn, nc, CHUNK, g, -1)
    a_cs = jnp.cumsum(a, axis=2)
    lower = jnp.tril(jnp.ones((CHUNK, CHUNK), dtype=bool))
    seg = a_cs[:, :, :, None] - a_cs[:, :, None, :]
    decay = jnp.exp(jnp.where(lower[:, :, None, None], seg, -jnp.inf))
    scores = jnp.einsum('bcign,bcjgn->bcijg', cm, bm)
    y_diag = jnp.einsum('bcijg,bcijge,bcjgep->bcigep', scores, decay, xdt)
    decay_to_end = jnp.exp(a_cs[:, :, -1:] - a_cs)
    chunk_states = jnp.einsum('bcjgn,bcjge,bcjgep->bcgepn', bm, decay_to_end, xdt)
    chunk_decay = jnp.exp(a_cs[:, :, -1])

    def step(h, inp):
        dec, s = inp
        return h * dec[..., None, None] + s, h

    h_final, h_enter = lax.scan(step, h0, (jnp.moveaxis(chunk_decay, 1, 0),
                                           jnp.moveaxis(chunk_states, 1, 0)))
    h_enter = jnp.moveaxis(h_enter, 0, 1)
    y_off = jnp.einsum('bcign,bcgepn,bcige->bcigep', cm, h_enter, jnp.exp(a_cs))
    return (y_diag + y_off).reshape(bn, L, g, e, p), h_final


def bidirectional_ssd(ssm_c, ssm_l, a_log):
    xs_c, bm_c, cm_c, dt_c = ssm_c
    xs_l, bm_l, cm_l, dt_l = ssm_l
    bn = xs_l.shape[0]
    y_c = jnp.zeros(xs_c.shape, jnp.float32)
    y_l = jnp.zeros(xs_l.shape, jnp.float32)
    for d in range(2):
        a_rate = -jnp.exp(a_log[d].astype(jnp.float32)).reshape(SSM_GROUPS, HEADS_PER_GROUP)

        def direction_inputs(xs, bm, cm, dt):
            dtd = dt[:, :, d]
            args = (xs.astype(jnp.float32) * dtd[..., None], dtd * a_rate,
                    bm.astype(jnp.float32), cm.astype(jnp.float32))
            if d == 1:
                args = tuple(jnp.flip(t, axis=1) for t in args)
            return args

        h0 = jnp.zeros((bn, SSM_GROUPS, HEADS_PER_GROUP, SSM_HEAD_DIM, SSM_STATE), jnp.float32)
        yc_d, h_ctx = ssd_chunked(*direction_inputs(xs_c, bm_c, cm_c, dt_c), h0)
        yl_d, _ = ssd_chunked(*direction_inputs(xs_l, bm_l, cm_l, dt_l), h_ctx)
        if d == 1:
            yc_d = jnp.flip(yc_d, axis=1)
            yl_d = jnp.flip(yl_d, axis=1)
        y_c = y_c + yc_d
        y_l = y_l + yl_d
    return y_c, y_l


def gated_group_rmsnorm(y, z, w):
    h = (y * jax.nn.silu(z)).astype(jnp.float32)
    hg = h.reshape(*h.shape[:-1], SSM_GROUPS, -1)
    hg = hg * lax.rsqrt(jnp.mean(hg * hg, axis=-1, keepdims=True) + LN_EPS)
    return (hg.reshape(h.shape) * w).astype(y.dtype)


def mixer_output(p, y, xs, conv_fn, d_skip, norm_w, w_ssm_out, ln_g, ln_b, w_conv_out, w_o):
    bn, L, _ = p.shape
    z = p[..., O_Z:O_XBC]
    glu = p[..., O_GLU:O_GATE]
    gate = p[..., O_GATE:]
    y = y + d_skip.reshape(SSM_GROUPS, HEADS_PER_GROUP)[..., None] * xs
    y = y.reshape(bn, L, D_INNER).astype(p.dtype)
    y_ssm = gated_group_rmsnorm(y, z, norm_w) @ w_ssm_out
    ga, gb = jnp.split(glu, 2, axis=-1)
    hcv = conv_fn(ga * jax.nn.sigmoid(gb))
    hcv = jax.nn.silu(layer_norm(hcv, ln_g, ln_b))
    y_conv = hcv @ w_conv_out
    g_ssm, g_conv = jnp.split(jax.nn.sigmoid(gate), 2, axis=-1)
    return (g_ssm * y_ssm + g_conv * y_conv) @ w_o


def expert_choice(u, w_router, w_gate, w_up, w_down):
    bn, n, dm = u.shape
    cap = max(1, EC_CAPACITY_FACTOR * n // N_EXPERTS)
    aff = jax.nn.softmax((u @ w_router).astype(jnp.float32), axis=-1)
    g, idx = lax.top_k(jnp.swapaxes(aff, 1, 2), cap)
    xe = jax.vmap(lambda ub, ib: ub[ib])(u, idx)
    h = jax.nn.silu(jnp.einsum('becd,edf->becf', xe, w_gate)) * jnp.einsum('becd,edf->becf', xe, w_up)
    ye = jnp.einsum('becf,efd->becd', h, w_down) * g[..., None].astype(u.dtype)
    return jax.vmap(lambda ib, yb: jnp.zeros((n, dm), yb.dtype).at[ib.reshape(-1)].add(yb.reshape(-1, dm)))(idx, ye)


def setup_inputs(seed: int = 0) -> dict:
    key = jax.random.key(seed)
    ks = jax.random.split(key, 32)
    f32 = jnp.float32
    L = DEPTH

    def nrm(k, shape, scale):
        return jax.random.normal(k, shape, f32) * scale

    dt0 = jnp.exp(jax.random.uniform(ks[8], (L, 2, SSM_HEADS), f32, math.log(1e-3), math.log(1e-1)))
    dt_bias = dt0 + jnp.log(-jnp.expm1(-dt0))
    a_log = jnp.log(jax.random.uniform(ks[9], (L, 2, SSM_HEADS), f32, 1.0, 16.0))
    return {
        'x': nrm(ks[0], (BATCH, SEQ, D_MODEL), 1.0),
        'c': nrm(ks[1], (BATCH, D_MODEL), 1.0),
        'ctx': nrm(ks[2], (BATCH, CTX_LEN, D_MODEL), 1.0),
        'c_ctx': nrm(ks[3], (D_MODEL,), 1.0),
        'w_ada': nrm(ks[4], (L, D_MODEL, N_MOD * D_MODEL), 0.5 * D_MODEL ** -0.5),
        'b_ada': nrm(ks[5], (L, N_MOD * D_MODEL), 0.01),
        'w_in': nrm(ks[6], (L, D_MODEL, D_PROJ), D_MODEL ** -0.5),
        'ssm_conv_w': nrm(ks[7], (L, SSM_CONV, D_XBC), SSM_CONV ** -0.5),
        'ssm_conv_b': nrm(ks[10], (L, D_XBC), 0.02),
        'ssm_dt_bias': dt_bias,
        'ssm_a_log': a_log,
        'ssm_d': 1.0 + nrm(ks[11], (L, SSM_HEADS), 0.02),
        'ssm_norm_w': 1.0 + nrm(ks[12], (L, D_INNER), 0.02),
        'w_ssm_out': nrm(ks[13], (L, D_INNER, D_MODEL), D_INNER ** -0.5),
        'conv_dw_w': nrm(ks[14], (L, CONV_K, D_CONV), CONV_K ** -0.5),
        'conv_dw_b': nrm(ks[15], (L, D_CONV), 0.02),
        'conv_ln_g': 1.0 + nrm(ks[16], (L, D_CONV), 0.02),
        'conv_ln_b': nrm(ks[17], (L, D_CONV), 0.02),
        'w_conv_out': nrm(ks[18], (L, D_CONV, D_MODEL), D_CONV ** -0.5),
        'w_o': nrm(ks[19], (L, D_MODEL, D_MODEL), BETA * D_MODEL ** -0.5),
        'ln1_g': 1.0 + nrm(ks[20], (L, D_MODEL), 0.02),
        'ln1_b': nrm(ks[21], (L, D_MODEL), 0.02),
        'w_router': nrm(ks[22], (L, D_MODEL, N_EXPERTS), D_MODEL ** -0.5),
        'w_exp_gate': nrm(ks[23], (L, N_EXPERTS, D_MODEL, D_EXPERT), D_MODEL ** -0.5),
        'w_exp_up': nrm(ks[24], (L, N_EXPERTS, D_MODEL, D_EXPERT), D_MODEL ** -0.5),
        'w_exp_down': nrm(ks[25], (L, N_EXPERTS, D_EXPERT, D_MODEL), BETA * D_EXPERT ** -0.5),
        'ln2_g': 1.0 + nrm(ks[26], (L, D_MODEL), 0.02),
        'ln2_b': nrm(ks[27], (L, D_MODEL), 0.02),
    }


def reference(x, c, ctx, c_ctx, w_ada, b_ada, w_in, ssm_conv_w, ssm_conv_b, ssm_dt_bias,
              ssm_a_log, ssm_d, ssm_norm_w, w_ssm_out, conv_dw_w, conv_dw_b, conv_ln_g,
              conv_ln_b, w_conv_out, w_o, ln1_g, ln1_b, w_router, w_exp_gate, w_exp_up,
              w_exp_down, ln2_g, ln2_b):
    xl, xc = x, ctx
    for i in range(DEPTH):
        last = i == DEPTH - 1
        sh1_l, sc1_l, g1_l, sh2_l, sc2_l, g2_l = jnp.split(
            (jax.nn.silu(c) @ w_ada[i] + b_ada[i])[:, None, :], N_MOD, axis=-1)
        sh1_c, sc1_c, g1_c, sh2_c, sc2_c, g2_c = jnp.split(
            jax.nn.silu(c_ctx) @ w_ada[i] + b_ada[i], N_MOD, axis=-1)

        u_l = modulate(xl, sh1_l, sc1_l)
        u_c = modulate(xc, sh1_c, sc1_c)
        p_l = u_l @ w_in[i]
        if last:
            p_c_scan = u_c @ w_in[i][:, O_XBC:O_GLU]
        else:
            p_c = u_c @ w_in[i]
            p_c_scan = p_c[..., O_XBC:O_GLU]
        p_l_scan = p_l[..., O_XBC:O_GLU]

        def scan_inputs(ps):
            xbc = jax.nn.silu(dwconv(ps[..., :D_XBC], ssm_conv_w[i], ssm_conv_b[i]))
            return ssd_inputs(xbc, ps[..., D_XBC:], ssm_dt_bias[i])

        ssm_c = scan_inputs(p_c_scan)
        ssm_l = scan_inputs(p_l_scan)
        y_c, y_l = bidirectional_ssd(ssm_c, ssm_l, ssm_a_log[i])

        out_l = mixer_output(p_l, y_l, ssm_l[0],
                             lambda h: axial_dwconv(h, conv_dw_w[i], conv_dw_b[i]),
                             ssm_d[i], ssm_norm_w[i], w_ssm_out[i], conv_ln_g[i], conv_ln_b[i],
                             w_conv_out[i], w_o[i])
        xl = layer_norm(ALPHA * xl + g1_l * out_l, ln1_g[i], ln1_b[i])

        moe_l = expert_choice(modulate(xl, sh2_l, sc2_l), w_router[i], w_exp_gate[i],
                              w_exp_up[i], w_exp_down[i])
        xl = layer_norm(ALPHA * xl + g2_l * moe_l, ln2_g[i], ln2_b[i])

        if not last:
            out_c = mixer_output(p_c, y_c, ssm_c[0],
                                 lambda h: dwconv(h, conv_dw_w[i], conv_dw_b[i]),
                                 ssm_d[i], ssm_norm_w[i], w_ssm_out[i], conv_ln_g[i], conv_ln_b[i],
                                 w_conv_out[i], w_o[i])
            xc = layer_norm(ALPHA * xc + g1_c * out_c, ln1_g[i], ln1_b[i])
            moe_c = expert_choice(modulate(xc, sh2_c, sc2_c), w_router[i], w_exp_gate[i],
                                  w_exp_up[i], w_exp_down[i])
            xc = layer_norm(ALPHA * xc + g2_c * moe_c, ln2_g[i], ln2_b[i])
    return xl
```

```python
import numpy as np
import ml_dtypes
import concourse.bass as bass
import concourse.mybir as mybir
from concourse.bass_utils import run_bass_kernel_spmd

F32 = mybir.dt.float32
BF16 = mybir.dt.bfloat16
I32 = mybir.dt.int32
U32 = mybir.dt.uint32
AF = mybir.ActivationFunctionType
ALU = mybir.AluOpType
AX = mybir.AxisListType
NPBF = ml_dtypes.bfloat16

NCORES = 8


class Prog:
    ENG = ("pe", "dve", "act", "pool", "sp")

    def __init__(self, nc, n_dma_sems=6, same_engine_sync=True):
        self.nc = nc
        self.e = {"pe": nc.tensor, "dve": nc.vector, "act": nc.scalar, "pool": nc.gpsimd, "sp": nc.sync}
        self.sem = {k: nc.alloc_semaphore("c_" + k) for k in self.ENG}
        self.cnt = {k: 0 for k in self.ENG}
        self.same = same_engine_sync
        self.dsem = {}
        self.dcnt = {}
        self.drr = {}
        for q in ("sp", "act", "pool"):
            self.dsem[q] = [nc.alloc_semaphore(f"d_{q}{i}") for i in range(n_dma_sems)]
            self.dcnt[q] = [0] * n_dma_sems
            self.drr[q] = 0
        self.seen = {k: {} for k in self.ENG}
        self.buf = {}
        self.semobj = {}
        for k in self.ENG:
            self.semobj[("c", k)] = self.sem[k]
        for q in self.dsem:
            for i, s in enumerate(self.dsem[q]):
                self.semobj[("d", q, i)] = s
        self.ninst = 0

    def _deps(self, reads, writes):
        deps = []
        for k in reads:
            st = self.buf.get(k)
            if st and st["w"]:
                deps.append(st["w"])
        for k in writes:
            st = self.buf.get(k)
            if st:
                if st["w"]:
                    deps.append(st["w"])
                deps.extend(st["r"])
        return deps

    def _wait(self, eng, deps):
        best = {}
        for sk, v in deps:
            if sk == ("c", eng) and (eng == "pe" or not self.same):
                continue
            if v > best.get(sk, 0):
                best[sk] = v
        for sk, v in best.items():
            if self.seen[eng].get(sk, 0) >= v:
                continue
            self.e[eng].wait_ge(self.semobj[sk], v)
            self.seen[eng][sk] = v

    def _mark(self, reads, writes, tag):
        for k in writes:
            self.buf[k] = {"w": tag, "r": []}
        for k in reads:
            if k in writes:
                continue
            st = self.buf.setdefault(k, {"w": None, "r": []})
            st["r"] = [t for t in st["r"] if t[0] != tag[0]] + [tag]

    def op(self, eng, fn, reads=(), writes=()):
        self._wait(eng, self._deps(reads, writes))
        ins = fn()
        self.cnt[eng] += 1
        ins.then_inc(self.sem[eng], 1)
        self._mark(reads, writes, (("c", eng), self.cnt[eng]))
        self.ninst += 1
        return ins

    def dma(self, q, out, in_, reads=(), writes=(), **kw):
        i = self.drr[q]
        self.drr[q] = (i + 1) % len(self.dsem[q])
        sk = ("d", q, i)
        deps = self._deps(reads, writes)
        if self.dcnt[q][i] > 0:
            deps.append((sk, self.dcnt[q][i]))
        self._wait(q, deps)
        ins = self.e[q].dma_start(out=out, in_=in_, **kw)
        self.dcnt[q][i] += 16
        ins.then_inc(self.semobj[sk], 16)
        self._mark(reads, writes, (sk, self.dcnt[q][i]))
        self.ninst += 1
        return ins

    def dma_custom(self, q, fn, reads=(), writes=()):
        i = self.drr[q]
        self.drr[q] = (i + 1) % len(self.dsem[q])
        sk = ("d", q, i)
        deps = self._deps(reads, writes)
        if self.dcnt[q][i] > 0:
            deps.append((sk, self.dcnt[q][i]))
        self._wait(q, deps)
        ins = fn()
        self.dcnt[q][i] += 16
        ins.then_inc(self.semobj[sk], 16)
        self._mark(reads, writes, (sk, self.dcnt[q][i]))
        self.ninst += 1
        return ins

    def barrier(self):
        deps = []
        for k in self.ENG:
            if self.cnt[k]:
                deps.append((("c", k), self.cnt[k]))
        for q in self.dsem:
            for i, v in enumerate(self.dcnt[q]):
                if v:
                    deps.append((("d", q, i), v))
        old = self.same
        self.same = False
        for e in self.ENG:
            self._wait(e, deps)
        self.same = old

    def finish(self):
        deps = []
        for k in self.ENG:
            if self.cnt[k] and k != "sp":
                deps.append((("c", k), self.cnt[k]))
        for q in self.dsem:
            for i, v in enumerate(self.dcnt[q]):
                if v:
                    deps.append((("d", q, i), v))
        self.same = True
        self._wait("sp", deps)
        self.e["sp"].nop() if hasattr(self.e["sp"], "nop") else None


D = 2048
SEQ = 8192
CTX = 256
T = SEQ + CTX
NCH = T // 128
DEPTH = 2
D_INNER = 4096
D_BC = 1024
D_XBC = 6144
O_Z = 0
O_XBC = 4096
O_DT = O_XBC + D_XBC
O_GLU = O_DT + 128
O_GATE = O_GLU + 4096
D_PROJ = O_GATE + 4096
NEXP = 16
DEXP = 3072
ALPHA = (2 * DEPTH) ** 0.25
LN_EPS = 1e-5
WG_COLS = 1808


class KB:
    def __init__(self):
        self.nc = bass.Bass("TRN2", target_bir_lowering=False)
        self.p = Prog(self.nc)
        self.banks = [self.nc.alloc_psum_tensor(f"bank{i}", [128, 512], F32).ap() for i in range(8)]

    def din(self, name, shape, dt=F32):
        return self.nc.dram_tensor(name, list(shape), dt, kind="ExternalInput").ap()

    def dout(self, name, shape, dt=F32):
        return self.nc.dram_tensor(name, list(shape), dt, kind="ExternalOutput").ap()

    def dscr(self, name, shape, dt=F32):
        return self.nc.dram_tensor(name, list(shape), dt, kind="Internal").ap()

    def sb(self, name, shape, dt=F32):
        return self.nc.alloc_sbuf_tensor("s_" + name, list(shape), dt).ap()

    def arena_init(self, words):
        self.arena = self.nc.alloc_sbuf_tensor("s_arena", [128, words], F32).ap()
        self.arena_words = words
        self.arena_off = 0

    def arena_reset(self):
        self.p.barrier()
        self.arena_off = 0

    def asb(self, name, shape, dt=F32):
        n = int(np.prod(shape[1:]))
        words = n if dt == F32 or dt == I32 or dt == U32 else (n + 1) // 2
        assert self.arena_off + words <= self.arena_words, (name, self.arena_off, words)
        ap = self.arena[:, self.arena_off:self.arena_off + words]
        self.arena_off += words
        if dt != F32:
            ap = ap.bitcast(dt)
            if ap.shape[1] != n:
                ap = ap[:, 0:n]
        if len(shape) == 3:
            ap = ap.rearrange("p (a b) -> p a b", a=shape[1])
        return ap

    def mm(self, out, lhsT, rhs, start, stop, r, w):
        nc = self.nc
        return self.p.op("pe", lambda: nc.tensor.matmul(out, lhsT, rhs, start=start, stop=stop), reads=r, writes=w)

    def tr(self, out, in_, ident, r, w):
        nc = self.nc
        return self.p.op("pe", lambda: nc.tensor.transpose(out, in_, ident), reads=r, writes=w)

    def dve(self, fn, r, w):
        return self.p.op("dve", fn, reads=r, writes=w)

    def act(self, fn, r, w):
        return self.p.op("act", fn, reads=r, writes=w)

    def pool(self, fn, r, w):
        return self.p.op("pool", fn, reads=r, writes=w)

    def consts(self):
        nc = self.nc
        ones = self.sb("c_ones", [128, 128])
        self.ident = self.sb("c_ident", [128, 128])
        self.identb = self.sb("c_identb", [128, 128], BF16)
        self.triU = self.sb("c_triU", [128, 128])
        self.triL = self.sb("c_triL", [128, 128])
        self.pool(lambda: nc.gpsimd.memset(ones, 1.0), [], ["c_ones"])
        self.pool(lambda: nc.gpsimd.affine_select(self.triU, ones, [[1, 128]], ALU.is_ge, 0.0, base=0, channel_multiplier=-1), ["c_ones"], ["c_triU"])
        self.pool(lambda: nc.gpsimd.affine_select(self.triL, ones, [[-1, 128]], ALU.is_ge, 0.0, base=0, channel_multiplier=1), ["c_ones"], ["c_triL"])
        self.pool(lambda: nc.gpsimd.affine_select(self.ident, self.triU, [[-1, 128]], ALU.is_ge, 0.0, base=0, channel_multiplier=1), ["c_triU"], ["c_ident"])
        self.pool(lambda: nc.gpsimd.tensor_copy(self.identb, self.ident), ["c_ident"], ["c_identb"])
        self.ones = ones


def bc_mid(ap, n):
    P, H = ap.shape
    return ap.unsqueeze(2).broadcast_to([P, H, n])


def build_p0():
    k = KB()
    nc = k.nc
    cc = k.din("cc", [128, 16, 2])
    w = k.din("w", [2, 128, 16, 1536])
    b = k.din("b", [128, 24])
    o = k.dout("o", [128, 24, 2])
    cct = k.sb("cct", [128, 16, 2])
    sg = k.sb("sg", [128, 16, 2])
    s = k.sb("s", [128, 16, 2])
    wt = k.sb("wt", [128, 16, 1536])
    bt = k.sb("bt", [128, 24])
    ot = k.sb("ot", [128, 24, 2])
    p = k.p
    p.dma("sp", cct, cc, writes=["cct"])
    p.dma("sp", bt, b, writes=["bt"])
    k.act(lambda: nc.scalar.activation(out=sg, in_=cct, func=AF.Sigmoid), ["cct"], ["sg"])
    k.dve(lambda: nc.vector.tensor_tensor(out=s, in0=cct, in1=sg, op=ALU.mult), ["cct", "sg"], ["s"])
    for i in range(2):
        for kt in range(16):
            p.dma("sp" if kt % 2 == 0 else "act", wt[:, kt, :], w[i, :, kt, :], writes=[("wt", kt)])
        for ct in range(12):
            ps = k.banks[ct % 4][:, 0:2]
            for kt in range(16):
                k.mm(ps, wt[:, kt, ct * 128:(ct + 1) * 128], s[:, kt, :], kt == 0, kt == 15, ["s", ("wt", kt)], [("bank", ct % 4)])
            j = i * 12 + ct
            k.act(lambda: nc.scalar.activation(out=ot[:, j, :], in_=ps, func=AF.Identity, bias=bt[:, j:j + 1]), [("bank", ct % 4), "bt"], ["ot"])
    p.dma("sp", o, ot, reads=["ot"])
    p.finish()
    return nc


def run_p0(inp):
    nc = build_p0()
    cvec = np.stack([inp["c"][0], inp["c_ctx"]], axis=-1)
    cc = np.ascontiguousarray(cvec.reshape(16, 128, 2).transpose(1, 0, 2))
    maps = []
    for g in range(NCORES):
        c0 = g * 1536
        wsl = inp["w_ada"][:, :, c0:c0 + 1536]
        wl = np.ascontiguousarray(wsl.reshape(2, 16, 128, 1536).transpose(0, 2, 1, 3))
        bl = inp["b_ada"][:, c0:c0 + 1536].reshape(2, 12, 128).transpose(2, 0, 1).reshape(128, 24)
        maps.append({"cc": cc, "w": wl, "b": np.ascontiguousarray(bl)})
    res = run_bass_kernel_spmd(nc, maps, core_ids=list(range(NCORES)))
    mod = np.zeros((2, 2, 6 * D), np.float32)
    for g in range(NCORES):
        og = res.results[g]["o"]
        og = og.reshape(128, 2, 12, 2)
        mod[:, :, g * 1536:(g + 1) * 1536] = og.transpose(1, 3, 2, 0).reshape(2, 2, 1536)
    return mod


ORDER_B = [1, 0] + list(range(NCH - 1, 1, -1))


STOP = 0


def build_pA(nchunks=NCH):
    k = KB()
    nc, p = k.nc, k.p
    xc = k.din("xc", [NCH, 128, 16, 132])
    modd = k.din("mod", [128, 16, 4])
    wg = k.din("wg", [128, 16, WG_COLS])
    cwd = k.din("cw", [128, 6, 5])
    cbd = k.din("cb", [128, 6])
    dwd = k.din("dw", [128, 2, 31])
    dbd = k.din("db", [128, 2])
    repd = k.din("rep", [128, 552])
    hs = k.dout("hs", [T, 512], BF16)
    hpre = k.dout("hpre", [2, 128, T])
    ypart_d = k.dscr("ypart_d", [NCH, 128, 512])
    sz_d = k.dscr("sz_d", [NCH, 128, 512])
    sb_d = k.dscr("sb_d", [NCH, 128, 512])
    glu_d = k.dscr("glu_d", [2, 128, T])

    k.consts()
    k.arena_init(43500)
    B = k.banks
    wb = k.asb("wb", [128, 16, WG_COLS], BF16)
    for kt in range(16):
        p.dma("pool", wb[:, kt, :], wg[:, kt, :], writes=[("wb", kt)])
    WBK = [("wb", kt) for kt in range(16)]
    mod = k.sb("modt", [128, 16, 4])
    cw = k.sb("cw", [128, 6, 5])
    cb = k.sb("cb", [128, 6])
    dw = k.sb("dwt", [128, 2, 31])
    db = k.sb("dbt", [128, 2])
    rep = k.sb("rep", [128, 552])
    p.dma("sp", mod, modd, writes=["mod"])
    p.dma("sp", cw, cwd, writes=["cw"])
    p.dma("sp", cb, cbd, writes=["cb"])
    p.dma("sp", dw, dwd, writes=["dw"])
    p.dma("sp", db, dbd, writes=["db"])
    p.dma("sp", rep, repd, writes=["rep"])
    scl = k.sb("scl", [128, 16, 2])
    k.dve(lambda: nc.vector.tensor_scalar(out=scl[:, :, 0], in0=mod[:, :, 1], scalar1=1.0, scalar2=None, op0=ALU.add), ["mod"], ["scl"])
    k.dve(lambda: nc.vector.tensor_scalar(out=scl[:, :, 1], in0=mod[:, :, 3], scalar1=1.0, scalar2=None, op0=ALU.add), ["mod", "scl"], ["scl"])
    dtb = rep[:, 0:16]
    Aneg = k.sb("Aneg", [128, 16])
    k.act(lambda: nc.scalar.activation(out=Aneg, in_=rep[:, 16:32], func=AF.Exp), ["rep"], ["Aneg"])
    k.dve(lambda: nc.vector.tensor_scalar(out=Aneg, in0=Aneg, scalar1=-1.0, scalar2=None, op0=ALU.mult), ["Aneg"], ["Aneg"])
    Dsk = rep[:, 32:40]
    normw = rep[:, 40:552]

    Ccm = k.sb("Ccm", [128, NCH, 128], BF16)
    ea_b = k.sb("ea_b", [128, NCH, 8])
    dec_b = k.sb("dec_b", [128, NCH, 8])
    h_f = k.sb("h_f", [128, 512])
    hb_f = k.sb("hb_f", [128, 512], BF16)
    k.dve(lambda: nc.vector.memset(h_f, 0.0), [], ["h_f"])
    k.dve(lambda: nc.vector.memset(hb_f, 0.0), [], ["hb_f"])

    def tmp(name, shape, dt=F32, n=2):
        if n == 1:
            a = k.asb(name, shape, dt)
            return [a, a]
        return [k.asb(f"{name}{i}", shape, dt) for i in range(n)]

    xt = tmp("xt", [128, 16, 132])
    ut = tmp("ut", [128, 16, 132], BF16)
    raw = tmp("raw", [128, 6, 132])
    acc = tmp("acc", [128, 6, 128])
    sgm = tmp("sgm", [128, 6, 128])
    xcm = tmp("xcm", [128, 4, 128])
    Bcm = tmp("Bcm", [128, 128], BF16)
    Btm = tmp("Btm", [128, 128], BF16)
    dtt = tmp("dtt", [128, 16])
    at = tmp("at", [128, 16])
    arep = tmp("arep", [128, 16, 128], n=1)
    acs = tmp("acs", [128, 16])
    nacs = tmp("nacs", [128, 16])
    GU = tmp("GU", [128, 128])
    GL = tmp("GL", [128, 128])
    E = tmp("E", [128, 16, 128], n=1)
    M = tmp("M", [128, 16, 128], BF16, n=1)
    xdt = tmp("xdt", [128, 2, 512], BF16)
    dte = tmp("dte", [128, 16])
    xdte = tmp("xdte", [128, 2, 512], BF16)
    ea_f = tmp("ea_f", [128, 8])
    dec_f = tmp("dec_f", [128, 8])
    t1 = tmp("t1", [128, 512])
    yp = tmp("yp", [128, 512])
    zs = tmp("zs", [128, 512])
    szt = tmp("szt", [128, 512])
    sbt = tmp("sbt", [128, 512])
    gsg = tmp("gsg", [128, 2, 128])
    glu = tmp("glu", [128, 2, 128])
    htmp = tmp("htmp", [128, 512])

    for c in range(nchunks):
        b = c % 2
        kb = lambda n: f"{n}0" if n in ("arep", "E", "M") else f"{n}{b}"
        mi = 1 if c < 2 else 0
        p.dma("sp" if c % 2 == 0 else "act", xt[b], xc[c], writes=[kb("xt")])
        for kt in range(16):
            (k.act if kt % 2 == 0 else k.dve)(
                (lambda kt=kt: nc.scalar.activation(out=ut[b][:, kt, :], in_=xt[b][:, kt, :], func=AF.Identity,
                                                    bias=mod[:, kt, 2 * mi:2 * mi + 1], scale=scl[:, kt, mi:mi + 1]))
                if kt % 2 == 0 else
                (lambda kt=kt: nc.vector.tensor_scalar(out=ut[b][:, kt, :], in0=xt[b][:, kt, :], scalar1=scl[:, kt, mi:mi + 1],
                                                       scalar2=mod[:, kt, 2 * mi:2 * mi + 1], op0=ALU.mult, op1=ALU.add)),
                [kb("xt"), "mod", "scl"], [(kb("ut"), kt)])
        if STOP == c * 100 + 1:
            p.finish()
            return nc
        UT = [(kb("ut"), kt) for kt in range(16)]
        for m in range(6):
            bk = m // 3
            o = B[bk][:, (m % 3) * 132:(m % 3) * 132 + 132]
            for kt in range(16):
                k.mm(o, wb[:, kt, m * 128:(m + 1) * 128], ut[b][:, kt, :], kt == 0, kt == 15, [("wb", kt), (kb("ut"), kt)], [("bank", bk)])
        for bk in range(2):
            k.act(lambda bk=bk: nc.scalar.copy(out=raw[b][:, 3 * bk:3 * bk + 3, :], in_=B[bk][:, 0:396].rearrange("p (m n) -> p m n", m=3)),
                  [("bank", bk)], [kb("raw")])
        if STOP == c * 100 + 2:
            p.finish()
            return nc
        if c == 0 or c == 2:
            k.dve(lambda: nc.vector.memset(raw[b][:, :, 0:2], 0.0), [kb("raw")], [kb("raw")])
        if c == 1 or c == NCH - 1:
            k.dve(lambda: nc.vector.memset(raw[b][:, :, 130:132], 0.0), [kb("raw")], [kb("raw")])
        for m in range(4):
            o = B[2][:, m * 128:(m + 1) * 128]
            for kt in range(16):
                k.mm(o, wb[:, kt, 768 + m * 128:768 + (m + 1) * 128], ut[b][:, kt, 2:130], kt == 0, kt == 15, [("wb", kt), (kb("ut"), kt)], [("bank", 2)])
        for kt in range(16):
            k.mm(B[3], ut[b][:, kt, 2:130], wb[:, kt, 1280:1792], kt == 0, kt == 15, [("wb", kt), (kb("ut"), kt)], [("bank", 3)])
        for kt in range(16):
            k.mm(B[4][:, 0:16], ut[b][:, kt, 2:130], wb[:, kt, 1792:1808], kt == 0, kt == 15, [("wb", kt), (kb("ut"), kt)], [("bank", 4)])
        if STOP == c * 100 + 3:
            p.finish()
            return nc
        k.act(lambda: nc.scalar.activation(out=gsg[b], in_=B[2][:, 256:512].rearrange("p (m n) -> p m n", m=2), func=AF.Sigmoid), [("bank", 2)], [kb("gsg")])
        k.dve(lambda: nc.vector.tensor_tensor(out=glu[b], in0=B[2][:, 0:256].rearrange("p (m n) -> p m n", m=2), in1=gsg[b], op=ALU.mult),
              [("bank", 2), kb("gsg")], [kb("glu")])
        p.dma("sp", glu_d[:, :, c * 128:(c + 1) * 128].rearrange("m p n -> p m n"), glu[b], reads=[kb("glu")], writes=["glu_d"])
        k.act(lambda: nc.scalar.activation(out=zs[b], in_=B[3], func=AF.Sigmoid), [("bank", 3)], [kb("zs")])
        k.dve(lambda: nc.vector.tensor_tensor(out=szt[b], in0=B[3], in1=zs[b], op=ALU.mult), [("bank", 3), kb("zs")], [kb("szt")])
        p.dma("sp", sz_d[c], szt[b], reads=[kb("szt")], writes=[("sz_d", c)])
        if STOP == c * 100 + 4:
            p.finish()
            return nc
        k.dve(lambda: nc.vector.tensor_tensor(out=dtt[b], in0=B[4][:, 0:16], in1=dtb, op=ALU.add), [("bank", 4), "rep"], [kb("dtt")])
        k.act(lambda: nc.scalar.activation(out=dtt[b], in_=dtt[b], func=AF.Exp), [kb("dtt")], [kb("dtt")])
        k.act(lambda: nc.scalar.activation(out=dtt[b], in_=dtt[b], func=AF.Ln, bias=1.0), [kb("dtt")], [kb("dtt")])
        k.dve(lambda: nc.vector.tensor_tensor(out=at[b], in0=dtt[b], in1=Aneg, op=ALU.mult), [kb("dtt"), "Aneg"], [kb("at")])
        if STOP == c * 100 + 5:
            p.finish()
            return nc
        k.dve(lambda: nc.vector.tensor_tensor(out=acc[b], in0=raw[b][:, :, 0:128], in1=bc_mid(cw[:, :, 0], 128), op=ALU.mult), [kb("raw"), "cw"], [kb("acc")])
        for s in range(1, 5):
            k.dve(lambda s=s: nc.vector.tensor_tensor(out=sgm[b], in0=raw[b][:, :, s:s + 128], in1=bc_mid(cw[:, :, s], 128), op=ALU.mult), [kb("raw"), "cw"], [kb("sgm")])
            k.dve(lambda: nc.vector.tensor_tensor(out=acc[b], in0=acc[b], in1=sgm[b], op=ALU.add), [kb("acc"), kb("sgm")], [kb("acc")])
        k.dve(lambda: nc.vector.tensor_tensor(out=acc[b], in0=acc[b], in1=bc_mid(cb, 128), op=ALU.add), [kb("acc"), "cb"], [kb("acc")])
        k.act(lambda: nc.scalar.activation(out=sgm[b], in_=acc[b], func=AF.Sigmoid), [kb("acc")], [kb("sgm")])
        k.dve(lambda: nc.vector.tensor_tensor(out=xcm[b], in0=acc[b][:, 0:4, :], in1=sgm[b][:, 0:4, :], op=ALU.mult), [kb("acc"), kb("sgm")], [kb("xcm")])
        k.dve(lambda: nc.vector.tensor_tensor(out=Bcm[b], in0=acc[b][:, 4, :], in1=sgm[b][:, 4, :], op=ALU.mult), [kb("acc"), kb("sgm")], [kb("Bcm")])
        k.dve(lambda: nc.vector.tensor_tensor(out=Ccm[:, c, :], in0=acc[b][:, 5, :], in1=sgm[b][:, 5, :], op=ALU.mult), [kb("acc"), kb("sgm")], [("Ccm", c)])
        if STOP == c * 100 + 6:
            p.finish()
            return nc
        for m in range(4):
            k.tr(B[5][:, m * 128:(m + 1) * 128], xcm[b][:, m, :], k.ident, [kb("xcm"), "c_ident"], [("bank", 5)])
        b4bf = B[4][:, 256:320].bitcast(BF16)
        k.tr(b4bf, Bcm[b], k.identb, [kb("Bcm"), "c_identb"], [("bank", 4)])
        k.act(lambda: nc.scalar.copy(out=Btm[b], in_=b4bf), [("bank", 4)], [kb("Btm")])
        if STOP == c * 100 + 7:
            p.finish()
            return nc
        k.mm(B[4][:, 128:256], Bcm[b], Ccm[:, c, :], True, True, [kb("Bcm"), ("Ccm", c)], [("bank", 4)])
        k.dve(lambda: nc.vector.tensor_tensor(out=GU[b], in0=B[4][:, 128:256], in1=k.triU, op=ALU.mult), [("bank", 4), "c_triU"], [kb("GU")])
        k.dve(lambda: nc.vector.tensor_tensor(out=GL[b], in0=B[4][:, 128:256], in1=k.triL, op=ALU.mult), [("bank", 4), "c_triL"], [kb("GL")])
        if STOP == c * 100 + 8:
            p.finish()
            return nc
        k.mm(B[4][:, 16:24], k.triU, at[b][:, 0:8], True, True, ["c_triU", kb("at")], [("bank", 4)])
        k.mm(B[4][:, 24:32], k.triL, at[b][:, 8:16], True, True, ["c_triL", kb("at")], [("bank", 4)])
        k.dve(lambda: nc.vector.tensor_copy(out=acs[b], in_=B[4][:, 16:32]), [("bank", 4)], [kb("acs")])
        k.dve(lambda: nc.vector.tensor_scalar(out=nacs[b], in0=B[4][:, 16:32], scalar1=-1.0, scalar2=None, op0=ALU.mult), [("bank", 4)], [kb("nacs")])
        k.pool(lambda: nc.gpsimd.tensor_copy(out=arep[b], in_=bc_mid(at[b], 128)), [kb("at")], [kb("arep")])
        if STOP == c * 100 + 9:
            p.finish()
            return nc
        xs3 = B[5].rearrange("p (e q) -> p e q", e=8)
        for d in range(2):
            k.dve(lambda d=d: nc.vector.tensor_tensor(out=xdt[b][:, d, :].rearrange("p (e q) -> p e q", e=8), in0=xs3,
                                                      in1=bc_mid(dtt[b][:, 8 * d:8 * d + 8], 64), op=ALU.mult),
                  [("bank", 5), kb("dtt")], [(kb("xdt"), d)])
        if STOP == c * 100 + 10:
            p.finish()
            return nc
        for d in range(2):
            tri = k.triU if d == 0 else k.triL
            G = GU[b] if d == 0 else GL[b]
            for e in range(8):
                h = 8 * d + e
                bk = e // 4
                k.mm(B[bk][:, (e % 4) * 128:(e % 4 + 1) * 128], arep[b][:, h, :], tri, True, True, [kb("arep"), "c_triU", "c_triL"], [("bank", bk)])
            for e in range(8):
                h = 8 * d + e
                bk = e // 4
                k.dve(lambda h=h, e=e, bk=bk: nc.vector.tensor_scalar(out=E[b][:, h, :], in0=B[bk][:, (e % 4) * 128:(e % 4 + 1) * 128],
                                                                      scalar1=acs[b][:, h:h + 1], scalar2=0.0, op0=ALU.subtract, op1=ALU.min),
                      [("bank", bk), kb("acs")], [(kb("E"), d)])
            k.act(lambda: nc.scalar.activation(out=E[b][:, 8 * d:8 * d + 8, :], in_=E[b][:, 8 * d:8 * d + 8, :], func=AF.Exp), [(kb("E"), d)], [(kb("E"), d)])
            k.dve(lambda: nc.vector.tensor_tensor(out=M[b][:, 8 * d:8 * d + 8, :], in0=E[b][:, 8 * d:8 * d + 8, :],
                                                  in1=G.unsqueeze(1).broadcast_to([128, 8, 128]), op=ALU.mult),
                  [(kb("E"), d), kb("GU"), kb("GL")], [(kb("M"), d)])
            col = 127 if d == 0 else 0
            for bk in range(2):
                k.dve(lambda bk=bk: nc.vector.tensor_tensor(out=dte[b][:, 8 * d + 4 * bk:8 * d + 4 * bk + 4],
                                                            in0=B[bk].rearrange("p (e n) -> p e n", e=4)[:, :, col],
                                                            in1=nacs[b][:, 8 * d + 4 * bk:8 * d + 4 * bk + 4], op=ALU.add),
                      [("bank", bk), kb("nacs")], [(kb("dte"), d)])
            decd = dec_f[b] if d == 0 else dec_b[:, c, :]
            deck = kb("dec_f") if d == 0 else ("dec_b", c)
            for bk in range(2):
                k.act(lambda bk=bk: nc.scalar.activation(out=decd[:, 4 * bk:4 * bk + 4], in_=B[bk].rearrange("p (e n) -> p e n", e=4)[:, :, col], func=AF.Exp),
                      [("bank", bk)], [deck])
            k.act(lambda: nc.scalar.activation(out=dte[b][:, 8 * d:8 * d + 8], in_=dte[b][:, 8 * d:8 * d + 8], func=AF.Exp), [(kb("dte"), d)], [(kb("dte"), d)])
            k.dve(lambda: nc.vector.tensor_tensor(out=xdte[b][:, d, :].rearrange("p (e q) -> p e q", e=8),
                                                  in0=xdt[b][:, d, :].rearrange("p (e q) -> p e q", e=8),
                                                  in1=bc_mid(dte[b][:, 8 * d:8 * d + 8], 64), op=ALU.mult),
                  [(kb("xdt"), d), (kb("dte"), d)], [(kb("xdte"), d)])
            k.mm(B[6 + d], Btm[b], xdte[b][:, d, :], True, True, [kb("Btm"), (kb("xdte"), d)], [("bank", 6 + d)])
        if STOP == c * 100 + 11:
            p.finish()
            return nc
        k.act(lambda: nc.scalar.activation(out=ea_f[b], in_=acs[b][:, 0:8], func=AF.Exp), [kb("acs")], [kb("ea_f")])
        k.act(lambda: nc.scalar.activation(out=ea_b[:, c, :], in_=acs[b][:, 8:16], func=AF.Exp), [kb("acs")], [("ea_b", c)])
        if STOP == c * 100 + 12:
            p.finish()
            return nc
        for e in range(8):
            o = B[2][:, e * 64:(e + 1) * 64]
            k.mm(o, M[b][:, e, :], xdt[b][:, 0, e * 64:(e + 1) * 64], True, False, [(kb("M"), 0), (kb("xdt"), 0)], [("bank", 2)])
            k.mm(o, M[b][:, 8 + e, :], xdt[b][:, 1, e * 64:(e + 1) * 64], False, True, [(kb("M"), 1), (kb("xdt"), 1)], [("bank", 2)])
        k.mm(B[3], Ccm[:, c, :], hb_f, True, True, [("Ccm", c), "hb_f"], [("bank", 3)])
        k.dve(lambda: nc.vector.tensor_tensor(out=t1[b].rearrange("p (e q) -> p e q", e=8), in0=B[3].rearrange("p (e q) -> p e q", e=8),
                                              in1=bc_mid(ea_f[b], 64), op=ALU.mult), [("bank", 3), kb("ea_f")], [kb("t1")])
        k.dve(lambda: nc.vector.tensor_tensor(out=yp[b], in0=B[2], in1=t1[b], op=ALU.add), [("bank", 2), kb("t1")], [kb("yp")])
        k.dve(lambda: nc.vector.tensor_tensor(out=t1[b].rearrange("p (e q) -> p e q", e=8), in0=xs3, in1=bc_mid(Dsk, 64), op=ALU.mult),
              [("bank", 5), "rep", kb("t1")], [kb("t1")])
        k.pool(lambda: nc.gpsimd.tensor_tensor(out=yp[b], in0=yp[b], in1=t1[b], op=ALU.add), [kb("yp"), kb("t1")], [kb("yp")])
        p.dma("act", ypart_d[c], yp[b], reads=[kb("yp")], writes=[("yp_d", c)])
        if STOP == c * 100 + 13:
            p.finish()
            return nc
        k.dve(lambda: nc.vector.tensor_tensor(out=htmp[b].rearrange("p (e q) -> p e q", e=8), in0=h_f.rearrange("p (e q) -> p e q", e=8),
                                              in1=bc_mid(dec_f[b], 64), op=ALU.mult), ["h_f", kb("dec_f")], [kb("htmp")])
        k.dve(lambda: nc.vector.tensor_tensor(out=h_f, in0=htmp[b], in1=B[6], op=ALU.add), [kb("htmp"), ("bank", 6)], ["h_f"])
        k.act(lambda: nc.scalar.copy(out=hb_f, in_=h_f), ["h_f"], ["hb_f"])
        k.act(lambda: nc.scalar.copy(out=sbt[b], in_=B[7]), [("bank", 7)], [kb("sbt")])
        p.dma("act", sb_d[c], sbt[b], reads=[kb("sbt")], writes=[("sb_d", c)])
        if c == 1:
            pass

    k.arena_reset()
    h_b = k.sb("h_b", [128, 512])
    hb_b = k.sb("hb_b", [128, 512], BF16)
    k.dve(lambda: nc.vector.memset(h_b, 0.0), [], ["h_b"])
    k.dve(lambda: nc.vector.memset(hb_b, 0.0), [], ["hb_b"])
    ypl = tmp("ypl", [128, 512])
    szl = tmp("szl", [128, 512])
    sbl = tmp("sbl", [128, 512])
    t2 = tmp("t2", [128, 512])
    y2 = tmp("y2", [128, 512])
    gt = tmp("gt", [128, 512])
    sq = tmp("sq", [128, 512])
    ss = tmp("ss", [128, 1])
    rs = tmp("rs", [128, 1])
    ho = tmp("ho", [128, 512], BF16)
    order = [c for c in ORDER_B if c < nchunks]
    for it, c in enumerate(order):
        b = it % 2
        kb = lambda n: f"{n}{b}"
        p.dma("sp", ypl[b], ypart_d[c], reads=[("yp_d", c)], writes=[kb("ypl")])
        p.dma("act", szl[b], sz_d[c], reads=[("sz_d", c)], writes=[kb("szl")])
        p.dma("sp", sbl[b], sb_d[c], reads=[("sb_d", c)], writes=[kb("sbl")])
        bk = 2 + b
        k.mm(B[bk], Ccm[:, c, :], hb_b, True, True, [("Ccm", c), "hb_b"], [("bank", bk)])
        k.dve(lambda: nc.vector.tensor_tensor(out=t2[b].rearrange("p (e q) -> p e q", e=8), in0=B[bk].rearrange("p (e q) -> p e q", e=8),
                                              in1=bc_mid(ea_b[:, c, :], 64), op=ALU.mult), [("bank", bk), ("ea_b", c)], [kb("t2")])
        k.pool(lambda: nc.gpsimd.tensor_tensor(out=y2[b], in0=t2[b], in1=ypl[b], op=ALU.add), [kb("t2"), kb("ypl")], [kb("y2")])
        k.dve(lambda: nc.vector.tensor_tensor(out=gt[b], in0=y2[b], in1=szl[b], op=ALU.mult), [kb("y2"), kb("szl")], [kb("gt")])
        k.act(lambda: nc.scalar.activation(out=sq[b], in_=gt[b], func=AF.Square, accum_out=ss[b]), [kb("gt")], [kb("sq"), kb("ss")])
        k.act(lambda: nc.scalar.activation(out=rs[b], in_=ss[b], func=AF.Sqrt, scale=1.0 / 512, bias=LN_EPS), [kb("ss")], [kb("rs")])
        k.dve(lambda: nc.vector.reciprocal(out=rs[b], in_=rs[b]), [kb("rs")], [kb("rs")])
        k.dve(lambda: nc.vector.scalar_tensor_tensor(out=ho[b], in0=gt[b], scalar=rs[b], in1=normw, op0=ALU.mult, op1=ALU.mult),
              [kb("gt"), kb("rs"), "rep"], [kb("ho")])
        p.dma("act", hs[c * 128:(c + 1) * 128, :], ho[b], reads=[kb("ho")], writes=["hs"])
        k.dve(lambda: nc.vector.tensor_tensor(out=t2[b].rearrange("p (e q) -> p e q", e=8), in0=h_b.rearrange("p (e q) -> p e q", e=8),
                                              in1=bc_mid(dec_b[:, c, :], 64), op=ALU.mult), ["h_b", ("dec_b", c), kb("t2")], [kb("t2")])
        k.dve(lambda: nc.vector.tensor_tensor(out=h_b, in0=t2[b], in1=sbl[b], op=ALU.add), [kb("t2"), kb("sbl")], ["h_b"])
        k.act(lambda: nc.scalar.copy(out=hb_b, in_=h_b), ["h_b"], ["hb_b"])

    if nchunks == NCH:
        k.arena_reset()
        gl = k.asb("gl_full", [128, T])
        ca = k.asb("conv_acc", [128, T])
        for m in range(2):
            p.dma("sp", gl, glu_d[m], reads=["glu_d"], writes=["gl_full"])
            w15 = dw[:, m, 15:16]
            k.dve(lambda: nc.vector.tensor_scalar(out=ca, in0=gl, scalar1=w15, scalar2=db[:, m:m + 1], op0=ALU.mult, op1=ALU.add),
                  ["gl_full", "dw", "db"], ["conv_acc"])
            cl, gll = ca[:, CTX:], gl[:, CTX:]
            for s in range(31):
                o = s - 15
                if o == 0:
                    continue
                ws = dw[:, m, s:s + 1]
                lo, hi = max(0, -o), min(CTX, CTX - o)
                k.dve(lambda: nc.vector.scalar_tensor_tensor(out=ca[:, lo:hi], in0=gl[:, lo + o:hi + o], scalar=ws, in1=ca[:, lo:hi], op0=ALU.mult, op1=ALU.add),
                      ["gl_full", "dw", "conv_acc"], ["conv_acc"])
                if m == 0:
                    c3 = cl.rearrange("p (r c) -> p r c", c=64)
                    g3 = gll.rearrange("p (r c) -> p r c", c=64)
                    lo, hi = max(0, -o), min(64, 64 - o)
                    k.dve(lambda: nc.vector.scalar_tensor_tensor(out=c3[:, :, lo:hi], in0=g3[:, :, lo + o:hi + o], scalar=ws, in1=c3[:, :, lo:hi], op0=ALU.mult, op1=ALU.add),
                          ["gl_full", "dw", "conv_acc"], ["conv_acc"])
                else:
                    lo, hi = max(0, -o) * 64, min(128, 128 - o) * 64
                    k.dve(lambda: nc.vector.scalar_tensor_tensor(out=cl[:, lo:hi], in0=gll[:, lo + o * 64:hi + o * 64], scalar=ws, in1=cl[:, lo:hi], op0=ALU.mult, op1=ALU.add),
                          ["gl_full", "dw", "conv_acc"], ["conv_acc"])
            p.dma("sp", hpre[m], ca, reads=["conv_acc"], writes=["hpre"])
    p.finish()
    return nc


def pA_inputs(Xall, mod_i, inp, i):
    XT = Xall.T
    xcs = np.zeros((NCH, 128, 16, 132), np.float32)
    for c in range(NCH):
        seg_lo, seg_hi = (0, CTX) if c < 2 else (CTX, T)
        s = c * 128
        lo, hi = max(s - 2, seg_lo), min(s + 130, seg_hi)
        blk = XT[:, lo:hi].reshape(16, 128, hi - lo).transpose(1, 0, 2)
        xcs[c, :, :, lo - (s - 2):hi - (s - 2)] = blk
    m = mod_i
    modv = np.stack([m[0, 0:D], m[0, D:2 * D], m[1, 0:D], m[1, D:2 * D]], axis=-1)
    modv = np.ascontiguousarray(modv.reshape(16, 128, 4).transpose(1, 0, 2))
    maps = []
    for g in range(NCORES):
        cols = np.concatenate([
            O_XBC + g * 512 + np.arange(512),
            O_XBC + D_INNER + g * 128 + np.arange(128),
            O_XBC + D_INNER + D_BC + g * 128 + np.arange(128),
            O_GLU + g * 128 + np.arange(128),
            O_GLU + 1024 + g * 128 + np.arange(128),
            O_GLU + 2048 + g * 128 + np.arange(128),
            O_GLU + 2048 + 1024 + g * 128 + np.arange(128),
            O_Z + g * 512 + np.arange(512),
            O_DT + g * 8 + np.arange(8),
            O_DT + 64 + g * 8 + np.arange(8),
        ])
        wgm = inp["w_in"][i][:, cols]
        wgm = np.ascontiguousarray(wgm.reshape(16, 128, WG_COLS).transpose(1, 0, 2))
        xbc_cols = cols[:768] - O_XBC
        cwm = inp["ssm_conv_w"][i][:, xbc_cols]
        cwm = np.ascontiguousarray(cwm.reshape(5, 6, 128).transpose(2, 1, 0))
        cbm = np.ascontiguousarray(inp["ssm_conv_b"][i][xbc_cols].reshape(6, 128).T)
        cch = np.concatenate([g * 128 + np.arange(128), 1024 + g * 128 + np.arange(128)])
        dwm = np.ascontiguousarray(inp["conv_dw_w"][i][:, cch].reshape(31, 2, 128).transpose(2, 1, 0))
        dbm = np.ascontiguousarray(inp["conv_dw_b"][i][cch].reshape(2, 128).T)
        rep = np.concatenate([
            inp["ssm_dt_bias"][i][0, g * 8:(g + 1) * 8], inp["ssm_dt_bias"][i][1, g * 8:(g + 1) * 8],
            inp["ssm_a_log"][i][0, g * 8:(g + 1) * 8], inp["ssm_a_log"][i][1, g * 8:(g + 1) * 8],
            inp["ssm_d"][i][g * 8:(g + 1) * 8], inp["ssm_norm_w"][i][g * 512:(g + 1) * 512]]).astype(np.float32)
        rep = np.ascontiguousarray(np.broadcast_to(rep[None, :], (128, 552)))
        maps.append({"xc": xcs, "mod": modv, "wg": wgm, "cw": cwm, "cb": cbm, "dw": dwm, "db": dbm, "rep": rep})
    return maps


NB = 528
BLKS = [(0, 512), (512, 528)]


def ln_cm(k, X, xkey, nb0, nb1, post, pfx):
    nc = k.nc
    n = nb1 - nb0
    B = k.banks
    for kt in range(16):
        sq = k.lnsq[kt % 2][:, 0:n]
        k.act(lambda: nc.scalar.activation(out=sq, in_=X[:, kt, nb0:nb1], func=AF.Square), [xkey], [f"lnsq{kt % 2}"])
        k.mm(B[6][:, 0:n], k.ones, X[:, kt, nb0:nb1], kt == 0, kt == 15, ["c_ones", xkey], [("bank", 6)])
        k.mm(B[7][:, 0:n], k.ones, sq, kt == 0, kt == 15, ["c_ones", f"lnsq{kt % 2}"], [("bank", 7)])
    mt, m2, rs = k.lnm[0][:, 0:n], k.lnm[1][:, 0:n], k.lnm[2][:, 0:n]
    k.dve(lambda: nc.vector.tensor_scalar(out=mt, in0=B[6][:, 0:n], scalar1=1.0 / D, scalar2=None, op0=ALU.mult), [("bank", 6)], ["lnm0"])
    k.dve(lambda: nc.vector.tensor_tensor(out=m2, in0=mt, in1=mt, op=ALU.mult), ["lnm0"], ["lnm1"])
    k.dve(lambda: nc.vector.scalar_tensor_tensor(out=rs, in0=B[7][:, 0:n], scalar=1.0 / D, in1=m2, op0=ALU.mult, op1=ALU.subtract), [("bank", 7), "lnm1"], ["lnm2"])
    k.act(lambda: nc.scalar.activation(out=rs, in_=rs, func=AF.Sqrt, bias=LN_EPS), ["lnm2"], ["lnm2"])
    k.dve(lambda: nc.vector.reciprocal(out=rs, in_=rs), ["lnm2"], ["lnm2"])
    for kt in range(16):
        t = k.lnt[kt % 2][:, 0:n]
        tk = f"lnt{kt % 2}"
        k.dve(lambda: nc.vector.tensor_tensor(out=t, in0=X[:, kt, nb0:nb1], in1=mt, op=ALU.subtract), [xkey, "lnm0"], [tk])
        k.dve(lambda: nc.vector.tensor_tensor(out=t, in0=t, in1=rs, op=ALU.mult), [tk, "lnm2"], [tk])
        post(kt, t, tk)


def ln_scratch(k):
    k.lnsq = [k.sb(f"lnsq{i}", [128, 512]) for i in range(2)]
    k.lnt = [k.sb(f"lnt{i}", [128, 512]) for i in range(2)]
    k.lnm = [k.sb(f"lnm{i}", [128, 512]) for i in range(3)]


def build_pB():
    k = KB()
    nc, p = k.nc, k.p
    B = k.banks
    xTd = k.din("xT", [128, 16, NB])
    modd = k.din("mod", [128, 16, 12])
    hsTd = k.din("hsT", [128, 32, NB], BF16)
    hpTd = k.din("hpT", [128, 16, NB])
    lnpd = k.din("lnp", [128, 16, 4])
    wsod = k.din("wso", [16, 128, 32, 128])
    wcod = k.din("wco", [16, 128, 16, 128])
    wgd = k.din("wgate", [32, 128, 16, 128])
    wod = k.din("wo", [16, 128, 16, 128])
    wrd = k.din("wr", [128, 16, 16])
    x1o = k.dout("x1T", [128, 16, NB])
    u2o = k.dout("u2T", [128, 16, NB], BF16)
    affo = k.dout("aff", [NB, 16])
    k.consts()
    ln_scratch(k)
    mod = k.sb("modt", [128, 16, 12])
    lnp = k.sb("lnpt", [128, 16, 4])
    wr = k.sb("wrt", [128, 16, 16])
    scl = k.sb("scl", [128, 16, 4])
    p.dma("sp", mod, modd, writes=["mod"])
    p.dma("sp", lnp, lnpd, writes=["lnp"])
    p.dma("sp", wr, wrd, writes=["wr"])
    for j, col in enumerate([1, 7, 4, 10]):
        k.dve(lambda j=j, col=col: nc.vector.tensor_scalar(out=scl[:, :, j], in0=mod[:, :, col], scalar1=1.0, scalar2=None, op0=ALU.add), ["mod", "scl"], ["scl"])
    uT = k.sb("uT", [128, 16, NB], BF16)
    hcv = k.sb("hcv", [128, 16, NB], BF16)
    hsT = k.sb("hsT_s", [128, 32, NB], BF16)
    mg = k.sb("mg", [128, 16, NB], BF16)
    R = k.sb("R", [128, 16, NB])
    u2f = mg.rearrange("p a b -> p (a b)").bitcast(F32)[:, 0:2048].rearrange("p (a b) -> p a b", a=16)
    u2b = uT
    for kt in range(0, 32, 8):
        p.dma("act", hsT[:, kt:kt + 8, :], hsTd[:, kt:kt + 8, :], writes=["hsT"])
    SEG = [(0, 512, 0), (512, 528, 1)]
    p.dma("sp", R, xTd, writes=["R"])
    for kt in range(16):
        for (a, b_, kind) in SEG:
            k.act(lambda kt=kt, a=a, b_=b_, kind=kind: nc.scalar.activation(out=uT[:, kt, a:b_], in_=R[:, kt, a:b_], func=AF.Identity,
                                                                             bias=mod[:, kt, 6 * kind:6 * kind + 1], scale=scl[:, kt, kind:kind + 1]),
                  ["R", "mod", "scl"], ["uT"])
    p.barrier()
    p.dma("sp", R, hpTd, reads=[], writes=["R"])
    yt = [k.sb(f"yt{i}", [128, 512]) for i in range(2)]
    sgt = [k.sb(f"sgt{i}", [128, 512]) for i in range(2)]
    for (a, b_) in BLKS:
        n = b_ - a

        def post(kt, t, tk, a=a, b_=b_, n=n):
            y, s = yt[kt % 2][:, 0:n], sgt[kt % 2][:, 0:n]
            k.act(lambda: nc.scalar.activation(out=y, in_=t, func=AF.Identity, bias=lnp[:, kt, 1:2], scale=lnp[:, kt, 0:1]), [tk, "lnp"], [f"yt{kt % 2}"])
            k.act(lambda: nc.scalar.activation(out=s, in_=y, func=AF.Sigmoid), [f"yt{kt % 2}"], [f"sgt{kt % 2}"])
            k.dve(lambda: nc.vector.tensor_tensor(out=hcv[:, kt, a:b_], in0=y, in1=s, op=ALU.mult), [f"yt{kt % 2}", f"sgt{kt % 2}"], ["hcv"])
        ln_cm(k, R, "R", a, b_, post, "c")
    p.barrier()
    wso = [k.sb(f"wso{i}", [128, 32, 128], BF16) for i in range(2)]
    wco = [k.sb(f"wco{i}", [128, 16, 128], BF16) for i in range(2)]
    wga = [k.sb(f"wga{i}", [128, 16, 128], BF16) for i in range(2)]
    wgb = [k.sb(f"wgb{i}", [128, 16, 128], BF16) for i in range(2)]
    wo = [k.sb(f"wo{i}", [128, 16, 128], BF16) for i in range(2)]
    s1 = [k.sb(f"s1_{i}", [128, 512]) for i in range(2)]
    s2 = [k.sb(f"s2_{i}", [128, 512]) for i in range(2)]
    tm_ = [k.sb(f"tm_{i}", [128, 512]) for i in range(2)]
    for dt in range(16):
        w = dt % 2
        p.dma("pool", wso[w], wsod[dt], writes=[f"wso{w}"])
        p.dma("pool", wco[w], wcod[dt], writes=[f"wco{w}"])
        p.dma("pool", wga[w], wgd[dt], writes=[f"wga{w}"])
        p.dma("pool", wgb[w], wgd[16 + dt], writes=[f"wgb{w}"])
        for bi, (a, b_) in enumerate(BLKS):
            n = b_ - a
            for kt in range(32):
                k.mm(B[0][:, 0:n], wso[w][:, kt, :], hsT[:, kt, a:b_], kt == 0, kt == 31, [f"wso{w}", "hsT"], [("bank", 0)])
            for kt in range(16):
                k.mm(B[1][:, 0:n], wga[w][:, kt, :], uT[:, kt, a:b_], kt == 0, kt == 15, [f"wga{w}", "uT"], [("bank", 1)])
            for kt in range(16):
                k.mm(B[2][:, 0:n], wco[w][:, kt, :], hcv[:, kt, a:b_], kt == 0, kt == 15, [f"wco{w}", "hcv"], [("bank", 2)])
            for kt in range(16):
                k.mm(B[3][:, 0:n], wgb[w][:, kt, :], uT[:, kt, a:b_], kt == 0, kt == 15, [f"wgb{w}", "uT"], [("bank", 3)])
            q = bi % 2
            k.act(lambda: nc.scalar.activation(out=s1[q][:, 0:n], in_=B[1][:, 0:n], func=AF.Sigmoid), [("bank", 1)], [f"s1_{q}"])
            k.act(lambda: nc.scalar.activation(out=s2[q][:, 0:n], in_=B[3][:, 0:n], func=AF.Sigmoid), [("bank", 3)], [f"s2_{q}"])
            k.dve(lambda: nc.vector.tensor_tensor(out=tm_[q][:, 0:n], in0=B[0][:, 0:n], in1=s1[q][:, 0:n], op=ALU.mult), [("bank", 0), f"s1_{q}"], [f"tm_{q}"])
            k.dve(lambda: nc.vector.tensor_tensor(out=s2[q][:, 0:n], in0=B[2][:, 0:n], in1=s2[q][:, 0:n], op=ALU.mult), [("bank", 2), f"s2_{q}"], [f"s2_{q}"])
            k.dve(lambda: nc.vector.tensor_tensor(out=mg[:, dt, a:b_], in0=tm_[q][:, 0:n], in1=s2[q][:, 0:n], op=ALU.add), [f"tm_{q}", f"s2_{q}"], [("mg", dt)])
    p.barrier()
    p.dma("sp", R, xTd, reads=[], writes=["R"])
    MG = [("mg", d_) for d_ in range(16)]
    for dt in range(16):
        w = dt % 2
        p.dma("pool", wo[w], wod[dt], writes=[f"wo{w}"])
        for bi, (a, b_) in enumerate(BLKS):
            n = b_ - a
            bk = 4 + bi % 2
            for kt in range(16):
                k.mm(B[bk][:, 0:n], wo[w][:, kt, :], mg[:, kt, a:b_], kt == 0, kt == 15, [f"wo{w}"] + MG, [("bank", bk)])
            for (sa, sb_, kind) in SEG:
                lo, hi = max(a, sa), min(b_, sb_)
                if lo >= hi:
                    continue
                q = bi % 2
                k.dve(lambda lo=lo, hi=hi, kind=kind: nc.vector.tensor_scalar(out=tm_[q][:, 0:hi - lo], in0=B[bk][:, lo - a:hi - a], scalar1=mod[:, dt, 6 * kind + 2:6 * kind + 3],
                                                                            scalar2=None, op0=ALU.mult), [("bank", bk), "mod"], [f"tm_{q}"])
                k.dve(lambda lo=lo, hi=hi: nc.vector.scalar_tensor_tensor(out=R[:, dt, lo:hi], in0=R[:, dt, lo:hi], scalar=ALPHA, in1=tm_[q][:, 0:hi - lo], op0=ALU.mult, op1=ALU.add),
                      ["R", f"tm_{q}"], ["R"])
    p.barrier()
    for (a, b_) in BLKS:
        def post1(kt, t, tk, a=a, b_=b_):
            k.act(lambda: nc.scalar.activation(out=R[:, kt, a:b_], in_=t, func=AF.Identity, bias=lnp[:, kt, 3:4], scale=lnp[:, kt, 2:3]), [tk, "lnp", "R"], ["R"])
        ln_cm(k, R, "R", a, b_, post1, "l")
    p.barrier()
    p.dma("sp", x1o, R, reads=["R"])
    for kt in range(16):
        for (a, b_, kind) in SEG:
            k.act(lambda kt=kt, a=a, b_=b_, kind=kind: nc.scalar.activation(out=u2b[:, kt, a:b_], in_=R[:, kt, a:b_], func=AF.Identity,
                                                                             bias=mod[:, kt, 6 * kind + 3:6 * kind + 4], scale=scl[:, kt, 2 + kind:3 + kind]),
                  ["R", "mod", "scl"], ["u2b"])
    p.dma("act", u2o, u2b, reads=["u2b"])
    lg = [k.sb(f"lg{i}", [128, 16]) for i in range(2)]
    mx = [k.sb(f"mx{i}", [128, 1]) for i in range(2)]
    sm = [k.sb(f"sm{i}", [128, 1]) for i in range(2)]
    tiles = [(i * 128, min((i + 1) * 128, NB)) for i in range((NB + 127) // 128)]
    for ti, (a, b_) in enumerate(tiles):
        n = b_ - a
        q = ti % 2
        kind = 0 if a < 512 else 1
        for kt in range(16):
            k.act(lambda kt=kt: nc.scalar.activation(out=u2f[:, kt, 0:n], in_=R[:, kt, a:b_], func=AF.Identity,
                                                     bias=mod[:, kt, 6 * kind + 3:6 * kind + 4], scale=scl[:, kt, 2 + kind:3 + kind]), ["R", "mod", "scl"], ["u2f"])
        bk = 4 + q
        for kt in range(16):
            k.mm(B[bk][0:n, 0:16], u2f[:, kt, 0:n], wr[:, kt, :], kt == 0, kt == 15, ["u2f", "wr"], [("bank", bk)])
        k.dve(lambda: nc.vector.tensor_reduce(out=mx[q][0:n], in_=B[bk][0:n, 0:16], axis=AX.X, op=ALU.max), [("bank", bk)], [f"mx{q}"])
        k.dve(lambda: nc.vector.tensor_scalar(out=mx[q][0:n], in0=mx[q][0:n], scalar1=-1.0, scalar2=None, op0=ALU.mult), [f"mx{q}"], [f"mx{q}"])
        k.act(lambda: nc.scalar.activation(out=lg[q][0:n], in_=B[bk][0:n, 0:16], func=AF.Exp, bias=mx[q][0:n], accum_out=sm[q][0:n]), [("bank", bk), f"mx{q}"], [f"lg{q}", f"sm{q}"])
        k.dve(lambda: nc.vector.reciprocal(out=sm[q][0:n], in_=sm[q][0:n]), [f"sm{q}"], [f"sm{q}"])
        k.dve(lambda: nc.vector.tensor_scalar(out=lg[q][0:n], in0=lg[q][0:n], scalar1=sm[q][0:n], scalar2=None, op0=ALU.mult), [f"lg{q}", f"sm{q}"], [f"lg{q}"])
        p.dma("sp", affo[a:b_, :], lg[q][0:n], reads=[f"lg{q}"])
    p.finish()
    return nc


def cm_tiles(w, nk):
    K, N = w.shape
    return np.ascontiguousarray(w.reshape(nk, 128, N // 128, 128).transpose(2, 1, 0, 3))


def cmvec(v):
    return v.reshape(16, 128).T


def to_cm(Xtok):
    n, C = Xtok.shape
    return np.ascontiguousarray(Xtok.T.reshape(C // 128, 128, n).transpose(1, 0, 2))


def from_cm(Xcm):
    p_, kt, n = Xcm.shape
    return np.ascontiguousarray(Xcm.transpose(2, 1, 0).reshape(n, kt * 128))


CBLK = [(0, CTX)] + [(CTX + i * 512, CTX + (i + 1) * 512) for i in range(SEQ // 512)]


def build_pC(nblk=len(CBLK)):
    k = KB()
    nc, p = k.nc, k.p
    B = k.banks
    u2d = k.din("u2T", [128, 16, T], BF16)
    affd = k.din("affT", [2, T])
    wgd = k.din("wg", [2, 24, 128, 16, 128])
    wud = k.din("wu", [2, 24, 128, 16, 128])
    wdd = k.din("wd", [2, 16, 128, 24, 128])
    outd = k.dout("part", [128, 16, T])
    k.consts()
    aff = k.sb("aff", [2, T])
    wts = k.sb("wts", [2, T])
    p.dma("sp", aff, affd, writes=["aff"])
    thr = k.sb("thr", [2, 2])
    for si, (a, b_, kk) in enumerate([(0, CTX, 2 * CTX // NEXP), (CTX, T, 2 * SEQ // NEXP)]):
        lo = k.sb(f"lo{si}", [2, 1])
        hi = k.sb(f"hi{si}", [2, 1])
        mid = k.sb(f"mid{si}", [2, 1])
        cnt = k.sb(f"cnt{si}", [2, 1])
        ge = k.sb(f"ge{si}", [2, 1])
        dl = k.sb(f"dl{si}", [2, 1])
        K_ = [f"bis{si}"]
        k.dve(lambda: nc.vector.memset(lo, 0.0), [], K_)
        k.dve(lambda: nc.vector.memset(hi, 1.0), K_, K_)
        for it in range(36):
            k.dve(lambda: nc.vector.tensor_tensor(out=mid, in0=lo, in1=hi, op=ALU.add), K_, K_)
            k.dve(lambda: nc.vector.tensor_scalar(out=mid, in0=mid, scalar1=0.5, scalar2=None, op0=ALU.mult), K_, K_)
            k.dve(lambda: nc.vector.tensor_scalar(out=wts[:, a:b_], in0=aff[:, a:b_], scalar1=mid, scalar2=0.0, op0=ALU.is_ge, op1=ALU.add, accum_out=cnt),
                  K_ + ["aff"], K_ + ["wts"])
            k.dve(lambda: nc.vector.tensor_scalar(out=ge, in0=cnt, scalar1=float(kk) - 0.5, scalar2=None, op0=ALU.is_ge), K_, K_)
            k.dve(lambda: nc.vector.tensor_tensor(out=dl, in0=mid, in1=lo, op=ALU.subtract), K_, K_)
            k.dve(lambda: nc.vector.tensor_tensor(out=dl, in0=dl, in1=ge, op=ALU.mult), K_, K_)
            k.dve(lambda: nc.vector.tensor_tensor(out=lo, in0=lo, in1=dl, op=ALU.add), K_, K_)
            k.dve(lambda: nc.vector.tensor_tensor(out=dl, in0=hi, in1=mid, op=ALU.subtract), K_, K_)
            k.dve(lambda: nc.vector.tensor_tensor(out=dl, in0=dl, in1=ge, op=ALU.mult), K_, K_)
            k.dve(lambda: nc.vector.tensor_tensor(out=hi, in0=mid, in1=dl, op=ALU.add), K_, K_)
        k.dve(lambda: nc.vector.tensor_scalar(out=wts[:, a:b_], in0=aff[:, a:b_], scalar1=lo, scalar2=None, op0=ALU.is_ge), K_ + ["aff", "wts"], ["wts"])
        k.dve(lambda: nc.vector.tensor_tensor(out=wts[:, a:b_], in0=wts[:, a:b_], in1=aff[:, a:b_], op=ALU.mult), ["wts", "aff"], ["wts"])
    sel = k.sb("sel", [2, 2, 128])
    k.dve(lambda: nc.vector.tensor_copy(out=sel, in_=k.ident[0:2, 0:2].unsqueeze(2).broadcast_to([2, 2, 128])), ["c_ident"], ["sel"])
    ub = [k.sb(f"ub{i}", [128, 16, 512], BF16) for i in range(2)]
    wgt = [k.sb(f"wgt{i}", [128, 16, 128], BF16) for i in range(2)]
    wut = [k.sb(f"wut{i}", [128, 16, 128], BF16) for i in range(2)]
    wdt = [k.sb(f"wdt{i}", [128, 24, 128], BF16) for i in range(2)]
    hT = k.sb("hT", [128, 24, 512], BF16)
    wbc = k.sb("wbc", [128, 512])
    sg = [k.sb(f"sg{i}", [128, 512]) for i in range(2)]
    t1 = [k.sb(f"t1_{i}", [128, 512]) for i in range(2)]
    acc = k.sb("acc", [128, 16, 512])
    for bi, (a, b_) in enumerate(CBLK[:nblk]):
        n = b_ - a
        u = ub[bi % 2]
        p.dma("sp", u[:, :, 0:n], u2d[:, :, a:b_], writes=[f"ub{bi % 2}"])
        for le in range(2):
            k.mm(B[7][:, 0:n], sel[:, le, :], wts[:, a:b_], True, True, ["sel", "wts"], [("bank", 7)])
            k.act(lambda: nc.scalar.copy(out=wbc[:, 0:n], in_=B[7][:, 0:n]), [("bank", 7)], ["wbc"])
            for ft in range(24):
                w = ft % 2
                p.dma("pool", wgt[w], wgd[le, ft], writes=[f"wgt{w}"])
                p.dma("pool", wut[w], wud[le, ft], writes=[f"wut{w}"])
                bg, bu = 2 * w, 2 * w + 1
                for kt in range(16):
                    k.mm(B[bg][:, 0:n], wgt[w][:, kt, :], u[:, kt, 0:n], kt == 0, kt == 15, [f"wgt{w}", f"ub{bi % 2}"], [("bank", bg)])
                for kt in range(16):
                    k.mm(B[bu][:, 0:n], wut[w][:, kt, :], u[:, kt, 0:n], kt == 0, kt == 15, [f"wut{w}", f"ub{bi % 2}"], [("bank", bu)])
                k.act(lambda: nc.scalar.activation(out=sg[w][:, 0:n], in_=B[bg][:, 0:n], func=AF.Sigmoid), [("bank", bg)], [f"sg{w}"])
                k.dve(lambda: nc.vector.tensor_tensor(out=sg[w][:, 0:n], in0=B[bg][:, 0:n], in1=sg[w][:, 0:n], op=ALU.mult), [("bank", bg), f"sg{w}"], [f"sg{w}"])
                k.dve(lambda: nc.vector.tensor_tensor(out=t1[w][:, 0:n], in0=B[bu][:, 0:n], in1=sg[w][:, 0:n], op=ALU.mult), [("bank", bu), f"sg{w}"], [f"t1_{w}"])
                k.pool(lambda: nc.gpsimd.tensor_tensor(out=hT[:, ft, 0:n], in0=t1[w][:, 0:n], in1=wbc[:, 0:n], op=ALU.mult), [f"t1_{w}", "wbc"], [("hT", ft)])
            HT = [("hT", f_) for f_ in range(24)]
            for dt in range(16):
                w = dt % 2
                p.dma("pool", wdt[w], wdd[le, dt], writes=[f"wdt{w}"])
                bk = 4 + w
                for ft in range(24):
                    k.mm(B[bk][:, 0:n], wdt[w][:, ft, :], hT[:, ft, 0:n], ft == 0, ft == 23, [f"wdt{w}"] + HT, [("bank", bk)])
                if le == 0:
                    k.act(lambda: nc.scalar.copy(out=acc[:, dt, 0:n], in_=B[bk][:, 0:n]), [("bank", bk)], [("acc", dt)])
                else:
                    k.dve(lambda: nc.vector.tensor_tensor(out=acc[:, dt, 0:n], in0=acc[:, dt, 0:n], in1=B[bk][:, 0:n], op=ALU.add), [("bank", bk), ("acc", dt)], [("acc", dt)])
        p.dma("act", outd[:, :, a:b_], acc[:, :, 0:n], reads=[("acc", d_) for d_ in range(16)])
    p.finish()
    return nc


def build_pD():
    k = KB()
    nc, p = k.nc, k.p
    partd = k.din("parts", [8, 128, 16, NB])
    x1d = k.din("x1T", [128, 16, NB])
    modd = k.din("mod", [128, 16, 12])
    lnpd = k.din("lnp", [128, 16, 2])
    outd = k.dout("x2T", [128, 16, NB])
    k.consts()
    ln_scratch(k)
    mod = k.sb("modt", [128, 16, 12])
    lnp = k.sb("lnpt", [128, 16, 2])
    R = k.sb("R", [128, 16, NB])
    S = k.sb("S", [128, 16, NB])
    pt = [k.sb(f"pt{i}", [128, 16, NB]) for i in range(2)]
    p.dma("sp", mod, modd, writes=["mod"])
    p.dma("sp", lnp, lnpd, writes=["lnp"])
    p.dma("sp", R, x1d, writes=["R"])
    p.dma("act", S, partd[0], writes=["S"])
    for j in range(1, 8):
        q = j % 2
        p.dma("sp" if q else "act", pt[q], partd[j], writes=[f"pt{q}"])
        k.dve(lambda: nc.vector.tensor_tensor(out=S, in0=S, in1=pt[q], op=ALU.add), ["S", f"pt{q}"], ["S"])
    SEG = [(0, 512, 0), (512, 528, 1)]
    for kt in range(16):
        for (a, b_, kind) in SEG:
            k.dve(lambda kt=kt, a=a, b_=b_, kind=kind: nc.vector.tensor_scalar(out=S[:, kt, a:b_], in0=S[:, kt, a:b_], scalar1=mod[:, kt, 6 * kind + 5:6 * kind + 6],
                                                                             scalar2=None, op0=ALU.mult), ["S", "mod"], ["S"])
        k.dve(lambda kt=kt: nc.vector.scalar_tensor_tensor(out=R[:, kt, :], in0=R[:, kt, :], scalar=ALPHA, in1=S[:, kt, :], op0=ALU.mult, op1=ALU.add), ["R", "S"], ["R"])
    p.barrier()
    for (a, b_) in BLKS:
        def post(kt, t, tk, a=a, b_=b_):
            k.act(lambda: nc.scalar.activation(out=S[:, kt, a:b_], in_=t, func=AF.Identity, bias=lnp[:, kt, 1:2], scale=lnp[:, kt, 0:1]), [tk, "lnp", "S"], ["S"])
        ln_cm(k, R, "R", a, b_, post, "d")
    p.dma("sp", outd, S, reads=["S"])
    p.finish()
    return nc


_PROGS = {}


def _prog(name, fn):
    if name not in _PROGS:
        _PROGS[name] = fn()
    return _PROGS[name]


def _run(nc, maps):
    return run_bass_kernel_spmd(nc, maps, core_ids=list(range(NCORES))).results


def _tok(h, g):
    j = h * NCORES + g
    return np.concatenate([CTX + j * 512 + np.arange(512), j * 16 + np.arange(16)])


def kernel(**inp):
    inp = {k_: np.asarray(v) for k_, v in inp.items()}
    mod = run_p0(inp)
    Xall = np.concatenate([inp["ctx"][0], inp["x"][0]], axis=0).astype(np.float32)
    for i in range(DEPTH):
        resA = _run(_prog("A", build_pA), pA_inputs(Xall, mod[i], inp, i))
        hs_all = np.zeros((T, D_INNER), NPBF)
        hpre_all = np.zeros((T, D), np.float32)
        for g in range(NCORES):
            hs_all[:, g * 512:(g + 1) * 512] = resA[g]["hs"]
            hpre_all[:, g * 128:(g + 1) * 128] = resA[g]["hpre"][0].T
            hpre_all[:, 1024 + g * 128:1024 + (g + 1) * 128] = resA[g]["hpre"][1].T
        del resA
        m = mod[i]
        mod12 = np.stack([m[kind, j * D:(j + 1) * D] for kind in range(2) for j in range(6)], axis=-1)
        mod12 = np.ascontiguousarray(mod12.reshape(16, 128, 12).transpose(1, 0, 2))
        lnpB = np.stack([inp["conv_ln_g"][i], inp["conv_ln_b"][i], inp["ln1_g"][i], inp["ln1_b"][i]], axis=-1)
        lnpB = np.ascontiguousarray(lnpB.reshape(16, 128, 4).transpose(1, 0, 2))
        wso = cm_tiles(inp["w_ssm_out"][i], 32)
        wco = cm_tiles(inp["w_conv_out"][i], 16)
        wgate = cm_tiles(np.ascontiguousarray(inp["w_in"][i][:, O_GATE:]), 16)
        wo = cm_tiles(inp["w_o"][i], 16)
        wr = np.ascontiguousarray(inp["w_router"][i].reshape(16, 128, 16).transpose(1, 0, 2))
        X1all = np.zeros((T, D), np.float32)
        U2all = np.zeros((T, D), NPBF)
        AFFall = np.zeros((T, NEXP), np.float32)
        for h in range(2):
            maps = []
            for g in range(NCORES):
                tok = _tok(h, g)
                maps.append({"xT": to_cm(Xall[tok]), "mod": mod12, "hsT": to_cm(hs_all[tok]), "hpT": to_cm(hpre_all[tok]), "lnp": lnpB,
                             "wso": wso, "wco": wco, "wgate": wgate, "wo": wo, "wr": wr})
            resB = _run(_prog("B", build_pB), maps)
            for g in range(NCORES):
                tok = _tok(h, g)
                X1all[tok] = from_cm(resB[g]["x1T"])
                U2all[tok] = from_cm(resB[g]["u2T"])
                AFFall[tok] = resB[g]["aff"]
            del resB, maps
        del wso, wco, wgate, wo
        u2T = to_cm(U2all)
        maps = []
        for j in range(NCORES):
            es = [2 * j, 2 * j + 1]
            maps.append({"u2T": u2T, "affT": np.ascontiguousarray(AFFall[:, es].T),
                         "wg": np.stack([cm_tiles(inp["w_exp_gate"][i][e], 16) for e in es]),
                         "wu": np.stack([cm_tiles(inp["w_exp_up"][i][e], 16) for e in es]),
                         "wd": np.stack([cm_tiles(inp["w_exp_down"][i][e], 24) for e in es])})
        resC = _run(_prog("C", build_pC), maps)
        PART = [resC[j]["part"] for j in range(NCORES)]
        del resC, maps
        lnpD = np.stack([inp["ln2_g"][i], inp["ln2_b"][i]], axis=-1)
        lnpD = np.ascontiguousarray(lnpD.reshape(16, 128, 2).transpose(1, 0, 2))
        X2all = np.zeros((T, D), np.float32)
        for h in range(2):
            maps = []
            for g in range(NCORES):
                tok = _tok(h, g)
                maps.append({"parts": np.stack([np.ascontiguousarray(PART[j][:, :, tok]) for j in range(NCORES)]),
                             "x1T": to_cm(X1all[tok]), "mod": mod12, "lnp": lnpD})
            resD = _run(_prog("D", build_pD), maps)
            for g in range(NCORES):
                X2all[_tok(h, g)] = from_cm(resD[g]["x2T"])
            del resD, maps
        Xall = X2all
    return np.ascontiguousarray(Xall[CTX:].reshape(1, SEQ, D)).astype(np.float32)
```

```python
import numpy as np
import ml_dtypes
import concourse.bass as bass
import concourse.mybir as mybir
from concourse.bass_utils import run_bass_kernel_spmd

F32 = mybir.dt.float32
BF16 = mybir.dt.bfloat16
I32 = mybir.dt.int32
U32 = mybir.dt.uint32
AF = mybir.ActivationFunctionType
ALU = mybir.AluOpType
AX = mybir.AxisListType
NPBF = ml_dtypes.bfloat16

NCORES = 8


class Prog:
    ENG = ("pe", "dve", "act", "pool", "sp")

    def __init__(self, nc, n_dma_sems=6, same_engine_sync=True):
        self.nc = nc
        self.e = {"pe": nc.tensor, "dve": nc.vector, "act": nc.scalar, "pool": nc.gpsimd, "sp": nc.sync}
        self.sem = {k: nc.alloc_semaphore("c_" + k) for k in self.ENG}
        self.cnt = {k: 0 for k in self.ENG}
        self.same = same_engine_sync
        self.dsem = {}
        self.dcnt = {}
        self.drr = {}
        for q in ("sp", "act", "pool"):
            self.dsem[q] = [nc.alloc_semaphore(f"d_{q}{i}") for i in range(n_dma_sems)]
            self.dcnt[q] = [0] * n_dma_sems
            self.drr[q] = 0
        self.seen = {k: {} for k in self.ENG}
        self.buf = {}
        self.semobj = {}
        for k in self.ENG:
            self.semobj[("c", k)] = self.sem[k]
        for q in self.dsem:
            for i, s in enumerate(self.dsem[q]):
                self.semobj[("d", q, i)] = s
        self.ninst = 0

    def _deps(self, reads, writes):
        deps = []
        for k in reads:
            st = self.buf.get(k)
            if st and st["w"]:
                deps.append(st["w"])
        for k in writes:
            st = self.buf.get(k)
            if st:
                if st["w"]:
                    deps.append(st["w"])
                deps.extend(st["r"])
        return deps

    def _wait(self, eng, deps):
        best = {}
        for sk, v in deps:
            if sk == ("c", eng) and (eng == "pe" or not self.same):
                continue
            if v > best.get(sk, 0):
                best[sk] = v
        for sk, v in best.items():
            if self.seen[eng].get(sk, 0) >= v:
                continue
            self.e[eng].wait_ge(self.semobj[sk], v)
            self.seen[eng][sk] = v

    def _mark(self, reads, writes, tag):
        for k in writes:
            self.buf[k] = {"w": tag, "r": []}
        for k in reads:
            if k in writes:
                continue
            st = self.buf.setdefault(k, {"w": None, "r": []})
            st["r"] = [t for t in st["r"] if t[0] != tag[0]] + [tag]

    def op(self, eng, fn, reads=(), writes=()):
        self._wait(eng, self._deps(reads, writes))
        ins = fn()
        self.cnt[eng] += 1
        ins.then_inc(self.sem[eng], 1)
        self._mark(reads, writes, (("c", eng), self.cnt[eng]))
        self.ninst += 1
        return ins

    def dma(self, q, out, in_, reads=(), writes=(), **kw):
        i = self.drr[q]
        self.drr[q] = (i + 1) % len(self.dsem[q])
        sk = ("d", q, i)
        deps = self._deps(reads, writes)
        if self.dcnt[q][i] > 0:
            deps.append((sk, self.dcnt[q][i]))
        self._wait(q, deps)
        ins = self.e[q].dma_start(out=out, in_=in_, **kw)
        self.dcnt[q][i] += 16
        ins.then_inc(self.semobj[sk], 16)
        self._mark(reads, writes, (sk, self.dcnt[q][i]))
        self.ninst += 1
        return ins

    def dma_custom(self, q, fn, reads=(), writes=()):
        i = self.drr[q]
        self.drr[q] = (i + 1) % len(self.dsem[q])
        sk = ("d", q, i)
        deps = self._deps(reads, writes)
        if self.dcnt[q][i] > 0:
            deps.append((sk, self.dcnt[q][i]))
        self._wait(q, deps)
        ins = fn()
        self.dcnt[q][i] += 16
        ins.then_inc(self.semobj[sk], 16)
        self._mark(reads, writes, (sk, self.dcnt[q][i]))
        self.ninst += 1
        return ins

    def barrier(self):
        deps = []
        for k in self.ENG:
            if self.cnt[k]:
                deps.append((("c", k), self.cnt[k]))
        for q in self.dsem:
            for i, v in enumerate(self.dcnt[q]):
                if v:
                    deps.append((("d", q, i), v))
        old = self.same
        self.same = False
        for e in self.ENG:
            self._wait(e, deps)
        self.same = old

    def finish(self):
        deps = []
        for k in self.ENG:
            if self.cnt[k] and k != "sp":
                deps.append((("c", k), self.cnt[k]))
        for q in self.dsem:
            for i, v in enumerate(self.dcnt[q]):
                if v:
                    deps.append((("d", q, i), v))
        self.same = True
        self._wait("sp", deps)
        self.e["sp"].nop() if hasattr(self.e["sp"], "nop") else None


D = 2048
SEQ = 8192
CTX = 256
T = SEQ + CTX
NCH = T // 128
DEPTH = 2
D_INNER = 4096
D_BC = 1024
D_XBC = 6144
O_Z = 0
O_XBC = 4096
O_DT = O_XBC + D_XBC
O_GLU = O_DT + 128
O_GATE = O_GLU + 4096
D_PROJ = O_GATE + 4096
NEXP = 16
DEXP = 3072
ALPHA = (2 * DEPTH) ** 0.25
LN_EPS = 1e-5
WG_COLS = 1808


class KB:
    def __init__(self):
        self.nc = bass.Bass("TRN2", target_bir_lowering=False)
        self.p = Prog(self.nc)
        self.banks = [self.nc.alloc_psum_tensor(f"bank{i}", [128, 512], F32).ap() for i in range(8)]

    def din(self, name, shape, dt=F32):
        return self.nc.dram_tensor(name, list(shape), dt, kind="ExternalInput").ap()

    def dout(self, name, shape, dt=F32):
        return self.nc.dram_tensor(name, list(shape), dt, kind="ExternalOutput").ap()

    def dscr(self, name, shape, dt=F32):
        return self.nc.dram_tensor(name, list(shape), dt, kind="Internal").ap()

    def sb(self, name, shape, dt=F32):
        return self.nc.alloc_sbuf_tensor("s_" + name, list(shape), dt).ap()

    def arena_init(self, words):
        self.arena = self.nc.alloc_sbuf_tensor("s_arena", [128, words], F32).ap()
        self.arena_words = words
        self.arena_off = 0

    def arena_reset(self):
        self.p.barrier()
        self.arena_off = 0

    def asb(self, name, shape, dt=F32):
        n = int(np.prod(shape[1:]))
        words = n if dt == F32 or dt == I32 or dt == U32 else (n + 1) // 2
        assert self.arena_off + words <= self.arena_words, (name, self.arena_off, words)
        ap = self.arena[:, self.arena_off:self.arena_off + words]
        self.arena_off += words
        if dt != F32:
            ap = ap.bitcast(dt)
            if ap.shape[1] != n:
                ap = ap[:, 0:n]
        if len(shape) == 3:
            ap = ap.rearrange("p (a b) -> p a b", a=shape[1])
        return ap

    def mm(self, out, lhsT, rhs, start, stop, r, w):
        nc = self.nc
        return self.p.op("pe", lambda: nc.tensor.matmul(out, lhsT, rhs, start=start, stop=stop), reads=r, writes=w)

    def tr(self, out, in_, ident, r, w):
        nc = self.nc
        return self.p.op("pe", lambda: nc.tensor.transpose(out, in_, ident), reads=r, writes=w)

    def dve(self, fn, r, w):
        return self.p.op("dve", fn, reads=r, writes=w)

    def act(self, fn, r, w):
        return self.p.op("act", fn, reads=r, writes=w)

    def pool(self, fn, r, w):
        return self.p.op("pool", fn, reads=r, writes=w)

    def consts(self):
        nc = self.nc
        ones = self.sb("c_ones", [128, 128])
        self.ident = self.sb("c_ident", [128, 128])
        self.identb = self.sb("c_identb", [128, 128], BF16)
        self.triU = self.sb("c_triU", [128, 128])
        self.triL = self.sb("c_triL", [128, 128])
        self.pool(lambda: nc.gpsimd.memset(ones, 1.0), [], ["c_ones"])
        self.pool(lambda: nc.gpsimd.affine_select(self.triU, ones, [[1, 128]], ALU.is_ge, 0.0, base=0, channel_multiplier=-1), ["c_ones"], ["c_triU"])
        self.pool(lambda: nc.gpsimd.affine_select(self.triL, ones, [[-1, 128]], ALU.is_ge, 0.0, base=0, channel_multiplier=1), ["c_ones"], ["c_triL"])
        self.pool(lambda: nc.gpsimd.affine_select(self.ident, self.triU, [[-1, 128]], ALU.is_ge, 0.0, base=0, channel_multiplier=1), ["c_triU"], ["c_ident"])
        self.pool(lambda: nc.gpsimd.tensor_copy(self.identb, self.ident), ["c_ident"], ["c_identb"])
        self.ones = ones


def bc_mid(ap, n):
    P, H = ap.shape
    return ap.unsqueeze(2).broadcast_to([P, H, n])


def build_p0():
    k = KB()
    nc = k.nc
    cc = k.din("cc", [128, 16, 2])
    w = k.din("w", [2, 128, 16, 1536])
    b = k.din("b", [128, 24])
    o = k.dout("o", [128, 24, 2])
    cct = k.sb("cct", [128, 16, 2])
    sg = k.sb("sg", [128, 16, 2])
    s = k.sb("s", [128, 16, 2])
    wt = k.sb("wt", [128, 16, 1536])
    bt = k.sb("bt", [128, 24])
    ot = k.sb("ot", [128, 24, 2])
    p = k.p
    p.dma("sp", cct, cc, writes=["cct"])
    p.dma("sp", bt, b, writes=["bt"])
    k.act(lambda: nc.scalar.activation(out=sg, in_=cct, func=AF.Sigmoid), ["cct"], ["sg"])
    k.dve(lambda: nc.vector.tensor_tensor(out=s, in0=cct, in1=sg, op=ALU.mult), ["cct", "sg"], ["s"])
    for i in range(2):
        for kt in range(16):
            p.dma("sp" if kt % 2 == 0 else "act", wt[:, kt, :], w[i, :, kt, :], writes=[("wt", kt)])
        for ct in range(12):
            ps = k.banks[ct % 4][:, 0:2]
            for kt in range(16):
                k.mm(ps, wt[:, kt, ct * 128:(ct + 1) * 128], s[:, kt, :], kt == 0, kt == 15, ["s", ("wt", kt)], [("bank", ct % 4)])
            j = i * 12 + ct
            k.act(lambda: nc.scalar.activation(out=ot[:, j, :], in_=ps, func=AF.Identity, bias=bt[:, j:j + 1]), [("bank", ct % 4), "bt"], ["ot"])
    p.dma("sp", o, ot, reads=["ot"])
    p.finish()
    return nc


def run_p0(inp):
    nc = build_p0()
    cvec = np.stack([inp["c"][0], inp["c_ctx"]], axis=-1)
    cc = np.ascontiguousarray(cvec.reshape(16, 128, 2).transpose(1, 0, 2))
    maps = []
    for g in range(NCORES):
        c0 = g * 1536
        wsl = inp["w_ada"][:, :, c0:c0 + 1536]
        wl = np.ascontiguousarray(wsl.reshape(2, 16, 128, 1536).transpose(0, 2, 1, 3))
        bl = inp["b_ada"][:, c0:c0 + 1536].reshape(2, 12, 128).transpose(2, 0, 1).reshape(128, 24)
        maps.append({"cc": cc, "w": wl, "b": np.ascontiguousarray(bl)})
    res = run_bass_kernel_spmd(nc, maps, core_ids=list(range(NCORES)))
    mod = np.zeros((2, 2, 6 * D), np.float32)
    for g in range(NCORES):
        og = res.results[g]["o"]
        og = og.reshape(128, 2, 12, 2)
        mod[:, :, g * 1536:(g + 1) * 1536] = og.transpose(1, 3, 2, 0).reshape(2, 2, 1536)
    return mod


ORDER_B = [1, 0] + list(range(NCH - 1, 1, -1))


STOP = 0


def build_pA(nchunks=NCH):
    k = KB()
    nc, p = k.nc, k.p
    xc = k.din("xc", [NCH, 128, 16, 132])
    modd = k.din("mod", [128, 16, 4])
    wg = k.din("wg", [128, 16, WG_COLS])
    cwd = k.din("cw", [128, 6, 5])
    cbd = k.din("cb", [128, 6])
    dwd = k.din("dw", [128, 2, 31])
    dbd = k.din("db", [128, 2])
    repd = k.din("rep", [128, 552])
    hs = k.dout("hs", [T, 512], BF16)
    hpre = k.dout("hpre", [2, 128, T])
    ypart_d = k.dscr("ypart_d", [NCH, 128, 512])
    sz_d = k.dscr("sz_d", [NCH, 128, 512])
    sb_d = k.dscr("sb_d", [NCH, 128, 512])
    glu_d = k.dscr("glu_d", [2, 128, T])

    k.consts()
    k.arena_init(43500)
    B = k.banks
    wb = k.asb("wb", [128, 16, WG_COLS], BF16)
    for kt in range(16):
        p.dma("pool", wb[:, kt, :], wg[:, kt, :], writes=[("wb", kt)])
    WBK = [("wb", kt) for kt in range(16)]
    mod = k.sb("modt", [128, 16, 4])
    cw = k.sb("cw", [128, 6, 5])
    cb = k.sb("cb", [128, 6])
    dw = k.sb("dwt", [128, 2, 31])
    db = k.sb("dbt", [128, 2])
    rep = k.sb("rep", [128, 552])
    p.dma("sp", mod, modd, writes=["mod"])
    p.dma("sp", cw, cwd, writes=["cw"])
    p.dma("sp", cb, cbd, writes=["cb"])
    p.dma("sp", dw, dwd, writes=["dw"])
    p.dma("sp", db, dbd, writes=["db"])
    p.dma("sp", rep, repd, writes=["rep"])
    scl = k.sb("scl", [128, 16, 2])
    k.dve(lambda: nc.vector.tensor_scalar(out=scl[:, :, 0], in0=mod[:, :, 1], scalar1=1.0, scalar2=None, op0=ALU.add), ["mod"], ["scl"])
    k.dve(lambda: nc.vector.tensor_scalar(out=scl[:, :, 1], in0=mod[:, :, 3], scalar1=1.0, scalar2=None, op0=ALU.add), ["mod", "scl"], ["scl"])
    dtb = rep[:, 0:16]
    Aneg = k.sb("Aneg", [128, 16])
    k.act(lambda: nc.scalar.activation(out=Aneg, in_=rep[:, 16:32], func=AF.Exp), ["rep"], ["Aneg"])
    k.dve(lambda: nc.vector.tensor_scalar(out=Aneg, in0=Aneg, scalar1=-1.0, scalar2=None, op0=ALU.mult), ["Aneg"], ["Aneg"])
    Dsk = rep[:, 32:40]
    normw = rep[:, 40:552]

    Ccm = k.sb("Ccm", [128, NCH, 128], BF16)
    ea_b = k.sb("ea_b", [128, NCH, 8])
    dec_b = k.sb("dec_b", [128, NCH, 8])
    h_f = k.sb("h_f", [128, 512])
    hb_f = k.sb("hb_f", [128, 512], BF16)
    k.dve(lambda: nc.vector.memset(h_f, 0.0), [], ["h_f"])
    k.dve(lambda: nc.vector.memset(hb_f, 0.0), [], ["hb_f"])

    def tmp(name, shape, dt=F32, n=2):
        if n == 1:
            a = k.asb(name, shape, dt)
            return [a, a]
        return [k.asb(f"{name}{i}", shape, dt) for i in range(n)]

    xt = tmp("xt", [128, 16, 132])
    ut = tmp("ut", [128, 16, 132], BF16)
    raw = tmp("raw", [128, 6, 132])
    acc = tmp("acc", [128, 6, 128])
    sgm = tmp("sgm", [128, 6, 128])
    xcm = tmp("xcm", [128, 4, 128])
    Bcm = tmp("Bcm", [128, 128], BF16)
    Btm = tmp("Btm", [128, 128], BF16)
    dtt = tmp("dtt", [128, 16])
    at = tmp("at", [128, 16])
    arep = tmp("arep", [128, 16, 128], n=1)
    acs = tmp("acs", [128, 16])
    nacs = tmp("nacs", [128, 16])
    GU = tmp("GU", [128, 128])
    GL = tmp("GL", [128, 128])
    E = tmp("E", [128, 16, 128], n=1)
    M = tmp("M", [128, 16, 128], BF16, n=1)
    xdt = tmp("xdt", [128, 2, 512], BF16)
    dte = tmp("dte", [128, 16])
    xdte = tmp("xdte", [128, 2, 512], BF16)
    ea_f = tmp("ea_f", [128, 8])
    dec_f = tmp("dec_f", [128, 8])
    t1 = tmp("t1", [128, 512])
    yp = tmp("yp", [128, 512])
    zs = tmp("zs", [128, 512])
    szt = tmp("szt", [128, 512])
    sbt = tmp("sbt", [128, 512])
    gsg = tmp("gsg", [128, 2, 128])
    glu = tmp("glu", [128, 2, 128])
    htmp = tmp("htmp", [128, 512])

    for c in range(nchunks):
        b = c % 2
        kb = lambda n: f"{n}0" if n in ("arep", "E", "M") else f"{n}{b}"
        mi = 1 if c < 2 else 0
        p.dma("sp" if c % 2 == 0 else "act", xt[b], xc[c], writes=[kb("xt")])
        for kt in range(16):
            (k.act if kt % 2 == 0 else k.dve)(
                (lambda kt=kt: nc.scalar.activation(out=ut[b][:, kt, :], in_=xt[b][:, kt, :], func=AF.Identity,
                                                    bias=mod[:, kt, 2 * mi:2 * mi + 1], scale=scl[:, kt, mi:mi + 1]))
                if kt % 2 == 0 else
                (lambda kt=kt: nc.vector.tensor_scalar(out=ut[b][:, kt, :], in0=xt[b][:, kt, :], scalar1=scl[:, kt, mi:mi + 1],
                                                       scalar2=mod[:, kt, 2 * mi:2 * mi + 1], op0=ALU.mult, op1=ALU.add)),
                [kb("xt"), "mod", "scl"], [(kb("ut"), kt)])
        if STOP == c * 100 + 1:
            p.finish()
            return nc
        UT = [(kb("ut"), kt) for kt in range(16)]
        for m in range(6):
            bk = m // 3
            o = B[bk][:, (m % 3) * 132:(m % 3) * 132 + 132]
            for kt in range(16):
                k.mm(o, wb[:, kt, m * 128:(m + 1) * 128], ut[b][:, kt, :], kt == 0, kt == 15, [("wb", kt), (kb("ut"), kt)], [("bank", bk)])
        for bk in range(2):
            k.act(lambda bk=bk: nc.scalar.copy(out=raw[b][:, 3 * bk:3 * bk + 3, :], in_=B[bk][:, 0:396].rearrange("p (m n) -> p m n", m=3)),
                  [("bank", bk)], [kb("raw")])
        if STOP == c * 100 + 2:
            p.finish()
            return nc
        if c == 0 or c == 2:
            k.dve(lambda: nc.vector.memset(raw[b][:, :, 0:2], 0.0), [kb("raw")], [kb("raw")])
        if c == 1 or c == NCH - 1:
            k.dve(lambda: nc.vector.memset(raw[b][:, :, 130:132], 0.0), [kb("raw")], [kb("raw")])
        for m in range(4):
            o = B[2][:, m * 128:(m + 1) * 128]
            for kt in range(16):
                k.mm(o, wb[:, kt, 768 + m * 128:768 + (m + 1) * 128], ut[b][:, kt, 2:130], kt == 0, kt == 15, [("wb", kt), (kb("ut"), kt)], [("bank", 2)])
        for kt in range(16):
            k.mm(B[3], ut[b][:, kt, 2:130], wb[:, kt, 1280:1792], kt == 0, kt == 15, [("wb", kt), (kb("ut"), kt)], [("bank", 3)])
        for kt in range(16):
            k.mm(B[4][:, 0:16], ut[b][:, kt, 2:130], wb[:, kt, 1792:1808], kt == 0, kt == 15, [("wb", kt), (kb("ut"), kt)], [("bank", 4)])
        if STOP == c * 100 + 3:
            p.finish()
            return nc
        k.act(lambda: nc.scalar.activation(out=gsg[b], in_=B[2][:, 256:512].rearrange("p (m n) -> p m n", m=2), func=AF.Sigmoid), [("bank", 2)], [kb("gsg")])
        k.dve(lambda: nc.vector.tensor_tensor(out=glu[b], in0=B[2][:, 0:256].rearrange("p (m n) -> p m n", m=2), in1=gsg[b], op=ALU.mult),
              [("bank", 2), kb("gsg")], [kb("glu")])
        p.dma("sp", glu_d[:, :, c * 128:(c + 1) * 128].rearrange("m p n -> p m n"), glu[b], reads=[kb("glu")], writes=["glu_d"])
        k.act(lambda: nc.scalar.activation(out=zs[b], in_=B[3], func=AF.Sigmoid), [("bank", 3)], [kb("zs")])
        k.dve(lambda: nc.vector.tensor_tensor(out=szt[b], in0=B[3], in1=zs[b], op=ALU.mult), [("bank", 3), kb("zs")], [kb("szt")])
        p.dma("sp", sz_d[c], szt[b], reads=[kb("szt")], writes=[("sz_d", c)])
        if STOP == c * 100 + 4:
            p.finish()
            return nc
        k.dve(lambda: nc.vector.tensor_tensor(out=dtt[b], in0=B[4][:, 0:16], in1=dtb, op=ALU.add), [("bank", 4), "rep"], [kb("dtt")])
        k.act(lambda: nc.scalar.activation(out=dtt[b], in_=dtt[b], func=AF.Exp), [kb("dtt")], [kb("dtt")])
        k.act(lambda: nc.scalar.activation(out=dtt[b], in_=dtt[b], func=AF.Ln, bias=1.0), [kb("dtt")], [kb("dtt")])
        k.dve(lambda: nc.vector.tensor_tensor(out=at[b], in0=dtt[b], in1=Aneg, op=ALU.mult), [kb("dtt"), "Aneg"], [kb("at")])
        if STOP == c * 100 + 5:
            p.finish()
            return nc
        k.dve(lambda: nc.vector.tensor_tensor(out=acc[b], in0=raw[b][:, :, 0:128], in1=bc_mid(cw[:, :, 0], 128), op=ALU.mult), [kb("raw"), "cw"], [kb("acc")])
        for s in range(1, 5):
            k.dve(lambda s=s: nc.vector.tensor_tensor(out=sgm[b], in0=raw[b][:, :, s:s + 128], in1=bc_mid(cw[:, :, s], 128), op=ALU.mult), [kb("raw"), "cw"], [kb("sgm")])
            k.dve(lambda: nc.vector.tensor_tensor(out=acc[b], in0=acc[b], in1=sgm[b], op=ALU.add), [kb("acc"), kb("sgm")], [kb("acc")])
        k.dve(lambda: nc.vector.tensor_tensor(out=acc[b], in0=acc[b], in1=bc_mid(cb, 128), op=ALU.add), [kb("acc"), "cb"], [kb("acc")])
        k.act(lambda: nc.scalar.activation(out=sgm[b], in_=acc[b], func=AF.Sigmoid), [kb("acc")], [kb("sgm")])
        k.dve(lambda: nc.vector.tensor_tensor(out=xcm[b], in0=acc[b][:, 0:4, :], in1=sgm[b][:, 0:4, :], op=ALU.mult), [kb("acc"), kb("sgm")], [kb("xcm")])
        k.dve(lambda: nc.vector.tensor_tensor(out=Bcm[b], in0=acc[b][:, 4, :], in1=sgm[b][:, 4, :], op=ALU.mult), [kb("acc"), kb("sgm")], [kb("Bcm")])
        k.dve(lambda: nc.vector.tensor_tensor(out=Ccm[:, c, :], in0=acc[b][:, 5, :], in1=sgm[b][:, 5, :], op=ALU.mult), [kb("acc"), kb("sgm")], [("Ccm", c)])
        if STOP == c * 100 + 6:
            p.finish()
            return nc
        for m in range(4):
            k.tr(B[5][:, m * 128:(m + 1) * 128], xcm[b][:, m, :], k.ident, [kb("xcm"), "c_ident"], [("bank", 5)])
        b4bf = B[4][:, 256:320].bitcast(BF16)
        k.tr(b4bf, Bcm[b], k.identb, [kb("Bcm"), "c_identb"], [("bank", 4)])
        k.act(lambda: nc.scalar.copy(out=Btm[b], in_=b4bf), [("bank", 4)], [kb("Btm")])
        if STOP == c * 100 + 7:
            p.finish()
            return nc
        k.mm(B[4][:, 128:256], Bcm[b], Ccm[:, c, :], True, True, [kb("Bcm"), ("Ccm", c)], [("bank", 4)])
        k.dve(lambda: nc.vector.tensor_tensor(out=GU[b], in0=B[4][:, 128:256], in1=k.triU, op=ALU.mult), [("bank", 4), "c_triU"], [kb("GU")])
        k.dve(lambda: nc.vector.tensor_tensor(out=GL[b], in0=B[4][:, 128:256], in1=k.triL, op=ALU.mult), [("bank", 4), "c_triL"], [kb("GL")])
        if STOP == c * 100 + 8:
            p.finish()
            return nc
        k.mm(B[4][:, 16:24], k.triU, at[b][:, 0:8], True, True, ["c_triU", kb("at")], [("bank", 4)])
        k.mm(B[4][:, 24:32], k.triL, at[b][:, 8:16], True, True, ["c_triL", kb("at")], [("bank", 4)])
        k.dve(lambda: nc.vector.tensor_copy(out=acs[b], in_=B[4][:, 16:32]), [("bank", 4)], [kb("acs")])
        k.dve(lambda: nc.vector.tensor_scalar(out=nacs[b], in0=B[4][:, 16:32], scalar1=-1.0, scalar2=None, op0=ALU.mult), [("bank", 4)], [kb("nacs")])
        k.pool(lambda: nc.gpsimd.tensor_copy(out=arep[b], in_=bc_mid(at[b], 128)), [kb("at")], [kb("arep")])
        if STOP == c * 100 + 9:
            p.finish()
            return nc
        xs3 = B[5].rearrange("p (e q) -> p e q", e=8)
        for d in range(2):
            k.dve(lambda d=d: nc.vector.tensor_tensor(out=xdt[b][:, d, :].rearrange("p (e q) -> p e q", e=8), in0=xs3,
                                                      in1=bc_mid(dtt[b][:, 8 * d:8 * d + 8], 64), op=ALU.mult),
                  [("bank", 5), kb("dtt")], [(kb("xdt"), d)])
        if STOP == c * 100 + 10:
            p.finish()
            return nc
        for d in range(2):
            tri = k.triU if d == 0 else k.triL
            G = GU[b] if d == 0 else GL[b]
            for e in range(8):
                h = 8 * d + e
                bk = e // 4
                k.mm(B[bk][:, (e % 4) * 128:(e % 4 + 1) * 128], arep[b][:, h, :], tri, True, True, [kb("arep"), "c_triU", "c_triL"], [("bank", bk)])
            for e in range(8):
                h = 8 * d + e
                bk = e // 4
                k.dve(lambda h=h, e=e, bk=bk: nc.vector.tensor_scalar(out=E[b][:, h, :], in0=B[bk][:, (e % 4) * 128:(e % 4 + 1) * 128],
                                                                      scalar1=acs[b][:, h:h + 1], scalar2=0.0, op0=ALU.subtract, op1=ALU.min),
                      [("bank", bk), kb("acs")], [(kb("E"), d)])
            k.act(lambda: nc.scalar.activation(out=E[b][:, 8 * d:8 * d + 8, :], in_=E[b][:, 8 * d:8 * d + 8, :], func=AF.Exp), [(kb("E"), d)], [(kb("E"), d)])
            k.dve(lambda: nc.vector.tensor_tensor(out=M[b][:, 8 * d:8 * d + 8, :], in0=E[b][:, 8 * d:8 * d + 8, :],
                                                  in1=G.unsqueeze(1).broadcast_to([128, 8, 128]), op=ALU.mult),
                  [(kb("E"), d), kb("GU"), kb("GL")], [(kb("M"), d)])
            col = 127 if d == 0 else 0
            for bk in range(2):
                k.dve(lambda bk=bk: nc.vector.tensor_tensor(out=dte[b][:, 8 * d + 4 * bk:8 * d + 4 * bk + 4],
                                                            in0=B[bk].rearrange("p (e n) -> p e n", e=4)[:, :, col],
                                                            in1=nacs[b][:, 8 * d + 4 * bk:8 * d + 4 * bk + 4], op=ALU.add),
                      [("bank", bk), kb("nacs")], [(kb("dte"), d)])
            decd = dec_f[b] if d == 0 else dec_b[:, c, :]
            deck = kb("dec_f") if d == 0 else ("dec_b", c)
            for bk in range(2):
                k.act(lambda bk=bk: nc.scalar.activation(out=decd[:, 4 * bk:4 * bk + 4], in_=B[bk].rearrange("p (e n) -> p e n", e=4)[:, :, col], func=AF.Exp),
                      [("bank", bk)], [deck])
            k.act(lambda: nc.scalar.activation(out=dte[b][:, 8 * d:8 * d + 8], in_=dte[b][:, 8 * d:8 * d + 8], func=AF.Exp), [(kb("dte"), d)], [(kb("dte"), d)])
            k.dve(lambda: nc.vector.tensor_tensor(out=xdte[b][:, d, :].rearrange("p (e q) -> p e q", e=8),
                                                  in0=xdt[b][:, d, :].rearrange("p (e q) -> p e q", e=8),
                                                  in1=bc_mid(dte[b][:, 8 * d:8 * d + 8], 64), op=ALU.mult),
                  [(kb("xdt"), d), (kb("dte"), d)], [(kb("xdte"), d)])
            k.mm(B[6 + d], Btm[b], xdte[b][:, d, :], True, True, [kb("Btm"), (kb("xdte"), d)], [("bank", 6 + d)])
        if STOP == c * 100 + 11:
            p.finish()
            return nc
        k.act(lambda: nc.scalar.activation(out=ea_f[b], in_=acs[b][:, 0:8], func=AF.Exp), [kb("acs")], [kb("ea_f")])
        k.act(lambda: nc.scalar.activation(out=ea_b[:, c, :], in_=acs[b][:, 8:16], func=AF.Exp), [kb("acs")], [("ea_b", c)])
        if STOP == c * 100 + 12:
            p.finish()
            return nc
        for e in range(8):
            o = B[2][:, e * 64:(e + 1) * 64]
            k.mm(o, M[b][:, e, :], xdt[b][:, 0, e * 64:(e + 1) * 64], True, False, [(kb("M"), 0), (kb("xdt"), 0)], [("bank", 2)])
            k.mm(o, M[b][:, 8 + e, :], xdt[b][:, 1, e * 64:(e + 1) * 64], False, True, [(kb("M"), 1), (kb("xdt"), 1)], [("bank", 2)])
        k.mm(B[3], Ccm[:, c, :], hb_f, True, True, [("Ccm", c), "hb_f"], [("bank", 3)])
        k.dve(lambda: nc.vector.tensor_tensor(out=t1[b].rearrange("p (e q) -> p e q", e=8), in0=B[3].rearrange("p (e q) -> p e q", e=8),
                                              in1=bc_mid(ea_f[b], 64), op=ALU.mult), [("bank", 3), kb("ea_f")], [kb("t1")])
        k.dve(lambda: nc.vector.tensor_tensor(out=yp[b], in0=B[2], in1=t1[b], op=ALU.add), [("bank", 2), kb("t1")], [kb("yp")])
        k.dve(lambda: nc.vector.tensor_tensor(out=t1[b].rearrange("p (e q) -> p e q", e=8), in0=xs3, in1=bc_mid(Dsk, 64), op=ALU.mult),
              [("bank", 5), "rep", kb("t1")], [kb("t1")])
        k.pool(lambda: nc.gpsimd.tensor_tensor(out=yp[b], in0=yp[b], in1=t1[b], op=ALU.add), [kb("yp"), kb("t1")], [kb("yp")])
        p.dma("act", ypart_d[c], yp[b], reads=[kb("yp")], writes=[("yp_d", c)])
        if STOP == c * 100 + 13:
            p.finish()
            return nc
        k.dve(lambda: nc.vector.tensor_tensor(out=htmp[b].rearrange("p (e q) -> p e q", e=8), in0=h_f.rearrange("p (e q) -> p e q", e=8),
                                              in1=bc_mid(dec_f[b], 64), op=ALU.mult), ["h_f", kb("dec_f")], [kb("htmp")])
        k.dve(lambda: nc.vector.tensor_tensor(out=h_f, in0=htmp[b], in1=B[6], op=ALU.add), [kb("htmp"), ("bank", 6)], ["h_f"])
        k.act(lambda: nc.scalar.copy(out=hb_f, in_=h_f), ["h_f"], ["hb_f"])
        k.act(lambda: nc.scalar.copy(out=sbt[b], in_=B[7]), [("bank", 7)], [kb("sbt")])
        p.dma("act", sb_d[c], sbt[b], reads=[kb("sbt")], writes=[("sb_d", c)])
        if c == 1:
            pass

    k.arena_reset()
    h_b = k.sb("h_b", [128, 512])
    hb_b = k.sb("hb_b", [128, 512], BF16)
    k.dve(lambda: nc.vector.memset(h_b, 0.0), [], ["h_b"])
    k.dve(lambda: nc.vector.memset(hb_b, 0.0), [], ["hb_b"])
    ypl = tmp("ypl", [128, 512])
    szl = tmp("szl", [128, 512])
    sbl = tmp("sbl", [128, 512])
    t2 = tmp("t2", [128, 512])
    y2 = tmp("y2", [128, 512])
    gt = tmp("gt", [128, 512])
    sq = tmp("sq", [128, 512])
    ss = tmp("ss", [128, 1])
    rs = tmp("rs", [128, 1])
    ho = tmp("ho", [128, 512], BF16)
    order = [c for c in ORDER_B if c < nchunks]
    for it, c in enumerate(order):
        b = it % 2
        kb = lambda n: f"{n}{b}"
        p.dma("sp", ypl[b], ypart_d[c], reads=[("yp_d", c)], writes=[kb("ypl")])
        p.dma("act", szl[b], sz_d[c], reads=[("sz_d", c)], writes=[kb("szl")])
        p.dma("sp", sbl[b], sb_d[c], reads=[("sb_d", c)], writes=[kb("sbl")])
        bk = 2 + b
        k.mm(B[bk], Ccm[:, c, :], hb_b, True, True, [("Ccm", c), "hb_b"], [("bank", bk)])
        k.dve(lambda: nc.vector.tensor_tensor(out=t2[b].rearrange("p (e q) -> p e q", e=8), in0=B[bk].rearrange("p (e q) -> p e q", e=8),
                                              in1=bc_mid(ea_b[:, c, :], 64), op=ALU.mult), [("bank", bk), ("ea_b", c)], [kb("t2")])
        k.pool(lambda: nc.gpsimd.tensor_tensor(out=y2[b], in0=t2[b], in1=ypl[b], op=ALU.add), [kb("t2"), kb("ypl")], [kb("y2")])
        k.dve(lambda: nc.vector.tensor_tensor(out=gt[b], in0=y2[b], in1=szl[b], op=ALU.mult), [kb("y2"), kb("szl")], [kb("gt")])
        k.act(lambda: nc.scalar.activation(out=sq[b], in_=gt[b], func=AF.Square, accum_out=ss[b]), [kb("gt")], [kb("sq"), kb("ss")])
        k.act(lambda: nc.scalar.activation(out=rs[b], in_=ss[b], func=AF.Sqrt, scale=1.0 / 512, bias=LN_EPS), [kb("ss")], [kb("rs")])
        k.dve(lambda: nc.vector.reciprocal(out=rs[b], in_=rs[b]), [kb("rs")], [kb("rs")])
        k.dve(lambda: nc.vector.scalar_tensor_tensor(out=ho[b], in0=gt[b], scalar=rs[b], in1=normw, op0=ALU.mult, op1=ALU.mult),
              [kb("gt"), kb("rs"), "rep"], [kb("ho")])
        p.dma("act", hs[c * 128:(c + 1) * 128, :], ho[b], reads=[kb("ho")], writes=["hs"])
        k.dve(lambda: nc.vector.tensor_tensor(out=t2[b].rearrange("p (e q) -> p e q", e=8), in0=h_b.rearrange("p (e q) -> p e q", e=8),
                                              in1=bc_mid(dec_b[:, c, :], 64), op=ALU.mult), ["h_b", ("dec_b", c), kb("t2")], [kb("t2")])
        k.dve(lambda: nc.vector.tensor_tensor(out=h_b, in0=t2[b], in1=sbl[b], op=ALU.add), [kb("t2"), kb("sbl")], ["h_b"])
        k.act(lambda: nc.scalar.copy(out=hb_b, in_=h_b), ["h_b"], ["hb_b"])

    if nchunks == NCH:
        k.arena_reset()
        gl = k.asb("gl_full", [128, T])
        ca = k.asb("conv_acc", [128, T])
        for m in range(2):
            p.dma("sp", gl, glu_d[m], reads=["glu_d"], writes=["gl_full"])
            w15 = dw[:, m, 15:16]
            k.dve(lambda: nc.vector.tensor_scalar(out=ca, in0=gl, scalar1=w15, scalar2=db[:, m:m + 1], op0=ALU.mult, op1=ALU.add),
                  ["gl_full", "dw", "db"], ["conv_acc"])
            cl, gll = ca[:, CTX:], gl[:, CTX:]
            for s in range(31):
                o = s - 15
                if o == 0:
                    continue
                ws = dw[:, m, s:s + 1]
                lo, hi = max(0, -o), min(CTX, CTX - o)
                k.dve(lambda: nc.vector.scalar_tensor_tensor(out=ca[:, lo:hi], in0=gl[:, lo + o:hi + o], scalar=ws, in1=ca[:, lo:hi], op0=ALU.mult, op1=ALU.add),
                      ["gl_full", "dw", "conv_acc"], ["conv_acc"])
                if m == 0:
                    c3 = cl.rearrange("p (r c) -> p r c", c=64)
                    g3 = gll.rearrange("p (r c) -> p r c", c=64)
                    lo, hi = max(0, -o), min(64, 64 - o)
                    k.dve(lambda: nc.vector.scalar_tensor_tensor(out=c3[:, :, lo:hi], in0=g3[:, :, lo + o:hi + o], scalar=ws, in1=c3[:, :, lo:hi], op0=ALU.mult, op1=ALU.add),
                          ["gl_full", "dw", "conv_acc"], ["conv_acc"])
                else:
                    lo, hi = max(0, -o) * 64, min(128, 128 - o) * 64
                    k.dve(lambda: nc.vector.scalar_tensor_tensor(out=cl[:, lo:hi], in0=gll[:, lo + o * 64:hi + o * 64], scalar=ws, in1=cl[:, lo:hi], op0=ALU.mult, op1=ALU.add),
                          ["gl_full", "dw", "conv_acc"], ["conv_acc"])
            p.dma("sp", hpre[m], ca, reads=["conv_acc"], writes=["hpre"])
    p.finish()
    return nc


def pA_inputs(Xall, mod_i, inp, i):
    XT = Xall.T
    xcs = np.zeros((NCH, 128, 16, 132), np.float32)
    for c in range(NCH):
        seg_lo, seg_hi = (0, CTX) if c < 2 else (CTX, T)
        s = c * 128
        lo, hi = max(s - 2, seg_lo), min(s + 130, seg_hi)
        blk = XT[:, lo:hi].reshape(16, 128, hi - lo).transpose(1, 0, 2)
        xcs[c, :, :, lo - (s - 2):hi - (s - 2)] = blk
    m = mod_i
    modv = np.stack([m[0, 0:D], m[0, D:2 * D], m[1, 0:D], m[1, D:2 * D]], axis=-1)
    modv = np.ascontiguousarray(modv.reshape(16, 128, 4).transpose(1, 0, 2))
    maps = []
    for g in range(NCORES):
        cols = np.concatenate([
            O_XBC + g * 512 + np.arange(512),
            O_XBC + D_INNER + g * 128 + np.arange(128),
            O_XBC + D_INNER + D_BC + g * 128 + np.arange(128),
            O_GLU + g * 128 + np.arange(128),
            O_GLU + 1024 + g * 128 + np.arange(128),
            O_GLU + 2048 + g * 128 + np.arange(128),
            O_GLU + 2048 + 1024 + g * 128 + np.arange(128),
            O_Z + g * 512 + np.arange(512),
            O_DT + g * 8 + np.arange(8),
            O_DT + 64 + g * 8 + np.arange(8),
        ])
        wgm = inp["w_in"][i][:, cols]
        wgm = np.ascontiguousarray(wgm.reshape(16, 128, WG_COLS).transpose(1, 0, 2))
        xbc_cols = cols[:768] - O_XBC
        cwm = inp["ssm_conv_w"][i][:, xbc_cols]
        cwm = np.ascontiguousarray(cwm.reshape(5, 6, 128).transpose(2, 1, 0))
        cbm = np.ascontiguousarray(inp["ssm_conv_b"][i][xbc_cols].reshape(6, 128).T)
        cch = np.concatenate([g * 128 + np.arange(128), 1024 + g * 128 + np.arange(128)])
        dwm = np.ascontiguousarray(inp["conv_dw_w"][i][:, cch].reshape(31, 2, 128).transpose(2, 1, 0))
        dbm = np.ascontiguousarray(inp["conv_dw_b"][i][cch].reshape(2, 128).T)
        rep = np.concatenate([
            inp["ssm_dt_bias"][i][0, g * 8:(g + 1) * 8], inp["ssm_dt_bias"][i][1, g * 8:(g + 1) * 8],
            inp["ssm_a_log"][i][0, g * 8:(g + 1) * 8], inp["ssm_a_log"][i][1, g * 8:(g + 1) * 8],
            inp["ssm_d"][i][g * 8:(g + 1) * 8], inp["ssm_norm_w"][i][g * 512:(g + 1) * 512]]).astype(np.float32)
        rep = np.ascontiguousarray(np.broadcast_to(rep[None, :], (128, 552)))
        maps.append({"xc": xcs, "mod": modv, "wg": wgm, "cw": cwm, "cb": cbm, "dw": dwm, "db": dbm, "rep": rep})
    return maps


NB = 528
BLKS = [(0, 512), (512, 528)]


def ln_cm(k, X, xkey, nb0, nb1, post, pfx):
    nc = k.nc
    n = nb1 - nb0
    B = k.banks
    for kt in range(16):
        sq = k.lnsq[kt % 2][:, 0:n]
        k.act(lambda: nc.scalar.activation(out=sq, in_=X[:, kt, nb0:nb1], func=AF.Square), [xkey], [f"lnsq{kt % 2}"])
        k.mm(B[6][:, 0:n], k.ones, X[:, kt, nb0:nb1], kt == 0, kt == 15, ["c_ones", xkey], [("bank", 6)])
        k.mm(B[7][:, 0:n], k.ones, sq, kt == 0, kt == 15, ["c_ones", f"lnsq{kt % 2}"], [("bank", 7)])
    mt, m2, rs = k.lnm[0][:, 0:n], k.lnm[1][:, 0:n], k.lnm[2][:, 0:n]
    k.dve(lambda: nc.vector.tensor_scalar(out=mt, in0=B[6][:, 0:n], scalar1=1.0 / D, scalar2=None, op0=ALU.mult), [("bank", 6)], ["lnm0"])
    k.dve(lambda: nc.vector.tensor_tensor(out=m2, in0=mt, in1=mt, op=ALU.mult), ["lnm0"], ["lnm1"])
    k.dve(lambda: nc.vector.scalar_tensor_tensor(out=rs, in0=B[7][:, 0:n], scalar=1.0 / D, in1=m2, op0=ALU.mult, op1=ALU.subtract), [("bank", 7), "lnm1"], ["lnm2"])
    k.act(lambda: nc.scalar.activation(out=rs, in_=rs, func=AF.Sqrt, bias=LN_EPS), ["lnm2"], ["lnm2"])
    k.dve(lambda: nc.vector.reciprocal(out=rs, in_=rs), ["lnm2"], ["lnm2"])
    for kt in range(16):
        t = k.lnt[kt % 2][:, 0:n]
        tk = f"lnt{kt % 2}"
        k.dve(lambda: nc.vector.tensor_tensor(out=t, in0=X[:, kt, nb0:nb1], in1=mt, op=ALU.subtract), [xkey, "lnm0"], [tk])
        k.dve(lambda: nc.vector.tensor_tensor(out=t, in0=t, in1=rs, op=ALU.mult), [tk, "lnm2"], [tk])
        post(kt, t, tk)


def ln_scratch(k):
    k.lnsq = [k.sb(f"lnsq{i}", [128, 512]) for i in range(2)]
    k.lnt = [k.sb(f"lnt{i}", [128, 512]) for i in range(2)]
    k.lnm = [k.sb(f"lnm{i}", [128, 512]) for i in range(3)]


def build_pB():
    k = KB()
    nc, p = k.nc, k.p
    B = k.banks
    xTd = k.din("xT", [128, 16, NB])
    modd = k.din("mod", [128, 16, 12])
    hsTd = k.din("hsT", [128, 32, NB], BF16)
    hpTd = k.din("hpT", [128, 16, NB])
    lnpd = k.din("lnp", [128, 16, 4])
    wsod = k.din("wso", [16, 128, 32, 128])
    wcod = k.din("wco", [16, 128, 16, 128])
    wgd = k.din("wgate", [32, 128, 16, 128])
    wod = k.din("wo", [16, 128, 16, 128])
    wrd = k.din("wr", [128, 16, 16])
    x1o = k.dout("x1T", [128, 16, NB])
    u2o = k.dout("u2T", [128, 16, NB], BF16)
    affo = k.dout("aff", [NB, 16])
    k.consts()
    ln_scratch(k)
    mod = k.sb("modt", [128, 16, 12])
    lnp = k.sb("lnpt", [128, 16, 4])
    wr = k.sb("wrt", [128, 16, 16])
    scl = k.sb("scl", [128, 16, 4])
    p.dma("sp", mod, modd, writes=["mod"])
    p.dma("sp", lnp, lnpd, writes=["lnp"])
    p.dma("sp", wr, wrd, writes=["wr"])
    for j, col in enumerate([1, 7, 4, 10]):
        k.dve(lambda j=j, col=col: nc.vector.tensor_scalar(out=scl[:, :, j], in0=mod[:, :, col], scalar1=1.0, scalar2=None, op0=ALU.add), ["mod", "scl"], ["scl"])
    uT = k.sb("uT", [128, 16, NB], BF16)
    hcv = k.sb("hcv", [128, 16, NB], BF16)
    hsT = k.sb("hsT_s", [128, 32, NB], BF16)
    mg = k.sb("mg", [128, 16, NB], BF16)
    R = k.sb("R", [128, 16, NB])
    u2f = mg.rearrange("p a b -> p (a b)").bitcast(F32)[:, 0:2048].rearrange("p (a b) -> p a b", a=16)
    u2b = uT
    for kt in range(0, 32, 8):
        p.dma("act", hsT[:, kt:kt + 8, :], hsTd[:, kt:kt + 8, :], writes=["hsT"])
    SEG = [(0, 512, 0), (512, 528, 1)]
    p.dma("sp", R, xTd, writes=["R"])
    for kt in range(16):
        for (a, b_, kind) in SEG:
            k.act(lambda kt=kt, a=a, b_=b_, kind=kind: nc.scalar.activation(out=uT[:, kt, a:b_], in_=R[:, kt, a:b_], func=AF.Identity,
                                                                             bias=mod[:, kt, 6 * kind:6 * kind + 1], scale=scl[:, kt, kind:kind + 1]),
                  ["R", "mod", "scl"], ["uT"])
    p.barrier()
    p.dma("sp", R, hpTd, reads=[], writes=["R"])
    yt = [k.sb(f"yt{i}", [128, 512]) for i in range(2)]
    sgt = [k.sb(f"sgt{i}", [128, 512]) for i in range(2)]
    for (a, b_) in BLKS:
        n = b_ - a

        def post(kt, t, tk, a=a, b_=b_, n=n):
            y, s = yt[kt % 2][:, 0:n], sgt[kt % 2][:, 0:n]
            k.act(lambda: nc.scalar.activation(out=y, in_=t, func=AF.Identity, bias=lnp[:, kt, 1:2], scale=lnp[:, kt, 0:1]), [tk, "lnp"], [f"yt{kt % 2}"])
            k.act(lambda: nc.scalar.activation(out=s, in_=y, func=AF.Sigmoid), [f"yt{kt % 2}"], [f"sgt{kt % 2}"])
            k.dve(lambda: nc.vector.tensor_tensor(out=hcv[:, kt, a:b_], in0=y, in1=s, op=ALU.mult), [f"yt{kt % 2}", f"sgt{kt % 2}"], ["hcv"])
        ln_cm(k, R, "R", a, b_, post, "c")
    p.barrier()
    wso = [k.sb(f"wso{i}", [128, 32, 128], BF16) for i in range(2)]
    wco = [k.sb(f"wco{i}", [128, 16, 128], BF16) for i in range(2)]
    wga = [k.sb(f"wga{i}", [128, 16, 128], BF16) for i in range(2)]
    wgb = [k.sb(f"wgb{i}", [128, 16, 128], BF16) for i in range(2)]
    wo = [k.sb(f"wo{i}", [128, 16, 128], BF16) for i in range(2)]
    s1 = [k.sb(f"s1_{i}", [128, 512]) for i in range(2)]
    s2 = [k.sb(f"s2_{i}", [128, 512]) for i in range(2)]
    tm_ = [k.sb(f"tm_{i}", [128, 512]) for i in range(2)]
    for dt in range(16):
        w = dt % 2
        p.dma("pool", wso[w], wsod[dt], writes=[f"wso{w}"])
        p.dma("pool", wco[w], wcod[dt], writes=[f"wco{w}"])
        p.dma("pool", wga[w], wgd[dt], writes=[f"wga{w}"])
        p.dma("pool", wgb[w], wgd[16 + dt], writes=[f"wgb{w}"])
        for bi, (a, b_) in enumerate(BLKS):
            n = b_ - a
            for kt in range(32):
                k.mm(B[0][:, 0:n], wso[w][:, kt, :], hsT[:, kt, a:b_], kt == 0, kt == 31, [f"wso{w}", "hsT"], [("bank", 0)])
            for kt in range(16):
                k.mm(B[1][:, 0:n], wga[w][:, kt, :], uT[:, kt, a:b_], kt == 0, kt == 15, [f"wga{w}", "uT"], [("bank", 1)])
            for kt in range(16):
                k.mm(B[2][:, 0:n], wco[w][:, kt, :], hcv[:, kt, a:b_], kt == 0, kt == 15, [f"wco{w}", "hcv"], [("bank", 2)])
            for kt in range(16):
                k.mm(B[3][:, 0:n], wgb[w][:, kt, :], uT[:, kt, a:b_], kt == 0, kt == 15, [f"wgb{w}", "uT"], [("bank", 3)])
            q = bi % 2
            k.act(lambda: nc.scalar.activation(out=s1[q][:, 0:n], in_=B[1][:, 0:n], func=AF.Sigmoid), [("bank", 1)], [f"s1_{q}"])
            k.act(lambda: nc.scalar.activation(out=s2[q][:, 0:n], in_=B[3][:, 0:n], func=AF.Sigmoid), [("bank", 3)], [f"s2_{q}"])
            k.dve(lambda: nc.vector.tensor_tensor(out=tm_[q][:, 0:n], in0=B[0][:, 0:n], in1=s1[q][:, 0:n], op=ALU.mult), [("bank", 0), f"s1_{q}"], [f"tm_{q}"])
            k.dve(lambda: nc.vector.tensor_tensor(out=s2[q][:, 0:n], in0=B[2][:, 0:n], in1=s2[q][:, 0:n], op=ALU.mult), [("bank", 2), f"s2_{q}"], [f"s2_{q}"])
            k.dve(lambda: nc.vector.tensor_tensor(out=mg[:, dt, a:b_], in0=tm_[q][:, 0:n], in1=s2[q][:, 0:n], op=ALU.add), [f"tm_{q}", f"s2_{q}"], [("mg", dt)])
    p.barrier()
    p.dma("sp", R, xTd, reads=[], writes=["R"])
    MG = [("mg", d_) for d_ in range(16)]
    for dt in range(16):
        w = dt % 2
        p.dma("pool", wo[w], wod[dt], writes=[f"wo{w}"])
        for bi, (a, b_) in enumerate(BLKS):
            n = b_ - a
            bk = 4 + bi % 2
            for kt in range(16):
                k.mm(B[bk][:, 0:n], wo[w][:, kt, :], mg[:, kt, a:b_], kt == 0, kt == 15, [f"wo{w}"] + MG, [("bank", bk)])
            for (sa, sb_, kind) in SEG:
                lo, hi = max(a, sa), min(b_, sb_)
                if lo >= hi:
                    continue
                q = bi % 2
                k.dve(lambda lo=lo, hi=hi, kind=kind: nc.vector.tensor_scalar(out=tm_[q][:, 0:hi - lo], in0=B[bk][:, lo - a:hi - a], scalar1=mod[:, dt, 6 * kind + 2:6 * kind + 3],
                                                                            scalar2=None, op0=ALU.mult), [("bank", bk), "mod"], [f"tm_{q}"])
                k.dve(lambda lo=lo, hi=hi: nc.vector.scalar_tensor_tensor(out=R[:, dt, lo:hi], in0=R[:, dt, lo:hi], scalar=ALPHA, in1=tm_[q][:, 0:hi - lo], op0=ALU.mult, op1=ALU.add),
                      ["R", f"tm_{q}"], ["R"])
    p.barrier()
    for (a, b_) in BLKS:
        def post1(kt, t, tk, a=a, b_=b_):
            k.act(lambda: nc.scalar.activation(out=R[:, kt, a:b_], in_=t, func=AF.Identity, bias=lnp[:, kt, 3:4], scale=lnp[:, kt, 2:3]), [tk, "lnp", "R"], ["R"])
        ln_cm(k, R, "R", a, b_, post1, "l")
    p.barrier()
    p.dma("sp", x1o, R, reads=["R"])
    for kt in range(16):
        for (a, b_, kind) in SEG:
            k.act(lambda kt=kt, a=a, b_=b_, kind=kind: nc.scalar.activation(out=u2b[:, kt, a:b_], in_=R[:, kt, a:b_], func=AF.Identity,
                                                                             bias=mod[:, kt, 6 * kind + 3:6 * kind + 4], scale=scl[:, kt, 2 + kind:3 + kind]),
                  ["R", "mod", "scl"], ["u2b"])
    p.dma("act", u2o, u2b, reads=["u2b"])
    lg = [k.sb(f"lg{i}", [128, 16]) for i in range(2)]
    mx = [k.sb(f"mx{i}", [128, 1]) for i in range(2)]
    sm = [k.sb(f"sm{i}", [128, 1]) for i in range(2)]
    tiles = [(i * 128, min((i + 1) * 128, NB)) for i in range((NB + 127) // 128)]
    for ti, (a, b_) in enumerate(tiles):
        n = b_ - a
        q = ti % 2
        kind = 0 if a < 512 else 1
        for kt in range(16):
            k.act(lambda kt=kt: nc.scalar.activation(out=u2f[:, kt, 0:n], in_=R[:, kt, a:b_], func=AF.Identity,
                                                     bias=mod[:, kt, 6 * kind + 3:6 * kind + 4], scale=scl[:, kt, 2 + kind:3 + kind]), ["R", "mod", "scl"], ["u2f"])
        bk = 4 + q
        for kt in range(16):
            k.mm(B[bk][0:n, 0:16], u2f[:, kt, 0:n], wr[:, kt, :], kt == 0, kt == 15, ["u2f", "wr"], [("bank", bk)])
        k.dve(lambda: nc.vector.tensor_reduce(out=mx[q][0:n], in_=B[bk][0:n, 0:16], axis=AX.X, op=ALU.max), [("bank", bk)], [f"mx{q}"])
        k.dve(lambda: nc.vector.tensor_scalar(out=mx[q][0:n], in0=mx[q][0:n], scalar1=-1.0, scalar2=None, op0=ALU.mult), [f"mx{q}"], [f"mx{q}"])
        k.act(lambda: nc.scalar.activation(out=lg[q][0:n], in_=B[bk][0:n, 0:16], func=AF.Exp, bias=mx[q][0:n], accum_out=sm[q][0:n]), [("bank", bk), f"mx{q}"], [f"lg{q}", f"sm{q}"])
        k.dve(lambda: nc.vector.reciprocal(out=sm[q][0:n], in_=sm[q][0:n]), [f"sm{q}"], [f"sm{q}"])
        k.dve(lambda: nc.vector.tensor_scalar(out=lg[q][0:n], in0=lg[q][0:n], scalar1=sm[q][0:n], scalar2=None, op0=ALU.mult), [f"lg{q}", f"sm{q}"], [f"lg{q}"])
        p.dma("sp", affo[a:b_, :], lg[q][0:n], reads=[f"lg{q}"])
    p.finish()
    return nc


def cm_tiles(w, nk):
    K, N = w.shape
    return np.ascontiguousarray(w.reshape(nk, 128, N // 128, 128).transpose(2, 1, 0, 3))


def cmvec(v):
    return v.reshape(16, 128).T


def to_cm(Xtok):
    n, C = Xtok.shape
    return np.ascontiguousarray(Xtok.T.reshape(C // 128, 128, n).transpose(1, 0, 2))


def from_cm(Xcm):
    p_, kt, n = Xcm.shape
    return np.ascontiguousarray(Xcm.transpose(2, 1, 0).reshape(n, kt * 128))


CBLK = [(0, CTX)] + [(CTX + i * 512, CTX + (i + 1) * 512) for i in range(SEQ // 512)]


def build_pC(nblk=len(CBLK)):
    k = KB()
    nc, p = k.nc, k.p
    B = k.banks
    u2d = k.din("u2T", [128, 16, T], BF16)
    affd = k.din("affT", [2, T])
    wgd = k.din("wg", [2, 24, 128, 16, 128])
    wud = k.din("wu", [2, 24, 128, 16, 128])
    wdd = k.din("wd", [2, 16, 128, 24, 128])
    outd = k.dout("part", [128, 16, T])
    k.consts()
    aff = k.sb("aff", [2, T])
    wts = k.sb("wts", [2, T])
    p.dma("sp", aff, affd, writes=["aff"])
    thr = k.sb("thr", [2, 2])
    for si, (a, b_, kk) in enumerate([(0, CTX, 2 * CTX // NEXP), (CTX, T, 2 * SEQ // NEXP)]):
        lo = k.sb(f"lo{si}", [2, 1])
        hi = k.sb(f"hi{si}", [2, 1])
        mid = k.sb(f"mid{si}", [2, 1])
        cnt = k.sb(f"cnt{si}", [2, 1])
        ge = k.sb(f"ge{si}", [2, 1])
        dl = k.sb(f"dl{si}", [2, 1])
        K_ = [f"bis{si}"]
        k.dve(lambda: nc.vector.memset(lo, 0.0), [], K_)
        k.dve(lambda: nc.vector.memset(hi, 1.0), K_, K_)
        for it in range(36):
            k.dve(lambda: nc.vector.tensor_tensor(out=mid, in0=lo, in1=hi, op=ALU.add), K_, K_)
            k.dve(lambda: nc.vector.tensor_scalar(out=mid, in0=mid, scalar1=0.5, scalar2=None, op0=ALU.mult), K_, K_)
            k.dve(lambda: nc.vector.tensor_scalar(out=wts[:, a:b_], in0=aff[:, a:b_], scalar1=mid, scalar2=0.0, op0=ALU.is_ge, op1=ALU.add, accum_out=cnt),
                  K_ + ["aff"], K_ + ["wts"])
            k.dve(lambda: nc.vector.tensor_scalar(out=ge, in0=cnt, scalar1=float(kk) - 0.5, scalar2=None, op0=ALU.is_ge), K_, K_)
            k.dve(lambda: nc.vector.tensor_tensor(out=dl, in0=mid, in1=lo, op=ALU.subtract), K_, K_)
            k.dve(lambda: nc.vector.tensor_tensor(out=dl, in0=dl, in1=ge, op=ALU.mult), K_, K_)
            k.dve(lambda: nc.vector.tensor_tensor(out=lo, in0=lo, in1=dl, op=ALU.add), K_, K_)
            k.dve(lambda: nc.vector.tensor_tensor(out=dl, in0=hi, in1=mid, op=ALU.subtract), K_, K_)
            k.dve(lambda: nc.vector.tensor_tensor(out=dl, in0=dl, in1=ge, op=ALU.mult), K_, K_)
            k.dve(lambda: nc.vector.tensor_tensor(out=hi, in0=mid, in1=dl, op=ALU.add), K_, K_)
        k.dve(lambda: nc.vector.tensor_scalar(out=wts[:, a:b_], in0=aff[:, a:b_], scalar1=lo, scalar2=None, op0=ALU.is_ge), K_ + ["aff", "wts"], ["wts"])
        k.dve(lambda: nc.vector.tensor_tensor(out=wts[:, a:b_], in0=wts[:, a:b_], in1=aff[:, a:b_], op=ALU.mult), ["wts", "aff"], ["wts"])
    sel = k.sb("sel", [2, 2, 128])
    k.dve(lambda: nc.vector.tensor_copy(out=sel, in_=k.ident[0:2, 0:2].unsqueeze(2).broadcast_to([2, 2, 128])), ["c_ident"], ["sel"])
    ub = [k.sb(f"ub{i}", [128, 16, 512], BF16) for i in range(2)]
    wgt = [k.sb(f"wgt{i}", [128, 16, 128], BF16) for i in range(2)]
    wut = [k.sb(f"wut{i}", [128, 16, 128], BF16) for i in range(2)]
    wdt = [k.sb(f"wdt{i}", [128, 24, 128], BF16) for i in range(2)]
    hT = k.sb("hT", [128, 24, 512], BF16)
    wbc = k.sb("wbc", [128, 512])
    sg = [k.sb(f"sg{i}", [128, 512]) for i in range(2)]
    t1 = [k.sb(f"t1_{i}", [128, 512]) for i in range(2)]
    acc = k.sb("acc", [128, 16, 512])
    for bi, (a, b_) in enumerate(CBLK[:nblk]):
        n = b_ - a
        u = ub[bi % 2]
        p.dma("sp", u[:, :, 0:n], u2d[:, :, a:b_], writes=[f"ub{bi % 2}"])
        for le in range(2):
            k.mm(B[7][:, 0:n], sel[:, le, :], wts[:, a:b_], True, True, ["sel", "wts"], [("bank", 7)])
            k.act(lambda: nc.scalar.copy(out=wbc[:, 0:n], in_=B[7][:, 0:n]), [("bank", 7)], ["wbc"])
            for ft in range(24):
                w = ft % 2
                p.dma("pool", wgt[w], wgd[le, ft], writes=[f"wgt{w}"])
                p.dma("pool", wut[w], wud[le, ft], writes=[f"wut{w}"])
                bg, bu = 2 * w, 2 * w + 1
                for kt in range(16):
                    k.mm(B[bg][:, 0:n], wgt[w][:, kt, :], u[:, kt, 0:n], kt == 0, kt == 15, [f"wgt{w}", f"ub{bi % 2}"], [("bank", bg)])
                for kt in range(16):
                    k.mm(B[bu][:, 0:n], wut[w][:, kt, :], u[:, kt, 0:n], kt == 0, kt == 15, [f"wut{w}", f"ub{bi % 2}"], [("bank", bu)])
                k.act(lambda: nc.scalar.activation(out=sg[w][:, 0:n], in_=B[bg][:, 0:n], func=AF.Sigmoid), [("bank", bg)], [f"sg{w}"])
                k.dve(lambda: nc.vector.tensor_tensor(out=sg[w][:, 0:n], in0=B[bg][:, 0:n], in1=sg[w][:, 0:n], op=ALU.mult), [("bank", bg), f"sg{w}"], [f"sg{w}"])
                k.dve(lambda: nc.vector.tensor_tensor(out=t1[w][:, 0:n], in0=B[bu][:, 0:n], in1=sg[w][:, 0:n], op=ALU.mult), [("bank", bu), f"sg{w}"], [f"t1_{w}"])
                k.pool(lambda: nc.gpsimd.tensor_tensor(out=hT[:, ft, 0:n], in0=t1[w][:, 0:n], in1=wbc[:, 0:n], op=ALU.mult), [f"t1_{w}", "wbc"], [("hT", ft)])
            HT = [("hT", f_) for f_ in range(24)]
            for dt in range(16):
                w = dt % 2
                p.dma("pool", wdt[w], wdd[le, dt], writes=[f"wdt{w}"])
                bk = 4 + w
                for ft in range(24):
                    k.mm(B[bk][:, 0:n], wdt[w][:, ft, :], hT[:, ft, 0:n], ft == 0, ft == 23, [f"wdt{w}"] + HT, [("bank", bk)])
                if le == 0:
                    k.act(lambda: nc.scalar.copy(out=acc[:, dt, 0:n], in_=B[bk][:, 0:n]), [("bank", bk)], [("acc", dt)])
                else:
                    k.dve(lambda: nc.vector.tensor_tensor(out=acc[:, dt, 0:n], in0=acc[:, dt, 0:n], in1=B[bk][:, 0:n], op=ALU.add), [("bank", bk), ("acc", dt)], [("acc", dt)])
        p.dma("act", outd[:, :, a:b_], acc[:, :, 0:n], reads=[("acc", d_) for d_ in range(16)])
    p.finish()
    return nc


NS = 1056
SBLK = [(0, 512), (512, 1024), (1024, NS)]


def build_pC1():
    k = KB()
    nc, p = k.nc, k.p
    affd = k.din("affT", [2, T])
    wtso = k.dout("wts", [2, T])
    sloto = k.dout("slot", [2, T])
    aff = k.sb("aff", [2, T])
    wts = k.sb("wts_s", [2, T])
    msk = k.sb("msk", [2, T])
    pos = k.sb("pos", [2, T])
    p.dma("sp", aff, affd, writes=["aff"])
    for si, (a, b_, kk) in enumerate([(0, CTX, 2 * CTX // NEXP), (CTX, T, 2 * SEQ // NEXP)]):
        lo = k.sb(f"lo{si}", [2, 1])
        hi = k.sb(f"hi{si}", [2, 1])
        mid = k.sb(f"mid{si}", [2, 1])
        cnt = k.sb(f"cnt{si}", [2, 1])
        ge = k.sb(f"ge{si}", [2, 1])
        dl = k.sb(f"dl{si}", [2, 1])
        K_ = [f"bis{si}"]
        k.dve(lambda: nc.vector.memset(lo, 0.0), [], K_)
        k.dve(lambda: nc.vector.memset(hi, 1.0), K_, K_)
        for it in range(36):
            k.dve(lambda: nc.vector.tensor_tensor(out=mid, in0=lo, in1=hi, op=ALU.add), K_, K_)
            k.dve(lambda: nc.vector.tensor_scalar(out=mid, in0=mid, scalar1=0.5, scalar2=None, op0=ALU.mult), K_, K_)
            k.dve(lambda: nc.vector.tensor_scalar(out=msk[:, a:b_], in0=aff[:, a:b_], scalar1=mid, scalar2=0.0, op0=ALU.is_ge, op1=ALU.add, accum_out=cnt),
                  K_ + ["aff"], K_ + ["msk"])
            k.dve(lambda: nc.vector.tensor_scalar(out=ge, in0=cnt, scalar1=float(kk) - 0.5, scalar2=None, op0=ALU.is_ge), K_, K_)
            k.dve(lambda: nc.vector.tensor_tensor(out=dl, in0=mid, in1=lo, op=ALU.subtract), K_, K_)
            k.dve(lambda: nc.vector.tensor_tensor(out=dl, in0=dl, in1=ge, op=ALU.mult), K_, K_)
            k.dve(lambda: nc.vector.tensor_tensor(out=lo, in0=lo, in1=dl, op=ALU.add), K_, K_)
            k.dve(lambda: nc.vector.tensor_tensor(out=dl, in0=hi, in1=mid, op=ALU.subtract), K_, K_)
            k.dve(lambda: nc.vector.tensor_tensor(out=dl, in0=dl, in1=ge, op=ALU.mult), K_, K_)
            k.dve(lambda: nc.vector.tensor_tensor(out=hi, in0=mid, in1=dl, op=ALU.add), K_, K_)
        k.dve(lambda: nc.vector.tensor_scalar(out=msk[:, a:b_], in0=aff[:, a:b_], scalar1=lo, scalar2=None, op0=ALU.is_ge), K_ + ["aff", "msk"], ["msk"])
        k.dve(lambda: nc.vector.tensor_tensor(out=wts[:, a:b_], in0=msk[:, a:b_], in1=aff[:, a:b_], op=ALU.mult), ["msk", "aff"], ["wts"])
        k.dve(lambda: nc.vector.tensor_tensor_scan(out=pos[:, a:b_], data0=msk[:, a:b_], data1=msk[:, a:b_], initial=0.0, op0=ALU.add, op1=ALU.max), ["msk"], ["pos"])
        k.dve(lambda: nc.vector.tensor_tensor(out=pos[:, a:b_], in0=pos[:, a:b_], in1=msk[:, a:b_], op=ALU.mult), ["pos", "msk"], ["pos"])
        k.dve(lambda: nc.vector.tensor_scalar(out=pos[:, a:b_], in0=pos[:, a:b_], scalar1=-1.0, scalar2=None, op0=ALU.add), ["pos"], ["pos"])
    p.dma("sp", wtso, wts, reads=["wts"])
    p.dma("act", sloto, pos, reads=["pos"])
    p.finish()
    return nc


def build_pC2():
    k = KB()
    nc, p = k.nc, k.p
    B = k.banks
    xed = k.din("xeT", [2, 128, 16, NS], BF16)
    gwd = k.din("gw", [2, NS])
    wgd = k.din("wg", [2, 24, 128, 16, 128])
    wud = k.din("wu", [2, 24, 128, 16, 128])
    wdd = k.din("wd", [2, 16, 128, 24, 128])
    outd = k.dout("yeT", [2, 128, 16, NS])
    k.consts()
    gw = k.sb("gw_s", [2, NS])
    p.dma("sp", gw, gwd, writes=["gw"])
    sel = k.sb("sel", [2, 2, 128])
    k.dve(lambda: nc.vector.tensor_copy(out=sel, in_=k.ident[0:2, 0:2].unsqueeze(2).broadcast_to([2, 2, 128])), ["c_ident"], ["sel"])
    ub = [k.sb(f"ub{i}", [128, 16, 512], BF16) for i in range(2)]
    wgt = [k.sb(f"wgt{i}", [128, 16, 128], BF16) for i in range(2)]
    wut = [k.sb(f"wut{i}", [128, 16, 128], BF16) for i in range(2)]
    wdt = [k.sb(f"wdt{i}", [128, 24, 128], BF16) for i in range(2)]
    hT = k.sb("hT", [128, 24, 512], BF16)
    wbc = k.sb("wbc", [128, 512])
    sg = [k.sb(f"sg{i}", [128, 512]) for i in range(2)]
    t1 = [k.sb(f"t1_{i}", [128, 512]) for i in range(2)]
    acc = [k.sb(f"acc{i}", [128, 16, 512]) for i in range(2)]
    it = 0
    for le in range(2):
        for bi, (a, b_) in enumerate(SBLK):
            n = b_ - a
            q = it % 2
            it += 1
            u = ub[q]
            p.dma("sp", u[:, :, 0:n], xed[le, :, :, a:b_], writes=[f"ub{q}"])
            k.mm(B[7][:, 0:n], sel[:, le, :], gw[:, a:b_], True, True, ["sel", "gw"], [("bank", 7)])
            k.act(lambda: nc.scalar.copy(out=wbc[:, 0:n], in_=B[7][:, 0:n]), [("bank", 7)], ["wbc"])
            for ft in range(24):
                w = ft % 2
                p.dma("pool", wgt[w], wgd[le, ft], writes=[f"wgt{w}"])
                p.dma("pool", wut[w], wud[le, ft], writes=[f"wut{w}"])
                bg, bu = 2 * w, 2 * w + 1
                for kt in range(16):
                    k.mm(B[bg][:, 0:n], wgt[w][:, kt, :], u[:, kt, 0:n], kt == 0, kt == 15, [f"wgt{w}", f"ub{q}"], [("bank", bg)])
                for kt in range(16):
                    k.mm(B[bu][:, 0:n], wut[w][:, kt, :], u[:, kt, 0:n], kt == 0, kt == 15, [f"wut{w}", f"ub{q}"], [("bank", bu)])
                k.act(lambda: nc.scalar.activation(out=sg[w][:, 0:n], in_=B[bg][:, 0:n], func=AF.Sigmoid), [("bank", bg)], [f"sg{w}"])
                k.dve(lambda: nc.vector.tensor_tensor(out=sg[w][:, 0:n], in0=B[bg][:, 0:n], in1=sg[w][:, 0:n], op=ALU.mult), [("bank", bg), f"sg{w}"], [f"sg{w}"])
                k.dve(lambda: nc.vector.tensor_tensor(out=t1[w][:, 0:n], in0=B[bu][:, 0:n], in1=sg[w][:, 0:n], op=ALU.mult), [("bank", bu), f"sg{w}"], [f"t1_{w}"])
                k.pool(lambda: nc.gpsimd.tensor_tensor(out=hT[:, ft, 0:n], in0=t1[w][:, 0:n], in1=wbc[:, 0:n], op=ALU.mult), [f"t1_{w}", "wbc"], [("hT", ft)])
            HT = [("hT", f_) for f_ in range(24)]
            for dt in range(16):
                w = dt % 2
                p.dma("pool", wdt[w], wdd[le, dt], writes=[f"wdt{w}"])
                bk = 4 + w
                for ft in range(24):
                    k.mm(B[bk][:, 0:n], wdt[w][:, ft, :], hT[:, ft, 0:n], ft == 0, ft == 23, [f"wdt{w}"] + HT, [("bank", bk)])
                k.act(lambda: nc.scalar.copy(out=acc[q][:, dt, 0:n], in_=B[bk][:, 0:n]), [("bank", bk)], [(f"acc{q}", dt)])
            p.dma("act", outd[le, :, :, a:b_], acc[q][:, :, 0:n], reads=[(f"acc{q}", d_) for d_ in range(16)])
    p.finish()
    return nc


def build_pD():
    k = KB()
    nc, p = k.nc, k.p
    partd = k.din("parts", [NEXP, 128, 16, NB])
    x1d = k.din("x1T", [128, 16, NB])
    modd = k.din("mod", [128, 16, 12])
    lnpd = k.din("lnp", [128, 16, 2])
    outd = k.dout("x2T", [128, 16, NB])
    k.consts()
    ln_scratch(k)
    mod = k.sb("modt", [128, 16, 12])
    lnp = k.sb("lnpt", [128, 16, 2])
    R = k.sb("R", [128, 16, NB])
    S = k.sb("S", [128, 16, NB])
    pt = [k.sb(f"pt{i}", [128, 16, NB]) for i in range(2)]
    p.dma("sp", mod, modd, writes=["mod"])
    p.dma("sp", lnp, lnpd, writes=["lnp"])
    p.dma("sp", R, x1d, writes=["R"])
    p.dma("act", S, partd[0], writes=["S"])
    for j in range(1, NEXP):
        q = j % 2
        p.dma("sp" if q else "act", pt[q], partd[j], writes=[f"pt{q}"])
        k.dve(lambda: nc.vector.tensor_tensor(out=S, in0=S, in1=pt[q], op=ALU.add), ["S", f"pt{q}"], ["S"])
    SEG = [(0, 512, 0), (512, 528, 1)]
    for kt in range(16):
        for (a, b_, kind) in SEG:
            k.dve(lambda kt=kt, a=a, b_=b_, kind=kind: nc.vector.tensor_scalar(out=S[:, kt, a:b_], in0=S[:, kt, a:b_], scalar1=mod[:, kt, 6 * kind + 5:6 * kind + 6],
                                                                             scalar2=None, op0=ALU.mult), ["S", "mod"], ["S"])
        k.dve(lambda kt=kt: nc.vector.scalar_tensor_tensor(out=R[:, kt, :], in0=R[:, kt, :], scalar=ALPHA, in1=S[:, kt, :], op0=ALU.mult, op1=ALU.add), ["R", "S"], ["R"])
    p.barrier()
    for (a, b_) in BLKS:
        def post(kt, t, tk, a=a, b_=b_):
            k.act(lambda: nc.scalar.activation(out=S[:, kt, a:b_], in_=t, func=AF.Identity, bias=lnp[:, kt, 1:2], scale=lnp[:, kt, 0:1]), [tk, "lnp", "S"], ["S"])
        ln_cm(k, R, "R", a, b_, post, "d")
    p.dma("sp", outd, S, reads=["S"])
    p.finish()
    return nc


_PROGS = {}


def _prog(name, fn):
    if name not in _PROGS:
        _PROGS[name] = fn()
    return _PROGS[name]


def _run(nc, maps):
    return run_bass_kernel_spmd(nc, maps, core_ids=list(range(NCORES))).results


def _tok(h, g):
    j = h * NCORES + g
    return np.concatenate([CTX + j * 512 + np.arange(512), j * 16 + np.arange(16)])


def kernel(**inp):
    inp = {k_: np.asarray(v) for k_, v in inp.items()}
    mod = run_p0(inp)
    Xall = np.concatenate([inp["ctx"][0], inp["x"][0]], axis=0).astype(np.float32)
    for i in range(DEPTH):
        resA = _run(_prog("A", build_pA), pA_inputs(Xall, mod[i], inp, i))
        hs_all = np.zeros((T, D_INNER), NPBF)
        hpre_all = np.zeros((T, D), np.float32)
        for g in range(NCORES):
            hs_all[:, g * 512:(g + 1) * 512] = resA[g]["hs"]
            hpre_all[:, g * 128:(g + 1) * 128] = resA[g]["hpre"][0].T
            hpre_all[:, 1024 + g * 128:1024 + (g + 1) * 128] = resA[g]["hpre"][1].T
        del resA
        m = mod[i]
        mod12 = np.stack([m[kind, j * D:(j + 1) * D] for kind in range(2) for j in range(6)], axis=-1)
        mod12 = np.ascontiguousarray(mod12.reshape(16, 128, 12).transpose(1, 0, 2))
        lnpB = np.stack([inp["conv_ln_g"][i], inp["conv_ln_b"][i], inp["ln1_g"][i], inp["ln1_b"][i]], axis=-1)
        lnpB = np.ascontiguousarray(lnpB.reshape(16, 128, 4).transpose(1, 0, 2))
        wso = cm_tiles(inp["w_ssm_out"][i], 32)
        wco = cm_tiles(inp["w_conv_out"][i], 16)
        wgate = cm_tiles(np.ascontiguousarray(inp["w_in"][i][:, O_GATE:]), 16)
        wo = cm_tiles(inp["w_o"][i], 16)
        wr = np.ascontiguousarray(inp["w_router"][i].reshape(16, 128, 16).transpose(1, 0, 2))
        X1all = np.zeros((T, D), np.float32)
        U2all = np.zeros((T, D), NPBF)
        AFFall = np.zeros((T, NEXP), np.float32)
        for h in range(2):
            maps = []
            for g in range(NCORES):
                tok = _tok(h, g)
                maps.append({"xT": to_cm(Xall[tok]), "mod": mod12, "hsT": to_cm(hs_all[tok]), "hpT": to_cm(hpre_all[tok]), "lnp": lnpB,
                             "wso": wso, "wco": wco, "wgate": wgate, "wo": wo, "wr": wr})
            resB = _run(_prog("B", build_pB), maps)
            for g in range(NCORES):
                tok = _tok(h, g)
                X1all[tok] = from_cm(resB[g]["x1T"])
                U2all[tok] = from_cm(resB[g]["u2T"])
                AFFall[tok] = resB[g]["aff"]
            del resB, maps
        del wso, wco, wgate, wo
        maps = [{"affT": np.ascontiguousarray(AFFall[:, [2 * j, 2 * j + 1]].T)} for j in range(NCORES)]
        resC1 = _run(_prog("C1", build_pC1), maps)
        idx = np.zeros((NEXP, NS), np.int64)
        gwv = np.zeros((NEXP, NS), np.float32)
        for j in range(NCORES):
            for le in range(2):
                e = 2 * j + le
                sl = np.rint(resC1[j]["slot"][le]).astype(np.int64)
                for (a, b_, cap, off) in [(0, CTX, 2 * CTX // NEXP, 1024), (CTX, T, 2 * SEQ // NEXP, 0)]:
                    s_ = sl[a:b_]
                    ok = (s_ >= 0) & (s_ < cap)
                    idx[e, off + s_[ok]] = a + np.flatnonzero(ok)
                gwv[e] = resC1[j]["wts"][le][idx[e]]
        del resC1
        maps = []
        for j in range(NCORES):
            es = [2 * j, 2 * j + 1]
            maps.append({"xeT": np.stack([to_cm(U2all[idx[e]]) for e in es]), "gw": np.ascontiguousarray(gwv[es]),
                         "wg": np.stack([cm_tiles(inp["w_exp_gate"][i][e], 16) for e in es]),
                         "wu": np.stack([cm_tiles(inp["w_exp_up"][i][e], 16) for e in es]),
                         "wd": np.stack([cm_tiles(inp["w_exp_down"][i][e], 24) for e in es])})
        resC = _run(_prog("C2", build_pC2), maps)
        PART = []
        for e in range(NEXP):
            pe_ = np.zeros((T, D), np.float32)
            pe_[idx[e]] = from_cm(resC[e // 2]["yeT"][e % 2])
            PART.append(pe_)
        del resC, maps
        lnpD = np.stack([inp["ln2_g"][i], inp["ln2_b"][i]], axis=-1)
        lnpD = np.ascontiguousarray(lnpD.reshape(16, 128, 2).transpose(1, 0, 2))
        X2all = np.zeros((T, D), np.float32)
        for h in range(2):
            maps = []
            for g in range(NCORES):
                tok = _tok(h, g)
                maps.append({"parts": np.stack([to_cm(PART[e][tok]) for e in range(NEXP)]),
                             "x1T": to_cm(X1all[tok]), "mod": mod12, "lnp": lnpD})
            resD = _run(_prog("D", build_pD), maps)
            for g in range(NCORES):
                X2all[_tok(h, g)] = from_cm(resD[g]["x2T"])
            del resD, maps
        Xall = X2all
    return np.ascontiguousarray(Xall[CTX:].reshape(1, SEQ, D)).astype(np.float32)
```

```python
import numpy as np
import ml_dtypes
import concourse.bass as bass
import concourse.mybir as mybir
from concourse.bass_utils import run_bass_kernel_spmd

F32 = mybir.dt.float32
BF16 = mybir.dt.bfloat16
I32 = mybir.dt.int32
U32 = mybir.dt.uint32
AF = mybir.ActivationFunctionType
ALU = mybir.AluOpType
AX = mybir.AxisListType
NPBF = ml_dtypes.bfloat16

NCORES = 8


class Prog:
    ENG = ("pe", "dve", "act", "pool", "sp")

    def __init__(self, nc, n_dma_sems=6, same_engine_sync=True):
        self.nc = nc
        self.e = {"pe": nc.tensor, "dve": nc.vector, "act": nc.scalar, "pool": nc.gpsimd, "sp": nc.sync}
        self.sem = {k: nc.alloc_semaphore("c_" + k) for k in self.ENG}
        self.cnt = {k: 0 for k in self.ENG}
        self.same = same_engine_sync
        self.dsem = {}
        self.dcnt = {}
        self.drr = {}
        for q in ("sp", "act", "pool"):
            self.dsem[q] = [nc.alloc_semaphore(f"d_{q}{i}") for i in range(n_dma_sems)]
            self.dcnt[q] = [0] * n_dma_sems
            self.drr[q] = 0
        self.seen = {k: {} for k in self.ENG}
        self.buf = {}
        self.semobj = {}
        for k in self.ENG:
            self.semobj[("c", k)] = self.sem[k]
        for q in self.dsem:
            for i, s in enumerate(self.dsem[q]):
                self.semobj[("d", q, i)] = s
        self.ninst = 0

    def _deps(self, reads, writes):
        deps = []
        for k in reads:
            st = self.buf.get(k)
            if st and st["w"]:
                deps.append(st["w"])
        for k in writes:
            st = self.buf.get(k)
            if st:
                if st["w"]:
                    deps.append(st["w"])
                deps.extend(st["r"])
        return deps

    def _wait(self, eng, deps):
        best = {}
        for sk, v in deps:
            if sk == ("c", eng) and (eng == "pe" or not self.same):
                continue
            if v > best.get(sk, 0):
                best[sk] = v
        for sk, v in best.items():
            if self.seen[eng].get(sk, 0) >= v:
                continue
            self.e[eng].wait_ge(self.semobj[sk], v)
            self.seen[eng][sk] = v

    def _mark(self, reads, writes, tag):
        for k in writes:
            self.buf[k] = {"w": tag, "r": []}
        for k in reads:
            if k in writes:
                continue
            st = self.buf.setdefault(k, {"w": None, "r": []})
            st["r"] = [t for t in st["r"] if t[0] != tag[0]] + [tag]

    def op(self, eng, fn, reads=(), writes=()):
        self._wait(eng, self._deps(reads, writes))
        ins = fn()
        self.cnt[eng] += 1
        ins.then_inc(self.sem[eng], 1)
        self._mark(reads, writes, (("c", eng), self.cnt[eng]))
        self.ninst += 1
        return ins

    def dma(self, q, out, in_, reads=(), writes=(), **kw):
        i = self.drr[q]
        self.drr[q] = (i + 1) % len(self.dsem[q])
        sk = ("d", q, i)
        deps = self._deps(reads, writes)
        if self.dcnt[q][i] > 0:
            deps.append((sk, self.dcnt[q][i]))
        self._wait(q, deps)
        ins = self.e[q].dma_start(out=out, in_=in_, **kw)
        self.dcnt[q][i] += 16
        ins.then_inc(self.semobj[sk], 16)
        self._mark(reads, writes, (sk, self.dcnt[q][i]))
        self.ninst += 1
        return ins

    def dma_custom(self, q, fn, reads=(), writes=()):
        i = self.drr[q]
        self.drr[q] = (i + 1) % len(self.dsem[q])
        sk = ("d", q, i)
        deps = self._deps(reads, writes)
        if self.dcnt[q][i] > 0:
            deps.append((sk, self.dcnt[q][i]))
        self._wait(q, deps)
        ins = fn()
        self.dcnt[q][i] += 16
        ins.then_inc(self.semobj[sk], 16)
        self._mark(reads, writes, (sk, self.dcnt[q][i]))
        self.ninst += 1
        return ins

    def barrier(self):
        deps = []
        for k in self.ENG:
            if self.cnt[k]:
                deps.append((("c", k), self.cnt[k]))
        for q in self.dsem:
            for i, v in enumerate(self.dcnt[q]):
                if v:
                    deps.append((("d", q, i), v))
        old = self.same
        self.same = False
        for e in self.ENG:
            self._wait(e, deps)
        self.same = old

    def finish(self):
        deps = []
        for k in self.ENG:
            if self.cnt[k] and k != "sp":
                deps.append((("c", k), self.cnt[k]))
        for q in self.dsem:
            for i, v in enumerate(self.dcnt[q]):
                if v:
                    deps.append((("d", q, i), v))
        self.same = True
        self._wait("sp", deps)
        self.e["sp"].nop() if hasattr(self.e["sp"], "nop") else None


D = 2048
SEQ = 8192
CTX = 256
T = SEQ + CTX
NCH = T // 128
DEPTH = 2
D_INNER = 4096
D_BC = 1024
D_XBC = 6144
O_Z = 0
O_XBC = 4096
O_DT = O_XBC + D_XBC
O_GLU = O_DT + 128
O_GATE = O_GLU + 4096
D_PROJ = O_GATE + 4096
NEXP = 16
DEXP = 3072
ALPHA = (2 * DEPTH) ** 0.25
LN_EPS = 1e-5
WG_COLS = 1808


SAME_SYNC = True


class KB:
    def __init__(self):
        self.nc = bass.Bass("TRN2", target_bir_lowering=False)
        self.p = Prog(self.nc, same_engine_sync=SAME_SYNC)
        self.banks = [self.nc.alloc_psum_tensor(f"bank{i}", [128, 512], F32).ap() for i in range(8)]

    def din(self, name, shape, dt=F32):
        return self.nc.dram_tensor(name, list(shape), dt, kind="ExternalInput").ap()

    def dout(self, name, shape, dt=F32):
        return self.nc.dram_tensor(name, list(shape), dt, kind="ExternalOutput").ap()

    def dscr(self, name, shape, dt=F32):
        return self.nc.dram_tensor(name, list(shape), dt, kind="Internal").ap()

    def sb(self, name, shape, dt=F32):
        return self.nc.alloc_sbuf_tensor("s_" + name, list(shape), dt).ap()

    def arena_init(self, words):
        self.arena = self.nc.alloc_sbuf_tensor("s_arena", [128, words], F32).ap()
        self.arena_words = words
        self.arena_off = 0

    def arena_reset(self):
        self.p.barrier()
        self.arena_off = 0

    def asb(self, name, shape, dt=F32):
        n = int(np.prod(shape[1:]))
        words = n if dt == F32 or dt == I32 or dt == U32 else (n + 1) // 2
        assert self.arena_off + words <= self.arena_words, (name, self.arena_off, words)
        ap = self.arena[:, self.arena_off:self.arena_off + words]
        self.arena_off += words
        if dt != F32:
            ap = ap.bitcast(dt)
            if ap.shape[1] != n:
                ap = ap[:, 0:n]
        if len(shape) == 3:
            ap = ap.rearrange("p (a b) -> p a b", a=shape[1])
        return ap

    def mm(self, out, lhsT, rhs, start, stop, r, w):
        nc = self.nc
        return self.p.op("pe", lambda: nc.tensor.matmul(out, lhsT, rhs, start=start, stop=stop), reads=r, writes=w)

    def tr(self, out, in_, ident, r, w):
        nc = self.nc
        return self.p.op("pe", lambda: nc.tensor.transpose(out, in_, ident), reads=r, writes=w)

    def dve(self, fn, r, w):
        return self.p.op("dve", fn, reads=r, writes=w)

    def act(self, fn, r, w):
        return self.p.op("act", fn, reads=r, writes=w)

    def pool(self, fn, r, w):
        return self.p.op("pool", fn, reads=r, writes=w)

    def consts(self):
        nc = self.nc
        ones = self.sb("c_ones", [128, 128])
        self.ident = self.sb("c_ident", [128, 128])
        self.identb = self.sb("c_identb", [128, 128], BF16)
        self.triU = self.sb("c_triU", [128, 128])
        self.triL = self.sb("c_triL", [128, 128])
        self.pool(lambda: nc.gpsimd.memset(ones, 1.0), [], ["c_ones"])
        self.pool(lambda: nc.gpsimd.affine_select(self.triU, ones, [[1, 128]], ALU.is_ge, 0.0, base=0, channel_multiplier=-1), ["c_ones"], ["c_triU"])
        self.pool(lambda: nc.gpsimd.affine_select(self.triL, ones, [[-1, 128]], ALU.is_ge, 0.0, base=0, channel_multiplier=1), ["c_ones"], ["c_triL"])
        self.pool(lambda: nc.gpsimd.affine_select(self.ident, self.triU, [[-1, 128]], ALU.is_ge, 0.0, base=0, channel_multiplier=1), ["c_triU"], ["c_ident"])
        self.pool(lambda: nc.gpsimd.tensor_copy(self.identb, self.ident), ["c_ident"], ["c_identb"])
        self.ones = ones


def bc_mid(ap, n):
    P, H = ap.shape
    return ap.unsqueeze(2).broadcast_to([P, H, n])


def build_p0():
    k = KB()
    nc = k.nc
    cc = k.din("cc", [128, 16, 2])
    w = k.din("w", [2, 128, 16, 1536])
    b = k.din("b", [128, 24])
    o = k.dout("o", [128, 24, 2])
    cct = k.sb("cct", [128, 16, 2])
    sg = k.sb("sg", [128, 16, 2])
    s = k.sb("s", [128, 16, 2])
    wt = k.sb("wt", [128, 16, 1536])
    bt = k.sb("bt", [128, 24])
    ot = k.sb("ot", [128, 24, 2])
    p = k.p
    p.dma("sp", cct, cc, writes=["cct"])
    p.dma("sp", bt, b, writes=["bt"])
    k.act(lambda: nc.scalar.activation(out=sg, in_=cct, func=AF.Sigmoid), ["cct"], ["sg"])
    k.dve(lambda: nc.vector.tensor_tensor(out=s, in0=cct, in1=sg, op=ALU.mult), ["cct", "sg"], ["s"])
    for i in range(2):
        for kt in range(16):
            p.dma("sp" if kt % 2 == 0 else "act", wt[:, kt, :], w[i, :, kt, :], writes=[("wt", kt)])
        for ct in range(12):
            ps = k.banks[ct % 4][:, 0:2]
            for kt in range(16):
                k.mm(ps, wt[:, kt, ct * 128:(ct + 1) * 128], s[:, kt, :], kt == 0, kt == 15, ["s", ("wt", kt)], [("bank", ct % 4)])
            j = i * 12 + ct
            k.act(lambda: nc.scalar.activation(out=ot[:, j, :], in_=ps, func=AF.Identity, bias=bt[:, j:j + 1]), [("bank", ct % 4), "bt"], ["ot"])
    p.dma("sp", o, ot, reads=["ot"])
    p.finish()
    return nc


def run_p0(inp):
    nc = build_p0()
    cvec = np.stack([inp["c"][0], inp["c_ctx"]], axis=-1)
    cc = np.ascontiguousarray(cvec.reshape(16, 128, 2).transpose(1, 0, 2))
    maps = []
    for g in range(NCORES):
        c0 = g * 1536
        wsl = inp["w_ada"][:, :, c0:c0 + 1536]
        wl = np.ascontiguousarray(wsl.reshape(2, 16, 128, 1536).transpose(0, 2, 1, 3))
        bl = inp["b_ada"][:, c0:c0 + 1536].reshape(2, 12, 128).transpose(2, 0, 1).reshape(128, 24)
        maps.append({"cc": cc, "w": wl, "b": np.ascontiguousarray(bl)})
    res = run_bass_kernel_spmd(nc, maps, core_ids=list(range(NCORES)))
    mod = np.zeros((2, 2, 6 * D), np.float32)
    for g in range(NCORES):
        og = res.results[g]["o"]
        og = og.reshape(128, 2, 12, 2)
        mod[:, :, g * 1536:(g + 1) * 1536] = og.transpose(1, 3, 2, 0).reshape(2, 2, 1536)
    return mod


ORDER_B = [1, 0] + list(range(NCH - 1, 1, -1))


STOP = 0
CUT = 0


def build_pA(nchunks=NCH):
    k = KB()
    nc, p = k.nc, k.p
    xc = k.din("xc", [NCH, 128, 16, 132])
    modd = k.din("mod", [128, 16, 4])
    wg = k.din("wg", [128, 16, WG_COLS])
    cwd = k.din("cw", [128, 6, 5])
    cbd = k.din("cb", [128, 6])
    dwd = k.din("dw", [128, 2, 31])
    dbd = k.din("db", [128, 2])
    repd = k.din("rep", [128, 552])
    hs = k.dout("hs", [T, 512], BF16)
    hpre = k.dout("hpre", [2, 128, T])
    ypart_d = k.dscr("ypart_d", [NCH, 128, 512])
    sz_d = k.dscr("sz_d", [NCH, 128, 512])
    sb_d = k.dscr("sb_d", [NCH, 128, 512])
    glu_d = k.dscr("glu_d", [2, 128, T])

    k.consts()
    k.arena_init(43500)
    B = k.banks
    wb = k.asb("wb", [128, 16, WG_COLS], BF16)
    for kt in range(16):
        p.dma("pool", wb[:, kt, :], wg[:, kt, :], writes=[("wb", kt)])
    WBK = [("wb", kt) for kt in range(16)]
    mod = k.sb("modt", [128, 16, 4])
    cw = k.sb("cw", [128, 6, 5])
    cb = k.sb("cb", [128, 6])
    dw = k.sb("dwt", [128, 2, 31])
    db = k.sb("dbt", [128, 2])
    rep = k.sb("rep", [128, 552])
    p.dma("sp", mod, modd, writes=["mod"])
    p.dma("sp", cw, cwd, writes=["cw"])
    p.dma("sp", cb, cbd, writes=["cb"])
    p.dma("sp", dw, dwd, writes=["dw"])
    p.dma("sp", db, dbd, writes=["db"])
    p.dma("sp", rep, repd, writes=["rep"])
    scl = k.sb("scl", [128, 16, 2])
    k.dve(lambda: nc.vector.tensor_scalar(out=scl[:, :, 0], in0=mod[:, :, 1], scalar1=1.0, scalar2=None, op0=ALU.add), ["mod"], ["scl"])
    k.dve(lambda: nc.vector.tensor_scalar(out=scl[:, :, 1], in0=mod[:, :, 3], scalar1=1.0, scalar2=None, op0=ALU.add), ["mod", "scl"], ["scl"])
    dtb = rep[:, 0:16]
    Aneg = k.sb("Aneg", [128, 16])
    k.act(lambda: nc.scalar.activation(out=Aneg, in_=rep[:, 16:32], func=AF.Exp), ["rep"], ["Aneg"])
    k.dve(lambda: nc.vector.tensor_scalar(out=Aneg, in0=Aneg, scalar1=-1.0, scalar2=None, op0=ALU.mult), ["Aneg"], ["Aneg"])
    Dsk = rep[:, 32:40]
    normw = rep[:, 40:552]

    Ccm = k.sb("Ccm", [128, NCH, 128], BF16)
    ea_b = k.sb("ea_b", [128, NCH, 8])
    dec_b = k.sb("dec_b", [128, NCH, 8])
    h_f = k.sb("h_f", [128, 512])
    hb_f = k.sb("hb_f", [128, 512], BF16)
    k.dve(lambda: nc.vector.memset(h_f, 0.0), [], ["h_f"])
    k.dve(lambda: nc.vector.memset(hb_f, 0.0), [], ["hb_f"])

    def tmp(name, shape, dt=F32, n=2):
        if n == 1:
            a = k.asb(name, shape, dt)
            return [a, a]
        return [k.asb(f"{name}{i}", shape, dt) for i in range(n)]

    xt = tmp("xt", [128, 16, 132])
    ut = tmp("ut", [128, 16, 132], BF16)
    raw = tmp("raw", [128, 6, 132])
    acc = tmp("acc", [128, 6, 128])
    sgm = tmp("sgm", [128, 6, 128])
    xcm = tmp("xcm", [128, 4, 128])
    Bcm = tmp("Bcm", [128, 128], BF16)
    Btm = tmp("Btm", [128, 128], BF16)
    dtt = tmp("dtt", [128, 16])
    at = tmp("at", [128, 16])
    arep = tmp("arep", [128, 16, 128])
    acs = tmp("acs", [128, 16])
    nacs = tmp("nacs", [128, 16])
    GU = tmp("GU", [128, 128])
    GL = tmp("GL", [128, 128])
    E = tmp("E", [128, 16, 128], n=1)
    M = tmp("M", [128, 16, 128], BF16, n=1)
    xdt = tmp("xdt", [128, 2, 512], BF16)
    dte = tmp("dte", [128, 16])
    xdte = tmp("xdte", [128, 2, 512], BF16)
    ea_f = tmp("ea_f", [128, 8])
    dec_f = tmp("dec_f", [128, 8])
    t1 = tmp("t1", [128, 512])
    yp = tmp("yp", [128, 512])
    zs = tmp("zs", [128, 512], n=1)
    szt = tmp("szt", [128, 512])
    sbt = tmp("sbt", [128, 512])
    gsg = tmp("gsg", [128, 2, 128], n=1)
    glu = tmp("glu", [128, 2, 128])
    htmp = tmp("htmp", [128, 512], n=1)

    sgp = tmp("sgp", [128, 6, 128], n=1)

    def keyf(b):
        return lambda n: f"{n}0" if n in ("E", "M", "sgp", "htmp", "zs", "gsg") else f"{n}{b}"

    def stage1(c):
        b = c % 2
        kb = keyf(b)
        mi = 1 if c < 2 else 0
        p.dma("sp" if c % 2 == 0 else "act", xt[b], xc[c], writes=[kb("xt")])
        for kt in range(16):
            (k.act if kt % 2 == 0 else k.dve)(
                (lambda kt=kt: nc.scalar.activation(out=ut[b][:, kt, :], in_=xt[b][:, kt, :], func=AF.Identity,
                                                    bias=mod[:, kt, 2 * mi:2 * mi + 1], scale=scl[:, kt, mi:mi + 1]))
                if kt % 2 == 0 else
                (lambda kt=kt: nc.vector.tensor_scalar(out=ut[b][:, kt, :], in0=xt[b][:, kt, :], scalar1=scl[:, kt, mi:mi + 1],
                                                       scalar2=mod[:, kt, 2 * mi:2 * mi + 1], op0=ALU.mult, op1=ALU.add)),
                [kb("xt"), "mod", "scl"], [(kb("ut"), kt)])
        for m in range(3):
            o = B[0][:, m * 132:m * 132 + 132]
            for kt in range(16):
                k.mm(o, wb[:, kt, m * 128:(m + 1) * 128], ut[b][:, kt, :], kt == 0, kt == 15, [("wb", kt), (kb("ut"), kt)], [("bank", 0)])
        k.act(lambda: nc.scalar.copy(out=raw[b][:, 0:3, :], in_=B[0][:, 0:396].rearrange("p (m n) -> p m n", m=3)), [("bank", 0)], [kb("raw")])
        for m in range(4):
            o = B[1][:, m * 128:(m + 1) * 128]
            for kt in range(16):
                k.mm(o, wb[:, kt, 768 + m * 128:768 + (m + 1) * 128], ut[b][:, kt, 2:130], kt == 0, kt == 15, [("wb", kt), (kb("ut"), kt)], [("bank", 1)])
        k.act(lambda: nc.scalar.activation(out=gsg[b], in_=B[1][:, 256:512].rearrange("p (m n) -> p m n", m=2), func=AF.Sigmoid), [("bank", 1)], [kb("gsg")])
        k.dve(lambda: nc.vector.tensor_tensor(out=glu[b], in0=B[1][:, 0:256].rearrange("p (m n) -> p m n", m=2), in1=gsg[b], op=ALU.mult),
              [("bank", 1), kb("gsg")], [kb("glu")])
        p.dma("sp", glu_d[:, :, c * 128:(c + 1) * 128].rearrange("m p n -> p m n"), glu[b], reads=[kb("glu")], writes=["glu_d"])
        for m in range(3, 6):
            o = B[0][:, (m - 3) * 132:(m - 3) * 132 + 132]
            for kt in range(16):
                k.mm(o, wb[:, kt, m * 128:(m + 1) * 128], ut[b][:, kt, :], kt == 0, kt == 15, [("wb", kt), (kb("ut"), kt)], [("bank", 0)])
        for kt in range(16):
            k.mm(B[4][:, 0:16], ut[b][:, kt, 2:130], wb[:, kt, 1792:1808], kt == 0, kt == 15, [("wb", kt), (kb("ut"), kt)], [("bank", 4)])
        k.act(lambda: nc.scalar.copy(out=raw[b][:, 3:6, :], in_=B[0][:, 0:396].rearrange("p (m n) -> p m n", m=3)), [("bank", 0)], [kb("raw")])
        k.dve(lambda: nc.vector.tensor_tensor(out=dtt[b], in0=B[4][:, 0:16], in1=dtb, op=ALU.add), [("bank", 4), "rep"], [kb("dtt")])
        for kt in range(16):
            k.mm(B[1], ut[b][:, kt, 2:130], wb[:, kt, 1280:1792], kt == 0, kt == 15, [("wb", kt), (kb("ut"), kt)], [("bank", 1)])
        k.act(lambda: nc.scalar.activation(out=zs[b], in_=B[1], func=AF.Sigmoid), [("bank", 1)], [kb("zs")])
        k.dve(lambda: nc.vector.tensor_tensor(out=szt[b], in0=B[1], in1=zs[b], op=ALU.mult), [("bank", 1), kb("zs")], [kb("szt")])
        p.dma("sp", sz_d[c], szt[b], reads=[kb("szt")], writes=[("sz_d", c)])
        if c == 0 or c == 2:
            k.dve(lambda: nc.vector.memset(raw[b][:, :, 0:2], 0.0), [kb("raw")], [kb("raw")])
        if c == 1 or c == NCH - 1:
            k.dve(lambda: nc.vector.memset(raw[b][:, :, 130:132], 0.0), [kb("raw")], [kb("raw")])
        k.act(lambda: nc.scalar.activation(out=dtt[b], in_=dtt[b], func=AF.Exp), [kb("dtt")], [kb("dtt")])
        k.act(lambda: nc.scalar.activation(out=dtt[b], in_=dtt[b], func=AF.Ln, bias=1.0), [kb("dtt")], [kb("dtt")])
        k.dve(lambda: nc.vector.tensor_tensor(out=at[b], in0=dtt[b], in1=Aneg, op=ALU.mult), [kb("dtt"), "Aneg"], [kb("at")])
        k.pool(lambda: nc.gpsimd.tensor_copy(out=arep[b], in_=bc_mid(at[b], 128)), [kb("at")], [kb("arep")])

    def stage2(c):
        if CUT == 1:
            return
        b = c % 2
        kb = keyf(b)
        k.dve(lambda: nc.vector.tensor_tensor(out=acc[b], in0=raw[b][:, :, 0:128], in1=bc_mid(cw[:, :, 0], 128), op=ALU.mult), [kb("raw"), "cw"], [kb("acc")])
        for s in range(1, 5):
            if False:
                pass
            else:
                k.dve(lambda s=s: nc.vector.tensor_tensor(out=sgm[b], in0=raw[b][:, :, s:s + 128], in1=bc_mid(cw[:, :, s], 128), op=ALU.mult), [kb("raw"), "cw"], [kb("sgm")])
                k.dve(lambda: nc.vector.tensor_tensor(out=acc[b], in0=acc[b], in1=sgm[b], op=ALU.add), [kb("acc"), kb("sgm")], [kb("acc")])
        k.dve(lambda: nc.vector.tensor_tensor(out=acc[b], in0=acc[b], in1=bc_mid(cb, 128), op=ALU.add), [kb("acc"), "cb"], [kb("acc")])
        k.act(lambda: nc.scalar.activation(out=sgm[b], in_=acc[b], func=AF.Sigmoid), [kb("acc")], [kb("sgm")])
        k.dve(lambda: nc.vector.tensor_tensor(out=xcm[b], in0=acc[b][:, 0:4, :], in1=sgm[b][:, 0:4, :], op=ALU.mult), [kb("acc"), kb("sgm")], [kb("xcm")])
        k.dve(lambda: nc.vector.tensor_tensor(out=Bcm[b], in0=acc[b][:, 4, :], in1=sgm[b][:, 4, :], op=ALU.mult), [kb("acc"), kb("sgm")], [kb("Bcm")])
        k.dve(lambda: nc.vector.tensor_tensor(out=Ccm[:, c, :], in0=acc[b][:, 5, :], in1=sgm[b][:, 5, :], op=ALU.mult), [kb("acc"), kb("sgm")], [("Ccm", c)])
        for m in range(4):
            k.tr(B[5][:, m * 128:(m + 1) * 128], xcm[b][:, m, :], k.ident, [kb("xcm"), "c_ident"], [("bank", 5)])
        b4bf = B[4][:, 256:320].bitcast(BF16)
        k.tr(b4bf, Bcm[b], k.identb, [kb("Bcm"), "c_identb"], [("bank", 4)])
        k.act(lambda: nc.scalar.copy(out=Btm[b], in_=b4bf), [("bank", 4)], [kb("Btm")])
        k.mm(B[4][:, 128:256], Bcm[b], Ccm[:, c, :], True, True, [kb("Bcm"), ("Ccm", c)], [("bank", 4)])
        k.mm(B[4][:, 16:24], k.triU, at[b][:, 0:8], True, True, ["c_triU", kb("at")], [("bank", 4)])
        k.mm(B[4][:, 24:32], k.triL, at[b][:, 8:16], True, True, ["c_triL", kb("at")], [("bank", 4)])
        k.dve(lambda: nc.vector.tensor_tensor(out=GU[b], in0=B[4][:, 128:256], in1=k.triU, op=ALU.mult), [("bank", 4), "c_triU"], [kb("GU")])
        k.dve(lambda: nc.vector.tensor_tensor(out=GL[b], in0=B[4][:, 128:256], in1=k.triL, op=ALU.mult), [("bank", 4), "c_triL"], [kb("GL")])
        k.dve(lambda: nc.vector.tensor_copy(out=acs[b], in_=B[4][:, 16:32]), [("bank", 4)], [kb("acs")])
        k.dve(lambda: nc.vector.tensor_scalar(out=nacs[b], in0=B[4][:, 16:32], scalar1=-1.0, scalar2=None, op0=ALU.mult), [("bank", 4)], [kb("nacs")])
        k.act(lambda: nc.scalar.activation(out=ea_f[b], in_=acs[b][:, 0:8], func=AF.Exp), [kb("acs")], [kb("ea_f")])
        k.act(lambda: nc.scalar.activation(out=ea_b[:, c, :], in_=acs[b][:, 8:16], func=AF.Exp), [kb("acs")], [("ea_b", c)])
        if CUT == 2:
            return
        k.mm(B[7], Ccm[:, c, :], hb_f, True, True, [("Ccm", c), "hb_f"], [("bank", 7)])
        k.dve(lambda: nc.vector.tensor_tensor(out=t1[b].rearrange("p (e q) -> p e q", e=8), in0=B[7].rearrange("p (e q) -> p e q", e=8),
                                              in1=bc_mid(ea_f[b], 64), op=ALU.mult), [("bank", 7), kb("ea_f")], [kb("t1")])
        xs3 = B[5].rearrange("p (e q) -> p e q", e=8)
        for d in range(2):
            k.dve(lambda d=d: nc.vector.tensor_tensor(out=xdt[b][:, d, :].rearrange("p (e q) -> p e q", e=8), in0=xs3,
                                                      in1=bc_mid(dtt[b][:, 8 * d:8 * d + 8], 64), op=ALU.mult),
                  [("bank", 5), kb("dtt")], [(kb("xdt"), d)])
        if CUT == 3:
            return
        for d in range(2):
            tri = k.triU if d == 0 else k.triL
            G = GU[b] if d == 0 else GL[b]
            for e in range(8):
                h = 8 * d + e
                bk = 2 + e // 4
                k.mm(B[bk][:, (e % 4) * 128:(e % 4 + 1) * 128], arep[b][:, h, :], tri, True, True, [kb("arep"), "c_triU", "c_triL"], [("bank", bk)])
            for e in range(8):
                h = 8 * d + e
                bk = 2 + e // 4
                k.dve(lambda h=h, e=e, bk=bk: nc.vector.tensor_scalar(out=E[b][:, h, :], in0=B[bk][:, (e % 4) * 128:(e % 4 + 1) * 128],
                                                                      scalar1=acs[b][:, h:h + 1], scalar2=0.0, op0=ALU.subtract, op1=ALU.min),
                      [("bank", bk), kb("acs")], [(kb("E"), d)])
            k.act(lambda: nc.scalar.activation(out=E[b][:, 8 * d:8 * d + 8, :], in_=E[b][:, 8 * d:8 * d + 8, :], func=AF.Exp), [(kb("E"), d)], [(kb("E"), d)])
            k.dve(lambda: nc.vector.tensor_tensor(out=M[b][:, 8 * d:8 * d + 8, :], in0=E[b][:, 8 * d:8 * d + 8, :],
                                                  in1=G.unsqueeze(1).broadcast_to([128, 8, 128]), op=ALU.mult),
                  [(kb("E"), d), kb("GU"), kb("GL")], [(kb("M"), d)])
            col = 127 if d == 0 else 0
            for q in range(2):
                bk = 2 + q
                k.dve(lambda q=q, bk=bk: nc.vector.tensor_tensor(out=dte[b][:, 8 * d + 4 * q:8 * d + 4 * q + 4],
                                                               in0=B[bk].rearrange("p (e n) -> p e n", e=4)[:, :, col],
                                                               in1=nacs[b][:, 8 * d + 4 * q:8 * d + 4 * q + 4], op=ALU.add),
                      [("bank", bk), kb("nacs")], [(kb("dte"), d)])
            decd = dec_f[b] if d == 0 else dec_b[:, c, :]
            deck = kb("dec_f") if d == 0 else ("dec_b", c)
            for q in range(2):
                bk = 2 + q
                k.act(lambda q=q, bk=bk: nc.scalar.activation(out=decd[:, 4 * q:4 * q + 4], in_=B[bk].rearrange("p (e n) -> p e n", e=4)[:, :, col], func=AF.Exp),
                      [("bank", bk)], [deck])
            k.act(lambda: nc.scalar.activation(out=dte[b][:, 8 * d:8 * d + 8], in_=dte[b][:, 8 * d:8 * d + 8], func=AF.Exp), [(kb("dte"), d)], [(kb("dte"), d)])
            k.dve(lambda: nc.vector.tensor_tensor(out=xdte[b][:, d, :].rearrange("p (e q) -> p e q", e=8),
                                                  in0=xdt[b][:, d, :].rearrange("p (e q) -> p e q", e=8),
                                                  in1=bc_mid(dte[b][:, 8 * d:8 * d + 8], 64), op=ALU.mult),
                  [(kb("xdt"), d), (kb("dte"), d)], [(kb("xdte"), d)])
            k.mm(B[6], Btm[b], xdte[b][:, d, :], True, True, [kb("Btm"), (kb("xdte"), d)], [("bank", 6)])
            if d == 0:
                k.dve(lambda: nc.vector.tensor_tensor(out=htmp[b].rearrange("p (e q) -> p e q", e=8), in0=h_f.rearrange("p (e q) -> p e q", e=8),
                                                      in1=bc_mid(dec_f[b], 64), op=ALU.mult), ["h_f", kb("dec_f")], [kb("htmp")])
                k.dve(lambda: nc.vector.tensor_tensor(out=h_f, in0=htmp[b], in1=B[6], op=ALU.add), [kb("htmp"), ("bank", 6)], ["h_f"])
                k.act(lambda: nc.scalar.copy(out=hb_f, in_=h_f), ["h_f"], ["hb_f"])
            else:
                k.act(lambda: nc.scalar.copy(out=sbt[b], in_=B[6]), [("bank", 6)], [kb("sbt")])
                p.dma("act", sb_d[c], sbt[b], reads=[kb("sbt")], writes=[("sb_d", c)])
        if CUT == 4:
            return
        for e in range(8):
            o = B[7][:, e * 64:(e + 1) * 64]
            k.mm(o, M[b][:, e, :], xdt[b][:, 0, e * 64:(e + 1) * 64], True, False, [(kb("M"), 0), (kb("xdt"), 0)], [("bank", 7)])
            k.mm(o, M[b][:, 8 + e, :], xdt[b][:, 1, e * 64:(e + 1) * 64], False, True, [(kb("M"), 1), (kb("xdt"), 1)], [("bank", 7)])
        k.dve(lambda: nc.vector.tensor_tensor(out=yp[b], in0=B[7], in1=t1[b], op=ALU.add), [("bank", 7), kb("t1")], [kb("yp")])
        k.dve(lambda: nc.vector.tensor_tensor(out=t1[b].rearrange("p (e q) -> p e q", e=8), in0=xs3, in1=bc_mid(Dsk, 64), op=ALU.mult),
              [("bank", 5), "rep", kb("t1")], [kb("t1")])
        k.pool(lambda: nc.gpsimd.tensor_tensor(out=yp[b], in0=yp[b], in1=t1[b], op=ALU.add), [kb("yp"), kb("t1")], [kb("yp")])
        p.dma("act", ypart_d[c], yp[b], reads=[kb("yp")], writes=[("yp_d", c)])

    if nchunks > 0:
        stage1(0)
    for c in range(nchunks):
        if c + 1 < nchunks:
            stage1(c + 1)
        stage2(c)

    k.arena_reset()
    h_b = k.sb("h_b", [128, 512])
    hb_b = k.sb("hb_b", [128, 512], BF16)
    k.dve(lambda: nc.vector.memset(h_b, 0.0), [], ["h_b"])
    k.dve(lambda: nc.vector.memset(hb_b, 0.0), [], ["hb_b"])
    ypl = tmp("ypl", [128, 512])
    szl = tmp("szl", [128, 512])
    sbl = tmp("sbl", [128, 512])
    t2 = tmp("t2", [128, 512])
    y2 = tmp("y2", [128, 512])
    gt = tmp("gt", [128, 512])
    sq = tmp("sq", [128, 512])
    ss = tmp("ss", [128, 1])
    rs = tmp("rs", [128, 1])
    ho = tmp("ho", [128, 512], BF16)
    order = [c for c in ORDER_B if c < nchunks] if CUT == 0 else []
    for it, c in enumerate(order):
        b = it % 2
        kb = lambda n: f"{n}{b}"
        p.dma("sp", ypl[b], ypart_d[c], reads=[("yp_d", c)], writes=[kb("ypl")])
        p.dma("act", szl[b], sz_d[c], reads=[("sz_d", c)], writes=[kb("szl")])
        p.dma("sp", sbl[b], sb_d[c], reads=[("sb_d", c)], writes=[kb("sbl")])
        bk = 2 + b
        k.mm(B[bk], Ccm[:, c, :], hb_b, True, True, [("Ccm", c), "hb_b"], [("bank", bk)])
        k.dve(lambda: nc.vector.tensor_tensor(out=t2[b].rearrange("p (e q) -> p e q", e=8), in0=B[bk].rearrange("p (e q) -> p e q", e=8),
                                              in1=bc_mid(ea_b[:, c, :], 64), op=ALU.mult), [("bank", bk), ("ea_b", c)], [kb("t2")])
        k.pool(lambda: nc.gpsimd.tensor_tensor(out=y2[b], in0=t2[b], in1=ypl[b], op=ALU.add), [kb("t2"), kb("ypl")], [kb("y2")])
        k.dve(lambda: nc.vector.tensor_tensor(out=gt[b], in0=y2[b], in1=szl[b], op=ALU.mult), [kb("y2"), kb("szl")], [kb("gt")])
        k.act(lambda: nc.scalar.activation(out=sq[b], in_=gt[b], func=AF.Square, accum_out=ss[b]), [kb("gt")], [kb("sq"), kb("ss")])
        k.act(lambda: nc.scalar.activation(out=rs[b], in_=ss[b], func=AF.Sqrt, scale=1.0 / 512, bias=LN_EPS), [kb("ss")], [kb("rs")])
        k.dve(lambda: nc.vector.reciprocal(out=rs[b], in_=rs[b]), [kb("rs")], [kb("rs")])
        k.dve(lambda: nc.vector.scalar_tensor_tensor(out=ho[b], in0=gt[b], scalar=rs[b], in1=normw, op0=ALU.mult, op1=ALU.mult),
              [kb("gt"), kb("rs"), "rep"], [kb("ho")])
        p.dma("act", hs[c * 128:(c + 1) * 128, :], ho[b], reads=[kb("ho")], writes=["hs"])
        k.dve(lambda: nc.vector.tensor_tensor(out=t2[b].rearrange("p (e q) -> p e q", e=8), in0=h_b.rearrange("p (e q) -> p e q", e=8),
                                              in1=bc_mid(dec_b[:, c, :], 64), op=ALU.mult), ["h_b", ("dec_b", c), kb("t2")], [kb("t2")])
        k.dve(lambda: nc.vector.tensor_tensor(out=h_b, in0=t2[b], in1=sbl[b], op=ALU.add), [kb("t2"), kb("sbl")], ["h_b"])
        k.act(lambda: nc.scalar.copy(out=hb_b, in_=h_b), ["h_b"], ["hb_b"])

    if nchunks == NCH:
        k.arena_reset()
        gl = k.asb("gl_full", [128, T])
        ca = k.asb("conv_acc", [128, T])
        for m in range(2):
            p.dma("sp", gl, glu_d[m], reads=["glu_d"], writes=["gl_full"])
            w15 = dw[:, m, 15:16]
            k.dve(lambda: nc.vector.tensor_scalar(out=ca, in0=gl, scalar1=w15, scalar2=db[:, m:m + 1], op0=ALU.mult, op1=ALU.add),
                  ["gl_full", "dw", "db"], ["conv_acc"])
            cl, gll = ca[:, CTX:], gl[:, CTX:]
            for s in range(31):
                o = s - 15
                if o == 0:
                    continue
                ws = dw[:, m, s:s + 1]
                lo, hi = max(0, -o), min(CTX, CTX - o)
                k.dve(lambda: nc.vector.scalar_tensor_tensor(out=ca[:, lo:hi], in0=gl[:, lo + o:hi + o], scalar=ws, in1=ca[:, lo:hi], op0=ALU.mult, op1=ALU.add),
                      ["gl_full", "dw", "conv_acc"], ["conv_acc"])
                if m == 0:
                    c3 = cl.rearrange("p (r c) -> p r c", c=64)
                    g3 = gll.rearrange("p (r c) -> p r c", c=64)
                    lo, hi = max(0, -o), min(64, 64 - o)
                    k.dve(lambda: nc.vector.scalar_tensor_tensor(out=c3[:, :, lo:hi], in0=g3[:, :, lo + o:hi + o], scalar=ws, in1=c3[:, :, lo:hi], op0=ALU.mult, op1=ALU.add),
                          ["gl_full", "dw", "conv_acc"], ["conv_acc"])
                else:
                    lo, hi = max(0, -o) * 64, min(128, 128 - o) * 64
                    k.dve(lambda: nc.vector.scalar_tensor_tensor(out=cl[:, lo:hi], in0=gll[:, lo + o * 64:hi + o * 64], scalar=ws, in1=cl[:, lo:hi], op0=ALU.mult, op1=ALU.add),
                          ["gl_full", "dw", "conv_acc"], ["conv_acc"])
            p.dma("sp", hpre[m], ca, reads=["conv_acc"], writes=["hpre"])
    p.finish()
    return nc


def pA_inputs(Xall, mod_i, inp, i):
    XT = Xall.T
    xcs = np.zeros((NCH, 128, 16, 132), np.float32)
    for c in range(NCH):
        seg_lo, seg_hi = (0, CTX) if c < 2 else (CTX, T)
        s = c * 128
        lo, hi = max(s - 2, seg_lo), min(s + 130, seg_hi)
        blk = XT[:, lo:hi].reshape(16, 128, hi - lo).transpose(1, 0, 2)
        xcs[c, :, :, lo - (s - 2):hi - (s - 2)] = blk
    m = mod_i
    modv = np.stack([m[0, 0:D], m[0, D:2 * D], m[1, 0:D], m[1, D:2 * D]], axis=-1)
    modv = np.ascontiguousarray(modv.reshape(16, 128, 4).transpose(1, 0, 2))
    maps = []
    for g in range(NCORES):
        cols = np.concatenate([
            O_XBC + g * 512 + np.arange(512),
            O_XBC + D_INNER + g * 128 + np.arange(128),
            O_XBC + D_INNER + D_BC + g * 128 + np.arange(128),
            O_GLU + g * 128 + np.arange(128),
            O_GLU + 1024 + g * 128 + np.arange(128),
            O_GLU + 2048 + g * 128 + np.arange(128),
            O_GLU + 2048 + 1024 + g * 128 + np.arange(128),
            O_Z + g * 512 + np.arange(512),
            O_DT + g * 8 + np.arange(8),
            O_DT + 64 + g * 8 + np.arange(8),
        ])
        wgm = inp["w_in"][i][:, cols]
        wgm = np.ascontiguousarray(wgm.reshape(16, 128, WG_COLS).transpose(1, 0, 2))
        xbc_cols = cols[:768] - O_XBC
        cwm = inp["ssm_conv_w"][i][:, xbc_cols]
        cwm = np.ascontiguousarray(cwm.reshape(5, 6, 128).transpose(2, 1, 0))
        cbm = np.ascontiguousarray(inp["ssm_conv_b"][i][xbc_cols].reshape(6, 128).T)
        cch = np.concatenate([g * 128 + np.arange(128), 1024 + g * 128 + np.arange(128)])
        dwm = np.ascontiguousarray(inp["conv_dw_w"][i][:, cch].reshape(31, 2, 128).transpose(2, 1, 0))
        dbm = np.ascontiguousarray(inp["conv_dw_b"][i][cch].reshape(2, 128).T)
        rep = np.concatenate([
            inp["ssm_dt_bias"][i][0, g * 8:(g + 1) * 8], inp["ssm_dt_bias"][i][1, g * 8:(g + 1) * 8],
            inp["ssm_a_log"][i][0, g * 8:(g + 1) * 8], inp["ssm_a_log"][i][1, g * 8:(g + 1) * 8],
            inp["ssm_d"][i][g * 8:(g + 1) * 8], inp["ssm_norm_w"][i][g * 512:(g + 1) * 512]]).astype(np.float32)
        rep = np.ascontiguousarray(np.broadcast_to(rep[None, :], (128, 552)))
        maps.append({"xc": xcs, "mod": modv, "wg": wgm, "cw": cwm, "cb": cbm, "dw": dwm, "db": dbm, "rep": rep})
    return maps


NB = 528
BLKS = [(0, 512), (512, 528)]


def ln_cm(k, X, xkey, nb0, nb1, post, pfx):
    nc = k.nc
    n = nb1 - nb0
    B = k.banks
    for kt in range(16):
        sq = k.lnsq[kt % 2][:, 0:n]
        k.act(lambda: nc.scalar.activation(out=sq, in_=X[:, kt, nb0:nb1], func=AF.Square), [xkey], [f"lnsq{kt % 2}"])
        k.mm(B[6][:, 0:n], k.ones, X[:, kt, nb0:nb1], kt == 0, kt == 15, ["c_ones", xkey], [("bank", 6)])
        k.mm(B[7][:, 0:n], k.ones, sq, kt == 0, kt == 15, ["c_ones", f"lnsq{kt % 2}"], [("bank", 7)])
    mt, m2, rs = k.lnm[0][:, 0:n], k.lnm[1][:, 0:n], k.lnm[2][:, 0:n]
    k.dve(lambda: nc.vector.tensor_scalar(out=mt, in0=B[6][:, 0:n], scalar1=1.0 / D, scalar2=None, op0=ALU.mult), [("bank", 6)], ["lnm0"])
    k.dve(lambda: nc.vector.tensor_tensor(out=m2, in0=mt, in1=mt, op=ALU.mult), ["lnm0"], ["lnm1"])
    k.dve(lambda: nc.vector.scalar_tensor_tensor(out=rs, in0=B[7][:, 0:n], scalar=1.0 / D, in1=m2, op0=ALU.mult, op1=ALU.subtract), [("bank", 7), "lnm1"], ["lnm2"])
    k.act(lambda: nc.scalar.activation(out=rs, in_=rs, func=AF.Sqrt, bias=LN_EPS), ["lnm2"], ["lnm2"])
    k.dve(lambda: nc.vector.reciprocal(out=rs, in_=rs), ["lnm2"], ["lnm2"])
    for kt in range(16):
        t = k.lnt[kt % 2][:, 0:n]
        tk = f"lnt{kt % 2}"
        k.dve(lambda: nc.vector.tensor_tensor(out=t, in0=X[:, kt, nb0:nb1], in1=mt, op=ALU.subtract), [xkey, "lnm0"], [tk])
        k.dve(lambda: nc.vector.tensor_tensor(out=t, in0=t, in1=rs, op=ALU.mult), [tk, "lnm2"], [tk])
        post(kt, t, tk)


def ln_scratch(k):
    k.lnsq = [k.sb(f"lnsq{i}", [128, 512]) for i in range(2)]
    k.lnt = [k.sb(f"lnt{i}", [128, 512]) for i in range(2)]
    k.lnm = [k.sb(f"lnm{i}", [128, 512]) for i in range(3)]


def build_pB():
    k = KB()
    nc, p = k.nc, k.p
    B = k.banks
    xTd = k.din("xT", [128, 16, NB])
    modd = k.din("mod", [128, 16, 12])
    hsTd = k.din("hsT", [128, 32, NB], BF16)
    hpTd = k.din("hpT", [128, 16, NB])
    lnpd = k.din("lnp", [128, 16, 4])
    wsod = k.din("wso", [16, 128, 32, 128])
    wcod = k.din("wco", [16, 128, 16, 128])
    wgd = k.din("wgate", [32, 128, 16, 128])
    wod = k.din("wo", [16, 128, 16, 128])
    wrd = k.din("wr", [128, 16, 16])
    x1o = k.dout("x1T", [128, 16, NB])
    u2o = k.dout("u2T", [128, 16, NB], BF16)
    affo = k.dout("aff", [NB, 16])
    k.consts()
    ln_scratch(k)
    mod = k.sb("modt", [128, 16, 12])
    lnp = k.sb("lnpt", [128, 16, 4])
    wr = k.sb("wrt", [128, 16, 16])
    scl = k.sb("scl", [128, 16, 4])
    p.dma("sp", mod, modd, writes=["mod"])
    p.dma("sp", lnp, lnpd, writes=["lnp"])
    p.dma("sp", wr, wrd, writes=["wr"])
    for j, col in enumerate([1, 7, 4, 10]):
        k.dve(lambda j=j, col=col: nc.vector.tensor_scalar(out=scl[:, :, j], in0=mod[:, :, col], scalar1=1.0, scalar2=None, op0=ALU.add), ["mod", "scl"], ["scl"])
    uT = k.sb("uT", [128, 16, NB], BF16)
    hcv = k.sb("hcv", [128, 16, NB], BF16)
    hsT = k.sb("hsT_s", [128, 32, NB], BF16)
    mg = k.sb("mg", [128, 16, NB], BF16)
    R = k.sb("R", [128, 16, NB])
    u2f = mg.rearrange("p a b -> p (a b)").bitcast(F32)[:, 0:2048].rearrange("p (a b) -> p a b", a=16)
    u2b = uT
    for kt in range(0, 32, 8):
        p.dma("act", hsT[:, kt:kt + 8, :], hsTd[:, kt:kt + 8, :], writes=["hsT"])
    SEG = [(0, 512, 0), (512, 528, 1)]
    p.dma("sp", R, xTd, writes=["R"])
    for kt in range(16):
        for (a, b_, kind) in SEG:
            k.act(lambda kt=kt, a=a, b_=b_, kind=kind: nc.scalar.activation(out=uT[:, kt, a:b_], in_=R[:, kt, a:b_], func=AF.Identity,
                                                                             bias=mod[:, kt, 6 * kind:6 * kind + 1], scale=scl[:, kt, kind:kind + 1]),
                  ["R", "mod", "scl"], ["uT"])
    p.barrier()
    p.dma("sp", R, hpTd, reads=[], writes=["R"])
    yt = [k.sb(f"yt{i}", [128, 512]) for i in range(2)]
    sgt = [k.sb(f"sgt{i}", [128, 512]) for i in range(2)]
    for (a, b_) in BLKS:
        n = b_ - a

        def post(kt, t, tk, a=a, b_=b_, n=n):
            y, s = yt[kt % 2][:, 0:n], sgt[kt % 2][:, 0:n]
            k.act(lambda: nc.scalar.activation(out=y, in_=t, func=AF.Identity, bias=lnp[:, kt, 1:2], scale=lnp[:, kt, 0:1]), [tk, "lnp"], [f"yt{kt % 2}"])
            k.act(lambda: nc.scalar.activation(out=s, in_=y, func=AF.Sigmoid), [f"yt{kt % 2}"], [f"sgt{kt % 2}"])
            k.dve(lambda: nc.vector.tensor_tensor(out=hcv[:, kt, a:b_], in0=y, in1=s, op=ALU.mult), [f"yt{kt % 2}", f"sgt{kt % 2}"], ["hcv"])
        ln_cm(k, R, "R", a, b_, post, "c")
    p.barrier()
    wso = [k.sb(f"wso{i}", [128, 32, 128], BF16) for i in range(2)]
    wco = [k.sb(f"wco{i}", [128, 16, 128], BF16) for i in range(2)]
    wga = [k.sb(f"wga{i}", [128, 16, 128], BF16) for i in range(2)]
    wgb = [k.sb(f"wgb{i}", [128, 16, 128], BF16) for i in range(2)]
    wo = [k.sb(f"wo{i}", [128, 16, 128], BF16) for i in range(2)]
    s1 = [k.sb(f"s1_{i}", [128, 512]) for i in range(2)]
    s2 = [k.sb(f"s2_{i}", [128, 512]) for i in range(2)]
    tm_ = [k.sb(f"tm_{i}", [128, 512]) for i in range(2)]
    for dt in range(16):
        w = dt % 2
        p.dma("pool", wso[w], wsod[dt], writes=[f"wso{w}"])
        p.dma("pool", wco[w], wcod[dt], writes=[f"wco{w}"])
        p.dma("pool", wga[w], wgd[dt], writes=[f"wga{w}"])
        p.dma("pool", wgb[w], wgd[16 + dt], writes=[f"wgb{w}"])
        for bi, (a, b_) in enumerate(BLKS):
            n = b_ - a
            for kt in range(32):
                k.mm(B[0][:, 0:n], wso[w][:, kt, :], hsT[:, kt, a:b_], kt == 0, kt == 31, [f"wso{w}", "hsT"], [("bank", 0)])
            for kt in range(16):
                k.mm(B[1][:, 0:n], wga[w][:, kt, :], uT[:, kt, a:b_], kt == 0, kt == 15, [f"wga{w}", "uT"], [("bank", 1)])
            for kt in range(16):
                k.mm(B[2][:, 0:n], wco[w][:, kt, :], hcv[:, kt, a:b_], kt == 0, kt == 15, [f"wco{w}", "hcv"], [("bank", 2)])
            for kt in range(16):
                k.mm(B[3][:, 0:n], wgb[w][:, kt, :], uT[:, kt, a:b_], kt == 0, kt == 15, [f"wgb{w}", "uT"], [("bank", 3)])
            q = bi % 2
            k.act(lambda: nc.scalar.activation(out=s1[q][:, 0:n], in_=B[1][:, 0:n], func=AF.Sigmoid), [("bank", 1)], [f"s1_{q}"])
            k.act(lambda: nc.scalar.activation(out=s2[q][:, 0:n], in_=B[3][:, 0:n], func=AF.Sigmoid), [("bank", 3)], [f"s2_{q}"])
            k.dve(lambda: nc.vector.tensor_tensor(out=tm_[q][:, 0:n], in0=B[0][:, 0:n], in1=s1[q][:, 0:n], op=ALU.mult), [("bank", 0), f"s1_{q}"], [f"tm_{q}"])
            k.dve(lambda: nc.vector.tensor_tensor(out=s2[q][:, 0:n], in0=B[2][:, 0:n], in1=s2[q][:, 0:n], op=ALU.mult), [("bank", 2), f"s2_{q}"], [f"s2_{q}"])
            k.dve(lambda: nc.vector.tensor_tensor(out=mg[:, dt, a:b_], in0=tm_[q][:, 0:n], in1=s2[q][:, 0:n], op=ALU.add), [f"tm_{q}", f"s2_{q}"], [("mg", dt)])
    p.barrier()
    p.dma("sp", R, xTd, reads=[], writes=["R"])
    MG = [("mg", d_) for d_ in range(16)]
    for dt in range(16):
        w = dt % 2
        p.dma("pool", wo[w], wod[dt], writes=[f"wo{w}"])
        for bi, (a, b_) in enumerate(BLKS):
            n = b_ - a
            bk = 4 + bi % 2
            for kt in range(16):
                k.mm(B[bk][:, 0:n], wo[w][:, kt, :], mg[:, kt, a:b_], kt == 0, kt == 15, [f"wo{w}"] + MG, [("bank", bk)])
            for (sa, sb_, kind) in SEG:
                lo, hi = max(a, sa), min(b_, sb_)
                if lo >= hi:
                    continue
                q = bi % 2
                k.dve(lambda lo=lo, hi=hi, kind=kind: nc.vector.tensor_scalar(out=tm_[q][:, 0:hi - lo], in0=B[bk][:, lo - a:hi - a], scalar1=mod[:, dt, 6 * kind + 2:6 * kind + 3],
                                                                            scalar2=None, op0=ALU.mult), [("bank", bk), "mod"], [f"tm_{q}"])
                k.dve(lambda lo=lo, hi=hi: nc.vector.scalar_tensor_tensor(out=R[:, dt, lo:hi], in0=R[:, dt, lo:hi], scalar=ALPHA, in1=tm_[q][:, 0:hi - lo], op0=ALU.mult, op1=ALU.add),
                      ["R", f"tm_{q}"], ["R"])
    p.barrier()
    for (a, b_) in BLKS:
        def post1(kt, t, tk, a=a, b_=b_):
            k.act(lambda: nc.scalar.activation(out=R[:, kt, a:b_], in_=t, func=AF.Identity, bias=lnp[:, kt, 3:4], scale=lnp[:, kt, 2:3]), [tk, "lnp", "R"], ["R"])
        ln_cm(k, R, "R", a, b_, post1, "l")
    p.barrier()
    p.dma("sp", x1o, R, reads=["R"])
    for kt in range(16):
        for (a, b_, kind) in SEG:
            k.act(lambda kt=kt, a=a, b_=b_, kind=kind: nc.scalar.activation(out=u2b[:, kt, a:b_], in_=R[:, kt, a:b_], func=AF.Identity,
                                                                             bias=mod[:, kt, 6 * kind + 3:6 * kind + 4], scale=scl[:, kt, 2 + kind:3 + kind]),
                  ["R", "mod", "scl"], ["u2b"])
    p.dma("act", u2o, u2b, reads=["u2b"])
    lg = [k.sb(f"lg{i}", [128, 16]) for i in range(2)]
    mx = [k.sb(f"mx{i}", [128, 1]) for i in range(2)]
    sm = [k.sb(f"sm{i}", [128, 1]) for i in range(2)]
    tiles = [(i * 128, min((i + 1) * 128, NB)) for i in range((NB + 127) // 128)]
    for ti, (a, b_) in enumerate(tiles):
        n = b_ - a
        q = ti % 2
        kind = 0 if a < 512 else 1
        for kt in range(16):
            k.act(lambda kt=kt: nc.scalar.activation(out=u2f[:, kt, 0:n], in_=R[:, kt, a:b_], func=AF.Identity,
                                                     bias=mod[:, kt, 6 * kind + 3:6 * kind + 4], scale=scl[:, kt, 2 + kind:3 + kind]), ["R", "mod", "scl"], ["u2f"])
        bk = 4 + q
        for kt in range(16):
            k.mm(B[bk][0:n, 0:16], u2f[:, kt, 0:n], wr[:, kt, :], kt == 0, kt == 15, ["u2f", "wr"], [("bank", bk)])
        k.dve(lambda: nc.vector.tensor_reduce(out=mx[q][0:n], in_=B[bk][0:n, 0:16], axis=AX.X, op=ALU.max), [("bank", bk)], [f"mx{q}"])
        k.dve(lambda: nc.vector.tensor_scalar(out=mx[q][0:n], in0=mx[q][0:n], scalar1=-1.0, scalar2=None, op0=ALU.mult), [f"mx{q}"], [f"mx{q}"])
        k.act(lambda: nc.scalar.activation(out=lg[q][0:n], in_=B[bk][0:n, 0:16], func=AF.Exp, bias=mx[q][0:n], accum_out=sm[q][0:n]), [("bank", bk), f"mx{q}"], [f"lg{q}", f"sm{q}"])
        k.dve(lambda: nc.vector.reciprocal(out=sm[q][0:n], in_=sm[q][0:n]), [f"sm{q}"], [f"sm{q}"])
        k.dve(lambda: nc.vector.tensor_scalar(out=lg[q][0:n], in0=lg[q][0:n], scalar1=sm[q][0:n], scalar2=None, op0=ALU.mult), [f"lg{q}", f"sm{q}"], [f"lg{q}"])
        p.dma("sp", affo[a:b_, :], lg[q][0:n], reads=[f"lg{q}"])
    p.finish()
    return nc


def cm_tiles(w, nk):
    K, N = w.shape
    return np.ascontiguousarray(w.reshape(nk, 128, N // 128, 128).transpose(2, 1, 0, 3))


def cmvec(v):
    return v.reshape(16, 128).T


def to_cm(Xtok):
    n, C = Xtok.shape
    return np.ascontiguousarray(Xtok.T.reshape(C // 128, 128, n).transpose(1, 0, 2))


def from_cm(Xcm):
    p_, kt, n = Xcm.shape
    return np.ascontiguousarray(Xcm.transpose(2, 1, 0).reshape(n, kt * 128))


CBLK = [(0, CTX)] + [(CTX + i * 512, CTX + (i + 1) * 512) for i in range(SEQ // 512)]


def build_pC(nblk=len(CBLK)):
    k = KB()
    nc, p = k.nc, k.p
    B = k.banks
    u2d = k.din("u2T", [128, 16, T], BF16)
    affd = k.din("affT", [2, T])
    wgd = k.din("wg", [2, 24, 128, 16, 128])
    wud = k.din("wu", [2, 24, 128, 16, 128])
    wdd = k.din("wd", [2, 16, 128, 24, 128])
    outd = k.dout("part", [128, 16, T])
    k.consts()
    aff = k.sb("aff", [2, T])
    wts = k.sb("wts", [2, T])
    p.dma("sp", aff, affd, writes=["aff"])
    thr = k.sb("thr", [2, 2])
    for si, (a, b_, kk) in enumerate([(0, CTX, 2 * CTX // NEXP), (CTX, T, 2 * SEQ // NEXP)]):
        lo = k.sb(f"lo{si}", [2, 1])
        hi = k.sb(f"hi{si}", [2, 1])
        mid = k.sb(f"mid{si}", [2, 1])
        cnt = k.sb(f"cnt{si}", [2, 1])
        ge = k.sb(f"ge{si}", [2, 1])
        dl = k.sb(f"dl{si}", [2, 1])
        K_ = [f"bis{si}"]
        k.dve(lambda: nc.vector.memset(lo, 0.0), [], K_)
        k.dve(lambda: nc.vector.memset(hi, 1.0), K_, K_)
        for it in range(36):
            k.dve(lambda: nc.vector.tensor_tensor(out=mid, in0=lo, in1=hi, op=ALU.add), K_, K_)
            k.dve(lambda: nc.vector.tensor_scalar(out=mid, in0=mid, scalar1=0.5, scalar2=None, op0=ALU.mult), K_, K_)
            k.dve(lambda: nc.vector.tensor_scalar(out=wts[:, a:b_], in0=aff[:, a:b_], scalar1=mid, scalar2=0.0, op0=ALU.is_ge, op1=ALU.add, accum_out=cnt),
                  K_ + ["aff"], K_ + ["wts"])
            k.dve(lambda: nc.vector.tensor_scalar(out=ge, in0=cnt, scalar1=float(kk) - 0.5, scalar2=None, op0=ALU.is_ge), K_, K_)
            k.dve(lambda: nc.vector.tensor_tensor(out=dl, in0=mid, in1=lo, op=ALU.subtract), K_, K_)
            k.dve(lambda: nc.vector.tensor_tensor(out=dl, in0=dl, in1=ge, op=ALU.mult), K_, K_)
            k.dve(lambda: nc.vector.tensor_tensor(out=lo, in0=lo, in1=dl, op=ALU.add), K_, K_)
            k.dve(lambda: nc.vector.tensor_tensor(out=dl, in0=hi, in1=mid, op=ALU.subtract), K_, K_)
            k.dve(lambda: nc.vector.tensor_tensor(out=dl, in0=dl, in1=ge, op=ALU.mult), K_, K_)
            k.dve(lambda: nc.vector.tensor_tensor(out=hi, in0=mid, in1=dl, op=ALU.add), K_, K_)
        k.dve(lambda: nc.vector.tensor_scalar(out=wts[:, a:b_], in0=aff[:, a:b_], scalar1=lo, scalar2=None, op0=ALU.is_ge), K_ + ["aff", "wts"], ["wts"])
        k.dve(lambda: nc.vector.tensor_tensor(out=wts[:, a:b_], in0=wts[:, a:b_], in1=aff[:, a:b_], op=ALU.mult), ["wts", "aff"], ["wts"])
    sel = k.sb("sel", [2, 2, 128])
    k.dve(lambda: nc.vector.tensor_copy(out=sel, in_=k.ident[0:2, 0:2].unsqueeze(2).broadcast_to([2, 2, 128])), ["c_ident"], ["sel"])
    ub = [k.sb(f"ub{i}", [128, 16, 512], BF16) for i in range(2)]
    wgt = [k.sb(f"wgt{i}", [128, 16, 128], BF16) for i in range(2)]
    wut = [k.sb(f"wut{i}", [128, 16, 128], BF16) for i in range(2)]
    wdt = [k.sb(f"wdt{i}", [128, 24, 128], BF16) for i in range(2)]
    hT = k.sb("hT", [128, 24, 512], BF16)
    wbc = k.sb("wbc", [128, 512])
    sg = [k.sb(f"sg{i}", [128, 512]) for i in range(2)]
    t1 = [k.sb(f"t1_{i}", [128, 512]) for i in range(2)]
    acc = k.sb("acc", [128, 16, 512])
    for bi, (a, b_) in enumerate(CBLK[:nblk]):
        n = b_ - a
        u = ub[bi % 2]
        p.dma("sp", u[:, :, 0:n], u2d[:, :, a:b_], writes=[f"ub{bi % 2}"])
        for le in range(2):
            k.mm(B[7][:, 0:n], sel[:, le, :], wts[:, a:b_], True, True, ["sel", "wts"], [("bank", 7)])
            k.act(lambda: nc.scalar.copy(out=wbc[:, 0:n], in_=B[7][:, 0:n]), [("bank", 7)], ["wbc"])
            for ft in range(24):
                w = ft % 2
                p.dma("pool", wgt[w], wgd[le, ft], writes=[f"wgt{w}"])
                p.dma("pool", wut[w], wud[le, ft], writes=[f"wut{w}"])
                bg, bu = 2 * w, 2 * w + 1
                for kt in range(16):
                    k.mm(B[bg][:, 0:n], wgt[w][:, kt, :], u[:, kt, 0:n], kt == 0, kt == 15, [f"wgt{w}", f"ub{bi % 2}"], [("bank", bg)])
                for kt in range(16):
                    k.mm(B[bu][:, 0:n], wut[w][:, kt, :], u[:, kt, 0:n], kt == 0, kt == 15, [f"wut{w}", f"ub{bi % 2}"], [("bank", bu)])
                k.act(lambda: nc.scalar.activation(out=sg[w][:, 0:n], in_=B[bg][:, 0:n], func=AF.Sigmoid), [("bank", bg)], [f"sg{w}"])
                k.dve(lambda: nc.vector.tensor_tensor(out=sg[w][:, 0:n], in0=B[bg][:, 0:n], in1=sg[w][:, 0:n], op=ALU.mult), [("bank", bg), f"sg{w}"], [f"sg{w}"])
                k.dve(lambda: nc.vector.tensor_tensor(out=t1[w][:, 0:n], in0=B[bu][:, 0:n], in1=sg[w][:, 0:n], op=ALU.mult), [("bank", bu), f"sg{w}"], [f"t1_{w}"])
                k.pool(lambda: nc.gpsimd.tensor_tensor(out=hT[:, ft, 0:n], in0=t1[w][:, 0:n], in1=wbc[:, 0:n], op=ALU.mult), [f"t1_{w}", "wbc"], [("hT", ft)])
            HT = [("hT", f_) for f_ in range(24)]
            for dt in range(16):
                w = dt % 2
                p.dma("pool", wdt[w], wdd[le, dt], writes=[f"wdt{w}"])
                bk = 4 + w
                for ft in range(24):
                    k.mm(B[bk][:, 0:n], wdt[w][:, ft, :], hT[:, ft, 0:n], ft == 0, ft == 23, [f"wdt{w}"] + HT, [("bank", bk)])
                if le == 0:
                    k.act(lambda: nc.scalar.copy(out=acc[:, dt, 0:n], in_=B[bk][:, 0:n]), [("bank", bk)], [("acc", dt)])
                else:
                    k.dve(lambda: nc.vector.tensor_tensor(out=acc[:, dt, 0:n], in0=acc[:, dt, 0:n], in1=B[bk][:, 0:n], op=ALU.add), [("bank", bk), ("acc", dt)], [("acc", dt)])
        p.dma("act", outd[:, :, a:b_], acc[:, :, 0:n], reads=[("acc", d_) for d_ in range(16)])
    p.finish()
    return nc


NS = 1056
SBLK = [(0, 512), (512, 1024), (1024, NS)]


def build_pC1():
    k = KB()
    nc, p = k.nc, k.p
    affd = k.din("affT", [2, T])
    wtso = k.dout("wts", [2, T])
    sloto = k.dout("slot", [2, T])
    aff = k.sb("aff", [2, T])
    wts = k.sb("wts_s", [2, T])
    msk = k.sb("msk", [2, T])
    pos = k.sb("pos", [2, T])
    p.dma("sp", aff, affd, writes=["aff"])
    for si, (a, b_, kk) in enumerate([(0, CTX, 2 * CTX // NEXP), (CTX, T, 2 * SEQ // NEXP)]):
        lo = k.sb(f"lo{si}", [2, 1])
        hi = k.sb(f"hi{si}", [2, 1])
        mid = k.sb(f"mid{si}", [2, 1])
        cnt = k.sb(f"cnt{si}", [2, 1])
        ge = k.sb(f"ge{si}", [2, 1])
        dl = k.sb(f"dl{si}", [2, 1])
        K_ = [f"bis{si}"]
        k.dve(lambda: nc.vector.memset(lo, 0.0), [], K_)
        k.dve(lambda: nc.vector.memset(hi, 1.0), K_, K_)
        for it in range(36):
            k.dve(lambda: nc.vector.tensor_tensor(out=mid, in0=lo, in1=hi, op=ALU.add), K_, K_)
            k.dve(lambda: nc.vector.tensor_scalar(out=mid, in0=mid, scalar1=0.5, scalar2=None, op0=ALU.mult), K_, K_)
            k.dve(lambda: nc.vector.tensor_scalar(out=msk[:, a:b_], in0=aff[:, a:b_], scalar1=mid, scalar2=0.0, op0=ALU.is_ge, op1=ALU.add, accum_out=cnt),
                  K_ + ["aff"], K_ + ["msk"])
            k.dve(lambda: nc.vector.tensor_scalar(out=ge, in0=cnt, scalar1=float(kk) - 0.5, scalar2=None, op0=ALU.is_ge), K_, K_)
            k.dve(lambda: nc.vector.tensor_tensor(out=dl, in0=mid, in1=lo, op=ALU.subtract), K_, K_)
            k.dve(lambda: nc.vector.tensor_tensor(out=dl, in0=dl, in1=ge, op=ALU.mult), K_, K_)
            k.dve(lambda: nc.vector.tensor_tensor(out=lo, in0=lo, in1=dl, op=ALU.add), K_, K_)
            k.dve(lambda: nc.vector.tensor_tensor(out=dl, in0=hi, in1=mid, op=ALU.subtract), K_, K_)
            k.dve(lambda: nc.vector.tensor_tensor(out=dl, in0=dl, in1=ge, op=ALU.mult), K_, K_)
            k.dve(lambda: nc.vector.tensor_tensor(out=hi, in0=mid, in1=dl, op=ALU.add), K_, K_)
        k.dve(lambda: nc.vector.tensor_scalar(out=msk[:, a:b_], in0=aff[:, a:b_], scalar1=lo, scalar2=None, op0=ALU.is_ge), K_ + ["aff", "msk"], ["msk"])
        k.dve(lambda: nc.vector.tensor_tensor(out=wts[:, a:b_], in0=msk[:, a:b_], in1=aff[:, a:b_], op=ALU.mult), ["msk", "aff"], ["wts"])
        k.dve(lambda: nc.vector.tensor_tensor_scan(out=pos[:, a:b_], data0=msk[:, a:b_], data1=msk[:, a:b_], initial=0.0, op0=ALU.add, op1=ALU.max), ["msk"], ["pos"])
        k.dve(lambda: nc.vector.tensor_tensor(out=pos[:, a:b_], in0=pos[:, a:b_], in1=msk[:, a:b_], op=ALU.mult), ["pos", "msk"], ["pos"])
        k.dve(lambda: nc.vector.tensor_scalar(out=pos[:, a:b_], in0=pos[:, a:b_], scalar1=-1.0, scalar2=None, op0=ALU.add), ["pos"], ["pos"])
    p.dma("sp", wtso, wts, reads=["wts"])
    p.dma("act", sloto, pos, reads=["pos"])
    p.finish()
    return nc


def build_pC2():
    k = KB()
    nc, p = k.nc, k.p
    B = k.banks
    xed = k.din("xeT", [2, 128, 16, NS], BF16)
    gwd = k.din("gw", [2, NS])
    wgd = k.din("wg", [2, 24, 128, 16, 128])
    wud = k.din("wu", [2, 24, 128, 16, 128])
    wdd = k.din("wd", [2, 16, 128, 24, 128])
    outd = k.dout("yeT", [2, 128, 16, NS])
    k.consts()
    gw = k.sb("gw_s", [2, NS])
    p.dma("sp", gw, gwd, writes=["gw"])
    sel = k.sb("sel", [2, 2, 128])
    k.dve(lambda: nc.vector.tensor_copy(out=sel, in_=k.ident[0:2, 0:2].unsqueeze(2).broadcast_to([2, 2, 128])), ["c_ident"], ["sel"])
    ub = [k.sb(f"ub{i}", [128, 16, NS], BF16) for i in range(2)]
    wgt = [k.sb(f"wgt{i}", [128, 16, 128], BF16) for i in range(2)]
    wut = [k.sb(f"wut{i}", [128, 16, 128], BF16) for i in range(2)]
    wdt = [k.sb(f"wdt{i}", [128, 24, 128], BF16) for i in range(2)]
    hT = k.sb("hT", [128, 24, NS], BF16)
    wbc = k.sb("wbc", [128, NS])
    sg = [k.sb(f"sg{i}", [128, 512]) for i in range(2)]
    t1 = [k.sb(f"t1_{i}", [128, 512]) for i in range(2)]
    ost = [k.sb(f"ost{i}", [128, 512]) for i in range(2)]
    oi = 0
    for le in range(2):
        u = ub[le]
        p.dma("sp", u, xed[le], writes=[f"ub{le}"])
        for (a, b_) in SBLK:
            n = b_ - a
            k.mm(B[7][:, 0:n], sel[:, le, :], gw[:, a:b_], True, True, ["sel", "gw"], [("bank", 7)])
            k.act(lambda: nc.scalar.copy(out=wbc[:, a:b_], in_=B[7][:, 0:n]), [("bank", 7)], ["wbc"])
        for ft in range(24):
            w = ft % 2
            p.dma("pool", wgt[w], wgd[le, ft], writes=[f"wgt{w}"])
            p.dma("pool", wut[w], wud[le, ft], writes=[f"wut{w}"])
            for bi, (a, b_) in enumerate(SBLK):
                n = b_ - a
                pair = 0 if bi == 0 else 1
                bg, bu = 4 * w + 2 * pair, 4 * w + 2 * pair + 1
                for kt in range(16):
                    k.mm(B[bg][:, 0:n], wgt[w][:, kt, :], u[:, kt, a:b_], kt == 0, kt == 15, [f"wgt{w}", f"ub{le}"], [("bank", bg)])
                for kt in range(16):
                    k.mm(B[bu][:, 0:n], wut[w][:, kt, :], u[:, kt, a:b_], kt == 0, kt == 15, [f"wut{w}", f"ub{le}"], [("bank", bu)])
                q = bi % 2
                k.act(lambda: nc.scalar.activation(out=sg[q][:, 0:n], in_=B[bg][:, 0:n], func=AF.Sigmoid), [("bank", bg)], [f"sg{q}"])
                k.dve(lambda: nc.vector.tensor_tensor(out=sg[q][:, 0:n], in0=B[bg][:, 0:n], in1=sg[q][:, 0:n], op=ALU.mult), [("bank", bg), f"sg{q}"], [f"sg{q}"])
                k.dve(lambda: nc.vector.tensor_tensor(out=t1[q][:, 0:n], in0=B[bu][:, 0:n], in1=sg[q][:, 0:n], op=ALU.mult), [("bank", bu), f"sg{q}"], [f"t1_{q}"])
                k.pool(lambda: nc.gpsimd.tensor_tensor(out=hT[:, ft, a:b_], in0=t1[q][:, 0:n], in1=wbc[:, a:b_], op=ALU.mult), [f"t1_{q}", "wbc"], [("hT", ft)])
        HT = [("hT", f_) for f_ in range(24)]
        for dt in range(16):
            w = dt % 2
            p.dma("pool", wdt[w], wdd[le, dt], writes=[f"wdt{w}"])
            for bi, (a, b_) in enumerate(SBLK):
                n = b_ - a
                bk = oi % 8
                q = oi % 2
                oi += 1
                for ft in range(24):
                    k.mm(B[bk][:, 0:n], wdt[w][:, ft, :], hT[:, ft, a:b_], ft == 0, ft == 23, [f"wdt{w}"] + HT, [("bank", bk)])
                k.act(lambda: nc.scalar.copy(out=ost[q][:, 0:n], in_=B[bk][:, 0:n]), [("bank", bk)], [f"ost{q}"])
                p.dma("act" if q else "sp", outd[le, :, dt, a:b_], ost[q][:, 0:n], reads=[f"ost{q}"])
    p.finish()
    return nc


def build_pD():
    k = KB()
    nc, p = k.nc, k.p
    partd = k.din("parts", [NEXP, 128, 16, NB])
    x1d = k.din("x1T", [128, 16, NB])
    modd = k.din("mod", [128, 16, 12])
    lnpd = k.din("lnp", [128, 16, 2])
    outd = k.dout("x2T", [128, 16, NB])
    k.consts()
    ln_scratch(k)
    mod = k.sb("modt", [128, 16, 12])
    lnp = k.sb("lnpt", [128, 16, 2])
    R = k.sb("R", [128, 16, NB])
    S = k.sb("S", [128, 16, NB])
    pt = [k.sb(f"pt{i}", [128, 16, NB]) for i in range(2)]
    p.dma("sp", mod, modd, writes=["mod"])
    p.dma("sp", lnp, lnpd, writes=["lnp"])
    p.dma("sp", R, x1d, writes=["R"])
    p.dma("act", S, partd[0], writes=["S"])
    for j in range(1, NEXP):
        q = j % 2
        p.dma("sp" if q else "act", pt[q], partd[j], writes=[f"pt{q}"])
        k.dve(lambda: nc.vector.tensor_tensor(out=S, in0=S, in1=pt[q], op=ALU.add), ["S", f"pt{q}"], ["S"])
    SEG = [(0, 512, 0), (512, 528, 1)]
    for kt in range(16):
        for (a, b_, kind) in SEG:
            k.dve(lambda kt=kt, a=a, b_=b_, kind=kind: nc.vector.tensor_scalar(out=S[:, kt, a:b_], in0=S[:, kt, a:b_], scalar1=mod[:, kt, 6 * kind + 5:6 * kind + 6],
                                                                             scalar2=None, op0=ALU.mult), ["S", "mod"], ["S"])
        k.dve(lambda kt=kt: nc.vector.scalar_tensor_tensor(out=R[:, kt, :], in0=R[:, kt, :], scalar=ALPHA, in1=S[:, kt, :], op0=ALU.mult, op1=ALU.add), ["R", "S"], ["R"])
    p.barrier()
    for (a, b_) in BLKS:
        def post(kt, t, tk, a=a, b_=b_):
            k.act(lambda: nc.scalar.activation(out=S[:, kt, a:b_], in_=t, func=AF.Identity, bias=lnp[:, kt, 1:2], scale=lnp[:, kt, 0:1]), [tk, "lnp", "S"], ["S"])
        ln_cm(k, R, "R", a, b_, post, "d")
    p.dma("sp", outd, S, reads=["S"])
    p.finish()
    return nc


_PROGS = {}


def _prog(name, fn):
    if name not in _PROGS:
        _PROGS[name] = fn()
    return _PROGS[name]


def _run(nc, maps):
    return run_bass_kernel_spmd(nc, maps, core_ids=list(range(NCORES))).results


def _tok(h, g):
    j = h * NCORES + g
    return np.concatenate([CTX + j * 512 + np.arange(512), j * 16 + np.arange(16)])


def kernel(**inp):
    inp = {k_: np.asarray(v) for k_, v in inp.items()}
    mod = run_p0(inp)
    Xall = np.concatenate([inp["ctx"][0], inp["x"][0]], axis=0).astype(np.float32)
    for i in range(DEPTH):
        resA = _run(_prog("A", build_pA), pA_inputs(Xall, mod[i], inp, i))
        hs_all = np.zeros((T, D_INNER), NPBF)
        hpre_all = np.zeros((T, D), np.float32)
        for g in range(NCORES):
            hs_all[:, g * 512:(g + 1) * 512] = resA[g]["hs"]
            hpre_all[:, g * 128:(g + 1) * 128] = resA[g]["hpre"][0].T
            hpre_all[:, 1024 + g * 128:1024 + (g + 1) * 128] = resA[g]["hpre"][1].T
        del resA
        m = mod[i]
        mod12 = np.stack([m[kind, j * D:(j + 1) * D] for kind in range(2) for j in range(6)], axis=-1)
        mod12 = np.ascontiguousarray(mod12.reshape(16, 128, 12).transpose(1, 0, 2))
        lnpB = np.stack([inp["conv_ln_g"][i], inp["conv_ln_b"][i], inp["ln1_g"][i], inp["ln1_b"][i]], axis=-1)
        lnpB = np.ascontiguousarray(lnpB.reshape(16, 128, 4).transpose(1, 0, 2))
        wso = cm_tiles(inp["w_ssm_out"][i], 32)
        wco = cm_tiles(inp["w_conv_out"][i], 16)
        wgate = cm_tiles(np.ascontiguousarray(inp["w_in"][i][:, O_GATE:]), 16)
        wo = cm_tiles(inp["w_o"][i], 16)
        wr = np.ascontiguousarray(inp["w_router"][i].reshape(16, 128, 16).transpose(1, 0, 2))
        X1all = np.zeros((T, D), np.float32)
        U2all = np.zeros((T, D), NPBF)
        AFFall = np.zeros((T, NEXP), np.float32)
        for h in range(2):
            maps = []
            for g in range(NCORES):
                tok = _tok(h, g)
                maps.append({"xT": to_cm(Xall[tok]), "mod": mod12, "hsT": to_cm(hs_all[tok]), "hpT": to_cm(hpre_all[tok]), "lnp": lnpB,
                             "wso": wso, "wco": wco, "wgate": wgate, "wo": wo, "wr": wr})
            resB = _run(_prog("B", build_pB), maps)
            for g in range(NCORES):
                tok = _tok(h, g)
                X1all[tok] = from_cm(resB[g]["x1T"])
                U2all[tok] = from_cm(resB[g]["u2T"])
                AFFall[tok] = resB[g]["aff"]
            del resB, maps
        del wso, wco, wgate, wo
        maps = [{"affT": np.ascontiguousarray(AFFall[:, [2 * j, 2 * j + 1]].T)} for j in range(NCORES)]
        resC1 = _run(_prog("C1", build_pC1), maps)
        idx = np.zeros((NEXP, NS), np.int64)
        gwv = np.zeros((NEXP, NS), np.float32)
        for j in range(NCORES):
            for le in range(2):
                e = 2 * j + le
                sl = np.rint(resC1[j]["slot"][le]).astype(np.int64)
                for (a, b_, cap, off) in [(0, CTX, 2 * CTX // NEXP, 1024), (CTX, T, 2 * SEQ // NEXP, 0)]:
                    s_ = sl[a:b_]
                    ok = (s_ >= 0) & (s_ < cap)
                    idx[e, off + s_[ok]] = a + np.flatnonzero(ok)
                gwv[e] = resC1[j]["wts"][le][idx[e]]
        del resC1
        maps = []
        for j in range(NCORES):
            es = [2 * j, 2 * j + 1]
            maps.append({"xeT": np.stack([to_cm(U2all[idx[e]]) for e in es]), "gw": np.ascontiguousarray(gwv[es]),
                         "wg": np.stack([cm_tiles(inp["w_exp_gate"][i][e], 16) for e in es]),
                         "wu": np.stack([cm_tiles(inp["w_exp_up"][i][e], 16) for e in es]),
                         "wd": np.stack([cm_tiles(inp["w_exp_down"][i][e], 24) for e in es])})
        resC = _run(_prog("C2", build_pC2), maps)
        PART = []
        for e in range(NEXP):
            pe_ = np.zeros((T, D), np.float32)
            pe_[idx[e]] = from_cm(resC[e // 2]["yeT"][e % 2])
            PART.append(pe_)
        del resC, maps
        lnpD = np.stack([inp["ln2_g"][i], inp["ln2_b"][i]], axis=-1)
        lnpD = np.ascontiguousarray(lnpD.reshape(16, 128, 2).transpose(1, 0, 2))
        X2all = np.zeros((T, D), np.float32)
        for h in range(2):
            maps = []
            for g in range(NCORES):
                tok = _tok(h, g)
                maps.append({"parts": np.stack([to_cm(PART[e][tok]) for e in range(NEXP)]),
                             "x1T": to_cm(X1all[tok]), "mod": mod12, "lnp": lnpD})
            resD = _run(_prog("D", build_pD), maps)
            for g in range(NCORES):
                X2all[_tok(h, g)] = from_cm(resD[g]["x2T"])
            del resD, maps
        Xall = X2all
    return np.ascontiguousarray(Xall[CTX:].reshape(1, SEQ, D)).astype(np.float32)
```
